# Optimizing a Trainium2 kernel written in Bass

```python
import math
import jax
import jax.numpy as jnp
from jax import lax
import numpy as np

D_MODEL = 1024
BATCH = 8
SEQ = 4096
DEPTH = 2

CTX_LEN = 256
GRID_W = 64
Q_BLOCK = 128
ROPE_BASE = 10000.0
HEAD_DIM = 64
N_BRANCH = 4
BRANCH_W = D_MODEL // N_BRANCH

MLA_HEADS = 4
MLA_Q_LORA = D_MODEL // 4
MLA_KV_LORA = D_MODEL // 8
MLA_NOPE = 64
MLA_ROPE = 32
MLA_V = 64
MLA_SCALE = (MLA_NOPE + MLA_ROPE) ** -0.5

RWKV_HEADS = 4
RWKV_N = 64
RWKV_DECAY_LORA = 64
RWKV_AAA_LORA = 64
RWKV_GN_EPS = 64e-5
RWKV_DECAY_SCALE = math.exp(-0.5)
RWKV_SPLITS = (BRANCH_W, BRANCH_W, BRANCH_W,
               RWKV_DECAY_LORA, RWKV_DECAY_LORA, RWKV_AAA_LORA, RWKV_AAA_LORA)
RWKV_SHIFT_W = sum(RWKV_SPLITS)

GQA_Q_HEADS = 4
GQA_KV_HEADS = 2
GQA_SCALE = HEAD_DIM ** -0.5

DIFF_HEADS = 4
DIFF_D = 32
DIFF_V = 64
DIFF_SCALE = DIFF_D ** -0.5

ALPHA = (2 * DEPTH) ** 0.25
BETA = (8 * DEPTH) ** -0.25

IN_SPLITS = (
    MLA_Q_LORA, MLA_KV_LORA, MLA_ROPE, BRANCH_W,
    RWKV_SHIFT_W, BRANCH_W,
    GQA_Q_HEADS * HEAD_DIM, GQA_KV_HEADS * HEAD_DIM, GQA_KV_HEADS * HEAD_DIM, BRANCH_W,
    DIFF_HEADS * 2 * DIFF_D, DIFF_HEADS * 2 * DIFF_D, DIFF_HEADS * DIFF_V, BRANCH_W,
    N_BRANCH * D_MODEL,
)
IN_W = sum(IN_SPLITS)

kernel_name = "hybrid_mla_rwkv7_gqa_diffattn_dit"


def split_cols(y, widths):
    out, o = [], 0
    for w in widths:
        out.append(y[..., o:o + w])
        o += w
    return out


def flat_heads(o):
    return o.reshape(o.shape[0], o.shape[1], -1)


def layer_norm(x, eps=1e-6):
    xf = x.astype(jnp.float32)
    mu = jnp.mean(xf, axis=-1, keepdims=True)
    var = jnp.mean(jnp.square(xf - mu), axis=-1, keepdims=True)
    return ((xf - mu) * lax.rsqrt(var + eps)).astype(x.dtype)


def post_norm(h, g, b):
    return layer_norm(h, 1e-5) * g + b


def rms_norm(x, g, eps):
    xf = x.astype(jnp.float32)
    return (xf * lax.rsqrt(jnp.mean(jnp.square(xf), axis=-1, keepdims=True) + eps)).astype(x.dtype) * g


def l2_normalize(t):
    tf = t.astype(jnp.float32)
    return (tf * lax.rsqrt(jnp.sum(jnp.square(tf), axis=-1, keepdims=True) + 1e-12)).astype(t.dtype)


def group_norm_heads(y, w, b):
    mu = jnp.mean(y, axis=-1, keepdims=True)
    var = jnp.mean(jnp.square(y - mu), axis=-1, keepdims=True)
    return ((y - mu) * lax.rsqrt(var + RWKV_GN_EPS)).astype(w.dtype) * w + b


def modulate(x, shift, scale):
    return layer_norm(x) * (1.0 + scale) + shift


def axial_rope_tables(row, col, rot_dim):
    quarter = rot_dim // 4
    inv_freq = ROPE_BASE ** (-jnp.arange(quarter, dtype=jnp.float32) / quarter)
    ang = jnp.concatenate([row[:, None] * inv_freq, col[:, None] * inv_freq], axis=-1)
    return jnp.cos(ang), jnp.sin(ang)


def apply_rope(t, cos, sin):
    half = t.shape[-1] // 2
    c = cos[:, None, :].astype(t.dtype)
    s = sin[:, None, :].astype(t.dtype)
    t1, t2 = t[..., :half], t[..., half:]
    return jnp.concatenate([t1 * c - t2 * s, t2 * c + t1 * s], axis=-1)


def sweep_query_blocks(fn, *qs):
    B, L = qs[0].shape[:2]
    nb = L // Q_BLOCK
    blocks = tuple(jnp.moveaxis(q.reshape(B, nb, Q_BLOCK, *q.shape[2:]), 1, 0) for q in qs)
    out = lax.map(lambda qb: fn(*qb), blocks)
    return jnp.moveaxis(out, 0, 1).reshape(B, L, *out.shape[3:])


def softmax_attention(q, k, v, scale):
    Hq, Hk = q.shape[2], k.shape[2]
    G = Hq // Hk

    def block(qb):
        Bq, Q = qb.shape[:2]
        qg = qb.reshape(Bq, Q, Hk, G, qb.shape[-1])
        s = jnp.einsum('bqhgd,bkhd->bhgqk', qg, k).astype(jnp.float32) * scale
        p = jax.nn.softmax(s, axis=-1).astype(v.dtype)
        o = jnp.einsum('bhgqk,bkhe->bqhge', p, v)
        return o.reshape(Bq, Q, Hq, v.shape[-1])

    return sweep_query_blocks(block, q)


def differential_attention(q1, q2, k1, k2, v, lam):
    def block(q1b, q2b):
        s1 = jnp.einsum('bqhd,bkhd->bhqk', q1b, k1).astype(jnp.float32) * DIFF_SCALE
        s2 = jnp.einsum('bqhd,bkhd->bhqk', q2b, k2).astype(jnp.float32) * DIFF_SCALE
        p = jax.nn.softmax(s1, axis=-1) - lam * jax.nn.softmax(s2, axis=-1)
        return jnp.einsum('bhqk,bkhe->bqhe', p.astype(v.dtype), v)

    return sweep_query_blocks(block, q1, q2)


def mla_project(q_lat, kv_lat, k_rope, q_norm, w_uq, kv_norm, w_ukv, rope):
    B, L = q_lat.shape[:2]
    q = (rms_norm(q_lat, q_norm, 1e-6) @ w_uq).reshape(B, L, MLA_HEADS, MLA_NOPE + MLA_ROPE)
    kv = (rms_norm(kv_lat, kv_norm, 1e-6) @ w_ukv).reshape(B, L, MLA_HEADS, MLA_NOPE + MLA_V)
    q_nope, q_pe = q[..., :MLA_NOPE], q[..., MLA_NOPE:]
    k_nope, v = kv[..., :MLA_NOPE], kv[..., MLA_NOPE:]
    k_pe = k_rope[:, :, None, :]
    if rope is not None:
        q_pe = apply_rope(q_pe, *rope)
        k_pe = apply_rope(k_pe, *rope)
    q = jnp.concatenate([q_nope, q_pe], axis=-1)
    k = jnp.concatenate([k_nope, jnp.broadcast_to(k_pe, (B, L, MLA_HEADS, MLA_ROPE))], axis=-1)
    return q, k, v


def gqa_project(q, k, v, q_norm, k_norm, rope):
    B, L = q.shape[:2]
    q = rms_norm(q.reshape(B, L, GQA_Q_HEADS, HEAD_DIM), q_norm, 1e-6)
    k = rms_norm(k.reshape(B, L, GQA_KV_HEADS, HEAD_DIM), k_norm, 1e-6)
    v = v.reshape(B, L, GQA_KV_HEADS, HEAD_DIM)
    if rope is not None:
        q = apply_rope(q, *rope)
        k = apply_rope(k, *rope)
    return q, k, v


def diff_project(q, k, v, rope):
    B, L = q.shape[:2]
    q = q.reshape(B, L, DIFF_HEADS, 2, DIFF_D)
    k = k.reshape(B, L, DIFF_HEADS, 2, DIFF_D)
    v = v.reshape(B, L, DIFF_HEADS, DIFF_V)
    q1, q2, k1, k2 = q[..., 0, :], q[..., 1, :], k[..., 0, :], k[..., 1, :]
    if rope is not None:
        q1, q2, k1, k2 = (apply_rope(t, *rope) for t in (q1, q2, k1, k2))
    return q1, q2, k1, k2, v


def centred_shift(t, mu_prev, mu_next):
    zero = jnp.zeros_like(t[:, :1])
    t_prev = jnp.concatenate([zero, t[:, :-1]], axis=1)
    t_next = jnp.concatenate([t[:, 1:], zero], axis=1)
    return t + mu_prev * (t_prev - t) + mu_next * (t_next - t)


def rwkv_prepare(sh, mu_prev, mu_next, k_k):
    sh = centred_shift(sh, mu_prev, mu_next)
    r, k, v, wd_f, wd_b, ad_f, ad_b = split_cols(sh, RWKV_SPLITS)
    B, L = sh.shape[:2]
    r, k, v = (t.reshape(B, L, RWKV_HEADS, RWKV_N) for t in (r, k, v))
    kk = l2_normalize(k * k_k)
    return r, k, v, kk, (wd_f, wd_b), (ad_f, ad_b)


def rwkv_direction_inputs(r, k, v, wd, ad, w0, w_up, a0, a_up, k_a, r_k):
    shape = r.shape
    z = (w0 + jnp.tanh(wd) @ w_up).astype(jnp.float32)
    w = jnp.exp(-RWKV_DECAY_SCALE * jax.nn.sigmoid(z)).reshape(shape)
    a = jax.nn.sigmoid(a0 + ad @ a_up).reshape(shape)
    k_dir = k * (1.0 + (a - 1.0) * k_a)
    bonus = jnp.sum(r * k_dir * r_k, axis=-1, keepdims=True) * v
    return w, k_dir, a, bonus


def rwkv_scan(state0, r, w, k, v, kk, a, reverse):
    xs = tuple(jnp.moveaxis(t.astype(jnp.float32), 1, 0) for t in (r, w, k, v, kk, a))

    def step(S, inp):
        r_t, w_t, k_t, v_t, kk_t, a_t = inp
        S = (S * w_t[:, :, None, :]
             - jnp.einsum('bhvk,bhk->bhv', S, kk_t)[..., None] * (kk_t * a_t)[:, :, None, :]
             + v_t[..., None] * k_t[:, :, None, :])
        return S, jnp.einsum('bhvk,bhk->bhv', S, r_t)

    S_final, ys = lax.scan(step, state0, xs, reverse=reverse)
    return jnp.moveaxis(ys, 0, 1), S_final


def rwkv_branch(sh_l, sh_c, mu, w0, w_up, a0, a_up, k_k, k_a, r_k, gn_w, gn_b, update_ctx):
    hd = (RWKV_HEADS, RWKV_N)
    k_k, k_a, r_k, gn_w, gn_b = (t.reshape(hd) for t in (k_k, k_a, r_k, gn_w, gn_b))
    lat = rwkv_prepare(sh_l, mu[0], mu[1], k_k)
    cx = rwkv_prepare(sh_c, mu[0], mu[1], k_k)
    B = sh_l.shape[0]
    ys_lat, bon_lat, ys_ctx, bon_ctx = [], [], [], []
    for d in range(2):
        reverse = d == 1
        w_c, kd_c, a_c, b_c = rwkv_direction_inputs(cx[0], cx[1], cx[2], cx[4][d], cx[5][d],
                                                    w0[d], w_up[d], a0[d], a_up[d], k_a, r_k)
        state0 = jnp.zeros((B, RWKV_HEADS, RWKV_N, RWKV_N), jnp.float32)
        y_c, state_c = rwkv_scan(state0, cx[0], w_c, kd_c, cx[2], cx[3], a_c, reverse)
        w_l, kd_l, a_l, b_l = rwkv_direction_inputs(lat[0], lat[1], lat[2], lat[4][d], lat[5][d],
                                                    w0[d], w_up[d], a0[d], a_up[d], k_a, r_k)
        y_l, _ = rwkv_scan(state_c, lat[0], w_l, kd_l, lat[2], lat[3], a_l, reverse)
        ys_lat.append(y_l)
        bon_lat.append(b_l)
        ys_ctx.append(y_c)
        bon_ctx.append(b_c)
    o_l = group_norm_heads(ys_lat[0] + ys_lat[1], gn_w, gn_b) + bon_lat[0] + bon_lat[1]
    o_c = None
    if update_ctx:
        o_c = flat_heads(group_norm_heads(ys_ctx[0] + ys_ctx[1], gn_w, gn_b) + bon_ctx[0] + bon_ctx[1])
    return flat_heads(o_l), o_c


def merge_branches(outs, gate_logits, merge_b, branch_w, out_w):
    g = jax.nn.sigmoid(gate_logits + merge_b)
    y = g[..., :D_MODEL] * (outs[0] @ branch_w[0])
    for i in range(1, N_BRANCH):
        y = y + g[..., i * D_MODEL:(i + 1) * D_MODEL] * (outs[i] @ branch_w[i])
    return y @ out_w


def hybrid_layer(x, ctx, c, c_ctx, ada_w, ada_b, in_w, mla_q_norm, mla_w_uq, mla_kv_norm, mla_w_ukv,
                 rwkv_mu, rwkv_w0, rwkv_w_up, rwkv_a0, rwkv_a_up, rwkv_k_k, rwkv_k_a, rwkv_r_k,
                 rwkv_gn_w, rwkv_gn_b, gqa_q_norm, gqa_k_norm, diff_lambda, diff_subln,
                 merge_b, branch_w, out_w, ln_g, ln_b, lambda_init, rope32, rope64, update_ctx):
    silu = jax.nn.silu
    shift_l, scale_l, gate_l = jnp.split((silu(c) @ ada_w + ada_b)[:, None, :], 3, axis=-1)
    shift_c, scale_c, gate_c = jnp.split(silu(c_ctx) @ ada_w + ada_b, 3, axis=-1)
    pl = split_cols(modulate(x, shift_l, scale_l) @ in_w, IN_SPLITS)
    pc = split_cols(modulate(ctx, shift_c, scale_c) @ in_w, IN_SPLITS)

    qa_l, ka_l, va_l = mla_project(pl[0], pl[1], pl[2], mla_q_norm, mla_w_uq, mla_kv_norm, mla_w_ukv, rope32)
    qa_c, ka_c, va_c = mla_project(pc[0], pc[1], pc[2], mla_q_norm, mla_w_uq, mla_kv_norm, mla_w_ukv, None)
    oa_l = softmax_attention(qa_l, jnp.concatenate([ka_c, ka_l], 1), jnp.concatenate([va_c, va_l], 1), MLA_SCALE)

    ob_l, ob_c = rwkv_branch(pl[4], pc[4], rwkv_mu, rwkv_w0, rwkv_w_up, rwkv_a0, rwkv_a_up,
                             rwkv_k_k, rwkv_k_a, rwkv_r_k, rwkv_gn_w, rwkv_gn_b, update_ctx)

    qc_l, kc_l, vc_l = gqa_project(pl[6], pl[7], pl[8], gqa_q_norm, gqa_k_norm, rope64)
    qc_c, kc_c, vc_c = gqa_project(pc[6], pc[7], pc[8], gqa_q_norm, gqa_k_norm, None)
    oc_l = softmax_attention(qc_l, jnp.concatenate([kc_c, kc_l], 1), jnp.concatenate([vc_c, vc_l], 1), GQA_SCALE)

    lam = (jnp.exp(jnp.sum(diff_lambda[0] * diff_lambda[1]).astype(jnp.float32))
           - jnp.exp(jnp.sum(diff_lambda[2] * diff_lambda[3]).astype(jnp.float32)) + lambda_init)
    q1_l, q2_l, k1_l, k2_l, vd_l = diff_project(pl[10], pl[11], pl[12], rope32)
    q1_c, q2_c, k1_c, k2_c, vd_c = diff_project(pc[10], pc[11], pc[12], None)
    od_l = differential_attention(q1_l, q2_l, jnp.concatenate([k1_c, k1_l], 1), jnp.concatenate([k2_c, k2_l], 1),
                                  jnp.concatenate([vd_c, vd_l], 1), lam)
    od_l = rms_norm(od_l, diff_subln, 1e-5) * (1.0 - lambda_init)

    outs_l = [flat_heads(oa_l) * silu(pl[3]), ob_l * silu(pl[5]),
              flat_heads(oc_l) * silu(pl[9]), flat_heads(od_l) * silu(pl[13])]
    x_new = post_norm(ALPHA * x + gate_l * merge_branches(outs_l, pl[14], merge_b, branch_w, out_w), ln_g, ln_b)

    ctx_new = ctx
    if update_ctx:
        oa_c = softmax_attention(qa_c, ka_c, va_c, MLA_SCALE)
        oc_c = softmax_attention(qc_c, kc_c, vc_c, GQA_SCALE)
        od_c = rms_norm(differential_attention(q1_c, q2_c, k1_c, k2_c, vd_c, lam), diff_subln, 1e-5) * (1.0 - lambda_init)
        outs_c = [flat_heads(oa_c) * silu(pc[3]), ob_c * silu(pc[5]),
                  flat_heads(oc_c) * silu(pc[9]), flat_heads(od_c) * silu(pc[13])]
        ctx_new = post_norm(ALPHA * ctx + gate_c * merge_branches(outs_c, pc[14], merge_b, branch_w, out_w), ln_g, ln_b)
    return x_new, ctx_new


def setup_inputs(seed: int = 0) -> dict:
    key = jax.random.key(seed)
    ks = iter(jax.random.split(key, 40))

    def nrm(shape, scale):
        return scale * jax.random.normal(next(ks), shape, jnp.float32)

    def gain(shape):
        return 1.0 + nrm(shape, 0.05)

    return {
        "x": nrm((BATCH, SEQ, D_MODEL), 1.0),
        "c": nrm((BATCH, D_MODEL), 1.0),
        "ctx": nrm((BATCH, CTX_LEN, D_MODEL), 1.0),
        "c_ctx": nrm((D_MODEL,), 1.0),
        "ada_w": nrm((DEPTH, D_MODEL, 3 * D_MODEL), D_MODEL ** -0.5),
        "ada_b": nrm((DEPTH, 3 * D_MODEL), 0.02),
        "in_w": nrm((DEPTH, D_MODEL, IN_W), D_MODEL ** -0.5),
        "mla_q_norm": gain((DEPTH, MLA_Q_LORA)),
        "mla_w_uq": nrm((DEPTH, MLA_Q_LORA, MLA_HEADS * (MLA_NOPE + MLA_ROPE)), MLA_Q_LORA ** -0.5),
        "mla_kv_norm": gain((DEPTH, MLA_KV_LORA)),
        "mla_w_ukv": nrm((DEPTH, MLA_KV_LORA, MLA_HEADS * (MLA_NOPE + MLA_V)), MLA_KV_LORA ** -0.5),
        "rwkv_mu": jax.random.uniform(next(ks), (DEPTH, 2, RWKV_SHIFT_W), jnp.float32, 0.0, 0.5),
        "rwkv_w0": nrm((DEPTH, 2, BRANCH_W), 0.5),
        "rwkv_w_up": nrm((DEPTH, 2, RWKV_DECAY_LORA, BRANCH_W), RWKV_DECAY_LORA ** -0.5),
        "rwkv_a0": nrm((DEPTH, 2, BRANCH_W), 0.5),
        "rwkv_a_up": nrm((DEPTH, 2, RWKV_AAA_LORA, BRANCH_W), RWKV_AAA_LORA ** -0.5),
        "rwkv_k_k": 0.85 + nrm((DEPTH, BRANCH_W), 0.05),
        "rwkv_k_a": gain((DEPTH, BRANCH_W)),
        "rwkv_r_k": nrm((DEPTH, BRANCH_W), 0.1),
        "rwkv_gn_w": gain((DEPTH, BRANCH_W)),
        "rwkv_gn_b": nrm((DEPTH, BRANCH_W), 0.02),
        "gqa_q_norm": gain((DEPTH, HEAD_DIM)),
        "gqa_k_norm": gain((DEPTH, HEAD_DIM)),
        "diff_lambda": nrm((DEPTH, 4, DIFF_D), 0.1),
        "diff_subln": gain((DEPTH, DIFF_V)),
        "merge_b": nrm((DEPTH, N_BRANCH * D_MODEL), 0.02),
        "branch_w": nrm((DEPTH, N_BRANCH, BRANCH_W, D_MODEL), BETA * BRANCH_W ** -0.5),
        "out_w": nrm((DEPTH, D_MODEL, D_MODEL), BETA * D_MODEL ** -0.5),
        "ln_g": gain((DEPTH, D_MODEL)),
        "ln_b": nrm((DEPTH, D_MODEL), 0.02),
    }


def reference(x, c, ctx, c_ctx, ada_w, ada_b, in_w, mla_q_norm, mla_w_uq, mla_kv_norm, mla_w_ukv,
              rwkv_mu, rwkv_w0, rwkv_w_up, rwkv_a0, rwkv_a_up, rwkv_k_k, rwkv_k_a, rwkv_r_k,
              rwkv_gn_w, rwkv_gn_b, gqa_q_norm, gqa_k_norm, diff_lambda, diff_subln,
              merge_b, branch_w, out_w, ln_g, ln_b):
    L = x.shape[1]
    rows = L // GRID_W
    row = jnp.repeat(jnp.arange(rows), GRID_W).astype(jnp.float32)
    col = jnp.tile(jnp.arange(GRID_W), rows).astype(jnp.float32)
    rope32 = axial_rope_tables(row, col, MLA_ROPE)
    rope64 = axial_rope_tables(row, col, HEAD_DIM)
    for l in range(DEPTH):
        x, ctx = hybrid_layer(
            x, ctx, c, c_ctx, ada_w[l], ada_b[l], in_w[l], mla_q_norm[l], mla_w_uq[l], mla_kv_norm[l], mla_w_ukv[l],
            rwkv_mu[l], rwkv_w0[l], rwkv_w_up[l], rwkv_a0[l], rwkv_a_up[l], rwkv_k_k[l], rwkv_k_a[l], rwkv_r_k[l],
            rwkv_gn_w[l], rwkv_gn_b[l], gqa_q_norm[l], gqa_k_norm[l], diff_lambda[l], diff_subln[l],
            merge_b[l], branch_w[l], out_w[l], ln_g[l], ln_b[l],
            lambda_init=0.8 - 0.6 * math.exp(-0.3 * l), rope32=rope32, rope64=rope64,
            update_ctx=l < DEPTH - 1)
    return x
```

```python
import math
import os
from contextlib import ExitStack

import numpy as np
import ml_dtypes

import concourse.bass as bass
import concourse.mybir as mybir
from concourse.bass_utils import run_bass_kernel_spmd

F32 = mybir.dt.float32
BF16 = mybir.dt.bfloat16
U16 = mybir.dt.uint16
AF = mybir.ActivationFunctionType
ALU = mybir.AluOpType
AX = mybir.AxisListType

D = 1024
NCTX = 256
SEQ = 4096
T = NCTX + SEQ
NT = T // 128
DEPTH = 2
IN_W = 7840
ALPHA = (2 * DEPTH) ** 0.25
MLA_SCALE = 96 ** -0.5
GQA_SCALE = 64 ** -0.5
DIFF_SCALE = 32 ** -0.5
DECAY_SCALE = math.exp(-0.5)

O_MQ, O_MKV, O_MKR, O_MG = 0, 256, 384, 416
O_RW, O_RG = 672, 1696
O_GQ, O_GK, O_GV, O_GG = 1952, 2208, 2336, 2464
O_DQ, O_DK, O_DV, O_DG = 2720, 2976, 3232, 3488
O_MERGE = 3744

SAME_ENGINE_SYNC = True
NDS = 24


REORDER = int(os.environ.get("REORDER", "1"))
DUR = {"pe": 0.1, "act": 1.3, "dve": 0.5, "pool": 2.0}
if os.environ.get("DUR"):
    DUR = dict(zip(["pe", "act", "dve", "pool"], [float(x) for x in os.environ["DUR"].split(",")]))
DMA_DUR = float(os.environ.get("DMA_DUR", "2.0"))
SW_INFLIGHT = int(os.environ.get("SW_INFLIGHT", "4"))
HINTS = int(os.environ.get("HINTS", "1"))
CP_ALPHA = float(os.environ.get("CP_ALPHA", "0.3"))


class Sched:
    def __init__(self, nc):
        self.nc = nc
        self.names = ["pe", "act", "dve", "pool", "sp"]
        self.ops = []
        self.cnt = {e: 0 for e in self.names}
        self.seen = {e: {} for e in self.names}
        self.csem = {e: nc.alloc_semaphore("c_" + e) for e in ["pe", "act", "dve", "pool"]}
        self.dsem = [nc.alloc_semaphore("d%d" % i) for i in range(NDS)]
        self.dval = [0] * NDS
        self.dnext = 0
        self.nins = 0
        self.phase = "x"
        self.nflush = 0
        self.swq = []

    def op(self, e, fn, R=(), W=(), c=None):
        self.ops.append((e, fn, tuple(R), tuple(W), False, c if HINTS else None))
        self.nins += 1

    def dma(self, q, out, in_, R=(), W=()):
        self.ops.append((q, (lambda eng, out=out, in_=in_: eng.dma_start(out=out, in_=in_)), tuple(R), tuple(W), True, None))
        self.nins += 1

    def barrier(self):
        self.ops.append(None)

    def flush(self, name=None):
        ops = self.ops
        self.ops = []
        seg = []
        for o in ops:
            if o is None:
                self._emit_segment(seg, True, name)
                seg = []
            else:
                seg.append(o)
        if seg:
            self._emit_segment(seg, False, name)

    def _emit_segment(self, ops, barrier, name):
        n = len(ops)
        preds = [[] for _ in range(n)]
        lastw = {}
        readers = {}
        for i, (e, fn, R, W, isd, cst) in enumerate(ops):
            p = preds[i]
            for k in R:
                t = lastw.get(k)
                if t is not None:
                    p.append(t)
                if k.startswith("ps"):
                    for r in readers.get(k, ()):
                        if ops[r][0] != e:
                            p.append(r)
            for k in W:
                t = lastw.get(k)
                if t is not None:
                    p.append(t)
                p.extend(readers.get(k, ()))
            for k in R:
                readers.setdefault(k, []).append(i)
            for k in W:
                lastw[k] = i
                readers[k] = []
        est = [0.0] * n
        fin = [0.0] * n
        for i in range(n):
            e, _, _, _, isd, cst = ops[i]
            t = 0.0
            for p in preds[i]:
                if fin[p] > t:
                    t = fin[p]
            est[i] = t
            fin[i] = t + (cst if cst is not None else (DMA_DUR if isd else DUR[e]))
        if CP_ALPHA > 0.0:
            tail = [0.0] * n
            for i in range(n - 1, -1, -1):
                d_i = fin[i] - est[i]
                if tail[i] < d_i:
                    tail[i] = d_i
                ti = tail[i]
                for p in preds[i]:
                    cand = ti + (fin[p] - est[p])
                    if tail[p] < cand:
                        tail[p] = cand
            keyv = [est[i] - CP_ALPHA * tail[i] for i in range(n)]
        else:
            keyv = est
        order = sorted(range(n), key=lambda i: (keyv[i], i)) if REORDER else list(range(n))
        tok = [None] * n
        prog = {e: [] for e in self.names}
        for i in order:
            e, fn, R, W, isd, cst = ops[i]
            deps = []
            for p in preds[i]:
                tp = tok[p]
                assert tp is not None, "scheduler order violates a dependency"
                if tp[0] == "c" and tp[1] == e and (e == "pe" or not SAME_ENGINE_SYNC):
                    continue
                deps.append(tp)
            if isd:
                j = self.dnext
                self.dnext = (j + 1) % NDS
                if self.dval[j] > 0:
                    deps.append(("d", j, self.dval[j]))
                if e == "pool":
                    self.swq.append(None)
                    if len(self.swq) > SW_INFLIGHT:
                        deps.append(self.swq[-1 - SW_INFLIGHT])
                self.dval[j] += 16
                tok[i] = ("d", j, self.dval[j])
                if e == "pool":
                    self.swq[-1] = tok[i]
            else:
                self.cnt[e] += 1
                tok[i] = ("c", e, self.cnt[e])
            seen = self.seen[e]
            need = {}
            for (kind, s_, v) in deps:
                key = (kind, s_)
                if seen.get(key, 0) >= v:
                    continue
                if need.get(key, 0) < v:
                    need[key] = v
            for key, v in need.items():
                seen[key] = v
            prog[e].append((list(need.items()), fn, (tok[i][0], tok[i][1])))
        if barrier:
            toks = [("c", o, self.cnt[o]) for o in ["pe", "act", "dve", "pool"] if self.cnt[o] > 0]
            toks += [("d", j, self.dval[j]) for j in range(NDS) if self.dval[j] > 0]
            for e in self.names:
                seen = self.seen[e]
                need = {}
                for (kind, s_, v) in toks:
                    if seen.get((kind, s_), 0) < v:
                        need[(kind, s_)] = v
                        seen[(kind, s_)] = v
                if need:
                    prog[e].append((list(need.items()), None, None))
        self._emit(prog, name)

    def _emit(self, progs, name):
        nc = self.nc
        self.nflush += 1
        name = "%02d_%s" % (self.nflush, name or self.phase)
        csem, dsem = self.csem, self.dsem

        def mk(e):
            def body(eng):
                for waits, fn, inc in progs[e]:
                    for (kind, s), v in waits:
                        eng.wait_ge(csem[s] if kind == "c" else dsem[s], v)
                    if fn is None:
                        continue
                    ins = fn(eng)
                    if inc[0] == "c":
                        ins.then_inc(csem[inc[1]], 1)
                    else:
                        ins.then_inc(dsem[inc[1]], 16)
            return body

        with nc.named_scope(name), nc.Block() as blk:
            blk.tensor(mk("pe"))
            blk.scalar(mk("act"))
            blk.vector(mk("dve"))
            blk.gpsimd(mk("pool"))
            blk.sync(mk("sp"))


class Ring:
    def __init__(self, items):
        self.items = items
        self.i = 0

    def next(self):
        it = self.items[self.i]
        self.i = (self.i + 1) % len(self.items)
        return it


class K:
    pass


_uid = [0]


def _sb(es, nc, name, shape, dt):
    _uid[0] += 1
    return es.enter_context(nc.sbuf_tensor("sb_%s_%d" % (name, _uid[0]), list(shape), dt))


MIX = ["mla", "rwkv", "gqa", "diff"]
CUT = int(os.environ.get("CUT", "99"))
SKIP_ATT = int(os.environ.get("SKIP_ATT", "0"))
SUB = int(os.environ.get("SUB", "99"))


def ring(es, nc, name, n, shape, dt):
    return Ring([(_sb(es, nc, "%s%d" % (name, i), shape, dt), "%s%d" % (name, i)) for i in range(n)])


def psring(k, idxs):
    return Ring([(k.ps[i], "ps%d" % i) for i in idxs])


def build(dbg=None, active=("gqa",)):
    nc = bass.Bass("TRN2", target_bir_lowering=False)
    S = Sched(nc)
    k = K()
    k.nc, k.S, k.dbg, k.active = nc, S, dbg, active

    def din(name, shape, dt=F32):
        return nc.dram_tensor(name, list(shape), dt, kind="ExternalInput").ap()

    def dscr(name, shape, dt):
        return nc.dram_tensor(name, list(shape), dt, kind="Internal").ap()

    I = {}
    for name, shape, dt in input_specs():
        I[name] = din(name, shape, dt)
    k.I = I
    k.out = nc.dram_tensor("out", [SEQ, D], F32, kind="ExternalOutput").ap()
    if dbg is not None:
        k.dbg_out = nc.dram_tensor("dbg", list(dbg[1]), dbg[2], kind="ExternalOutput").ap()
    k.XM = dscr("XM", [128, 8, T], BF16)
    k.X1 = dscr("X1", [T, D], F32)
    k.OUTS = {m: dscr("OUTS_" + m, [128, 2, T], BF16) for m in MIX}
    k.SH = [dscr("SH%d" % f, [128, T], F32) for f in range(6)]
    k.SC = [dscr("SC%d" % d, [128, 2, 6, T], BF16) for d in range(2)]
    k.BT = dscr("BT", [128, 2, T], BF16)

    with ExitStack() as top:
        k.psw = [top.enter_context(nc.psum_tensor("psw%d" % i, [128, 1024], F32)) for i in range(4)]
        k.ps = [k.psw[i // 2][:, (i % 2) * 512:(i % 2 + 1) * 512] for i in range(8)]
        k.ident = _sb(top, nc, "ident", [128, 128], BF16)
        k.modB = _sb(top, nc, "modB", [128, 2, 3, D], F32)
        S.dma("sp", k.ident[:], I["ident"], W=["ident"])
        stop = False
        for l in range(DEPTH):
            phase_adaln(k, l)
            if dbg is not None and dbg[0] == "ln%d" % l:
                break
            for m in MIX:
                if m in active:
                    {"gqa": mixer_gqa, "mla": mixer_mla, "diff": mixer_diff, "rwkv": mixer_rwkv}[m](k, l)
                if dbg is not None and dbg[0] == "%s%d" % (m, l):
                    stop = True
                    break
            if stop:
                break
            phase_merge(k, l)
            if dbg is not None and dbg[0] == "x%d" % l:
                break
        if dbg is not None:
            emit_dbg(k)
        S.barrier()
        S.flush()
    return nc


def emit_dbg(k):
    nc, S = k.nc, k.S
    name = k.dbg[0]
    if name.startswith("ln"):
        S.dma("sp", k.dbg_out, k.XM, R=["XM"], W=["dbg"])
    elif name[:-1] in MIX:
        S.dma("sp", k.dbg_out, k.OUTS[name[:-1]], W=["dbg"])
    elif name == "x0":
        S.dma("sp", k.dbg_out, k.X1, W=["dbg"])
    S.barrier()
    S.flush()


def phase_adaln(k, l):
    nc, S, I = k.nc, k.S, k.I
    S.phase = "adaln%d" % l
    with ExitStack() as es:
        cv = _sb(es, nc, "cv", [128, 8, 2], F32)
        sc = _sb(es, nc, "sc", [128, 8, 2], F32)
        sel = _sb(es, nc, "sel", [2, 2, 128], F32)
        ab = _sb(es, nc, "ab", [2, 3 * D], F32)
        mod = _sb(es, nc, "mod", [2, 3 * D], F32)
        wst = [_sb(es, nc, "adaw%d" % i, [128, 8, 512], F32) for i in range(2)]
        S.dma("sp", cv[:], I["cvec"], W=["cv"])
        S.dma("sp", sel[:], I["sel2"], W=["sel"])
        S.dma("sp", ab[:], I["ada_b"][l:l + 1, :].to_broadcast([2, 3 * D]), W=["ab"])
        S.op("act", lambda e: e.activation(out=sc[:], in_=cv[:], func=AF.Silu), R=["cv"], W=["sc"])
        awv = I["ada_w"][l].rearrange("(kc p) n -> p kc n", p=128)
        for j in range(6):
            w = wst[j % 2]
            wk = "adaw%d" % (j % 2)
            S.dma("sp" if j % 2 == 0 else "act", w[:], awv[:, :, j * 512:(j + 1) * 512], W=[wk])
            ps = k.ps[j % 2]
            pk = "ps%d" % (j % 2)
            for kc in range(8):
                S.op("pe", lambda e, ps=ps, w=w, kc=kc: e.matmul(ps[0:2, :], lhsT=sc[:, kc, :], rhs=w[:, kc, :],
                                                                 start=(kc == 0), stop=(kc == 7)),
                     R=["sc", wk], W=[pk])
            S.op("dve", lambda e, ps=ps, j=j: e.tensor_tensor(out=mod[:, j * 512:(j + 1) * 512], in0=ps[0:2, :],
                                                              in1=ab[:, j * 512:(j + 1) * 512], op=ALU.add),
                 R=[pk, "ab"], W=["mod"])
        S.op("dve", lambda e: e.tensor_scalar(out=mod[:, D:2 * D], in0=mod[:, D:2 * D], scalar1=1.0, scalar2=None,
                                              op0=ALU.add), R=["mod"], W=["mod"])
        n = 0
        for which in range(2):
            for part in range(3):
                for hb in range(2):
                    ps = k.ps[2 + n % 2]
                    pk = "ps%d" % (2 + n % 2)
                    n += 1
                    c0 = part * D + hb * 512
                    S.op("pe", lambda e, ps=ps, which=which, c0=c0: e.matmul(ps[:, :], lhsT=sel[:, which, :],
                                                                               rhs=mod[:, c0:c0 + 512],
                                                                               start=True, stop=True),
                         R=["sel", "mod"], W=[pk])
                    S.op("act", lambda e, ps=ps, which=which, part=part, hb=hb: e.activation(
                        out=k.modB[:, which, part, hb * 512:(hb + 1) * 512], in_=ps[:, :], func=AF.Copy),
                        R=[pk], W=["modB"])
        phase_ln(k, l)


def ln_rows(k, stt, mvt, rst, src, sk, eps):
    S = k.S
    st, stk = stt
    mv, mvk = mvt
    rs, rsk = rst
    for h in range(2):
        S.op("dve", lambda e, h=h: e.bn_stats(out=st[:, h, :], in_=src[:, h * 512:(h + 1) * 512]), R=[sk], W=[stk])
    S.op("dve", lambda e: e.bn_aggr(out=mv[:], in_=st[:].rearrange("p a s -> p (a s)")), R=[stk], W=[mvk])
    S.op("act", lambda e: e.activation(out=rs[:], in_=mv[:, 1:2], func=AF.Sqrt, bias=float(eps), scale=1.0),
         R=[mvk], W=[rsk])
    S.op("dve", lambda e: e.reciprocal(out=rs[:], in_=rs[:]), R=[rsk], W=[rsk])


def phase_ln(k, l):
    nc, S, I = k.nc, k.S, k.I
    S.phase = "ln%d" % l
    src = I["xin"] if l == 0 else k.X1
    with ExitStack() as es:
        xtr = ring(es, nc, "xt", 3, [128, D], F32)
        xnr = ring(es, nc, "xn", 2, [128, D], F32)
        xmr = ring(es, nc, "xm", 2, [128, D], BF16)
        xTr = ring(es, nc, "xT", 2, [128, 8, 128], BF16)
        str_ = ring(es, nc, "st", 2, [128, 2, 6], F32)
        mvr = ring(es, nc, "mv", 2, [128, 2], F32)
        rsr = ring(es, nc, "rs", 2, [128, 1], F32)
        psr = psring(k, [4, 5])
        for i in range(NT):
            which = 1 if i < 2 else 0
            xt, xtk = xtr.next()
            xn, xnk = xnr.next()
            xm, xmk = xmr.next()
            xT, xTk = xTr.next()
            stt, mvt, rst = str_.next(), mvr.next(), rsr.next()
            mv, mvk = mvt
            rs, rsk = rst
            S.dma("sp", xt[:], src[i * 128:(i + 1) * 128, :], W=[xtk])
            ln_rows(k, stt, mvt, rst, xt, xtk, 1e-6)
            S.op("dve", lambda e, xn=xn, xt=xt, mv=mv, rs=rs: e.tensor_scalar(
                out=xn[:], in0=xt[:], scalar1=mv[:, 0:1], scalar2=rs[:, 0:1], op0=ALU.subtract, op1=ALU.mult),
                R=[xtk, mvk, rsk], W=[xnk])
            S.op("pool", lambda e, xn=xn, which=which: e.tensor_tensor(out=xn[:], in0=xn[:], in1=k.modB[:, which, 1, :],
                                                                        op=ALU.mult), R=[xnk, "modB"], W=[xnk])
            S.op("dve", lambda e, xn=xn, xm=xm, which=which: e.tensor_tensor(out=xm[:], in0=xn[:],
                                                                              in1=k.modB[:, which, 0, :], op=ALU.add),
                 R=[xnk, "modB"], W=[xmk])
            pT, pk = psr.next()
            pTb = pT[:, :].bitcast(BF16)
            for kc in range(8):
                S.op("pe", lambda e, xm=xm, kc=kc, pTb=pTb: e.transpose(out=pTb[:, kc * 128:(kc + 1) * 128],
                                                                         in_=xm[:, kc * 128:(kc + 1) * 128],
                                                                         identity=k.ident[:]),
                     R=[xmk, "ident"], W=[pk])
            S.op("act", lambda e, xT=xT, pTb=pTb: e.activation(out=xT[:].rearrange("p a t -> p (a t)"), in_=pTb,
                                                                func=AF.Copy), R=[pk], W=[xTk])
            S.dma("act", k.XM[:, :, i * 128:(i + 1) * 128], xT[:], R=[xTk], W=["XM"])
        S.barrier()
        S.flush()


class XmStream:
    def __init__(self, k, es):
        self.k = k
        self.ring = ring(es, k.nc, "xmc", 2, [128, 8, 512], BF16)
        self.cur = None

    def chunk(self, ci):
        self.tile(ci * 4)
        return self.cur[1], self.cur[2]

    def tile(self, i):
        ci = i // 4
        if self.cur is None or self.cur[0] != ci:
            xb, xbk = self.ring.next()
            q0 = ci * 512
            nq = min(512, T - q0)
            for kc in range(8):
                self.k.S.dma("sp" if kc % 2 == 0 else "act", xb[:, kc, 0:nq], self.k.XM[:, kc, q0:q0 + nq],
                             W=["%s_%d" % (xbk, kc)])
            self.cur = (ci, xb, xbk)
        _, xb, xbk = self.cur
        o = (i % 4) * 128
        return (lambda kc: xb[:, kc, o:o + 128]), (lambda kc: "%s_%d" % (xbk, kc))


def load_w(k, w, name, src, n, stg_ring=None, rows=8, group=256, order=None):
    S = k.S
    view = src.rearrange("(kc p) n -> p kc n", p=128)
    starts = list(range(0, n, group)) if order is None else [g * group for g in order]
    for c0 in starts:
        wd = min(group, n - c0)
        S.dma("pool", w[:, :, c0:c0 + wd], view[:, :, c0:c0 + wd], W=["%s_%d" % (name, c0 // group)])


def wkeys(name, c0, n, group=256):
    return ["%s_%d" % (name, g) for g in range(c0 // group, (c0 + n - 1) // group + 1)]


def inproj_tm(k, ps, pk, xs, i, w, wname, c0, n):
    apf, keyf = xs.tile(i)
    for kc in range(8):
        k.S.op("pe", lambda e, kc=kc: e.matmul(ps[:, 0:n], lhsT=apf(kc), rhs=w[:, kc, c0:c0 + n],
                                               start=(kc == 0), stop=(kc == 7)),
               R=[keyf(kc)] + wkeys(wname, c0, n), W=[pk])


class Item:
    pass


def gate_fm_chunk(k, xs, ci, w, wname, gcol0, GtT, psg):
    S = k.S
    q0 = ci * 512
    nq = min(512, T - q0)
    xb, xbk = xs.chunk(ci)
    for c in range(2):
        ps, pk = psg.next()
        for kc in range(8):
            S.op("pe", lambda e, kc=kc, c=c, ps=ps: e.matmul(
                ps[:, 0:nq], lhsT=w[:, kc, gcol0 + c * 128:gcol0 + (c + 1) * 128], rhs=xb[:, kc, 0:nq],
                start=(kc == 0), stop=(kc == 7)),
                R=["%s_%d" % (xbk, kc)] + wkeys(wname, gcol0 + c * 128, 128), W=[pk])
        S.op("act", lambda e, c=c, ps=ps: e.activation(out=GtT[:, c, q0:q0 + nq], in_=ps[:, 0:nq], func=AF.Silu),
             R=[pk], W=["GtT"])


def attention_phase(k, l, mname, chunks, scale, GtT, kind="soft", extra=None):
    nc, S = k.nc, k.S
    with ExitStack() as es:
        ptr = ring(es, nc, "Pt2", 3, [128, 1024], BF16)
        recr = ring(es, nc, "arec", 2, [128, 512], F32)
        tmpr = ring(es, nc, "atmp", 2, [128, 512], F32)
        obr = ring(es, nc, "aob", 2, [128, 2, 512], BF16)
        if kind == "diff":
            ofmr = ring(es, nc, "aofm", 2, [128, 2, 512], F32)
            sqr = ring(es, nc, "asq", 2, [128, 512], F32)
        spair = Ring([(k.psw[i], ("ps%d" % (2 * i), "ps%d" % (2 * i + 1))) for i in range(3)])
        oring = psring(k, [6, 7])
        allitems = chunks[0] + chunks[1]
        masked = [it for it in allitems if getattr(it, "qmask", None) is not None]
        if masked:
            qzr = ring(es, nc, "aqz", 2, [128, len(masked), 512], BF16)

        def one_item(it, q0, nq, kts, ob, obk, ofm, ofmk):
            psO, pOk = oring.next()
            npair = len(kts) // 2
            pend = []

            def do_pv(n, kt0, kt1, Pt, ptk):
                S.op("pe", lambda e: e.matmul(psO[:, 0:nq], lhsT=it.VA(kt0), rhs=Pt[:, 0:nq], start=(n == 0), stop=False),
                     R=[ptk, it.vkey], W=[pOk], c=0.23)
                S.op("pe", lambda e: e.matmul(psO[:, 0:nq], lhsT=it.VA(kt1), rhs=Pt[:, 512:512 + nq], start=False,
                                              stop=(n == npair - 1)), R=[ptk, it.vkey], W=[pOk], c=0.23)

            for n in range(npair):
                kt0, kt1 = kts[2 * n], kts[2 * n + 1]
                pw, (ka, kb) = spair.next()
                qap = it.QT(q0, nq)
                S.op("pe", lambda e, pw=pw, kt0=kt0, qap=qap: e.matmul(pw[:, 0:nq], lhsT=it.KT(kt0), rhs=qap,
                                                              start=True, stop=True), R=[it.kkey, it.qkey_blk], W=[ka], c=0.23)
                S.op("pe", lambda e, pw=pw, kt1=kt1, qap=qap: e.matmul(pw[:, 512:512 + nq], lhsT=it.KT(kt1), rhs=qap,
                                                              start=True, stop=True), R=[it.kkey, it.qkey_blk], W=[kb], c=0.23)
                Pt, ptk = ptr.next()
                if nq == 512:
                    S.op("act", lambda e, pw=pw, Pt=Pt: e.activation(out=Pt[:, :], in_=pw[:, :], func=AF.Exp,
                                                                     scale=float(scale)), R=[ka, kb], W=[ptk], c=0.95)
                else:
                    S.op("act", lambda e, pw=pw, Pt=Pt: e.activation(
                        out=Pt[:, :].rearrange("p (a c) -> p a c", a=2)[:, :, 0:nq],
                        in_=pw[:, :].rearrange("p (a c) -> p a c", a=2)[:, :, 0:nq], func=AF.Exp, scale=float(scale)),
                        R=[ka, kb], W=[ptk])
                pend.append((n, kt0, kt1, Pt, ptk))
                if len(pend) > 1:
                    do_pv(*pend.pop(0))
            while pend:
                do_pv(*pend.pop(0))
            re_ = slice(64 * it.e, 64 * it.e + 64)
            ro_ = slice(64 * (1 - it.e), 64 * (1 - it.e) + 64)
            rec, rk = recr.next()
            S.op("dve", lambda e: e.reciprocal(out=rec[re_, 0:nq], in_=psO[ro_, 0:nq]), R=[pOk], W=[rk])
            if kind == "soft":
                tmp, tk = tmpr.next()
                S.op("dve", lambda e: e.tensor_tensor(out=tmp[re_, 0:nq], in0=psO[re_, 0:nq], in1=rec[re_, 0:nq],
                                                      op=ALU.mult), R=[pOk, rk], W=[tk])
                S.op("pool", lambda e: e.tensor_tensor(out=ob[re_, it.c, 0:nq], in0=tmp[re_, 0:nq],
                                                       in1=GtT[re_, it.c, q0:q0 + nq], op=ALU.mult),
                     R=[tk, "GtT"], W=[obk])
            elif it.which == 0:
                S.op("dve", lambda e: e.tensor_tensor(out=ofm[re_, it.c, 0:nq], in0=psO[re_, 0:nq], in1=rec[re_, 0:nq],
                                                      op=ALU.mult), R=[pOk, rk], W=[ofmk])
            else:
                tmp, tk = tmpr.next()
                S.op("dve", lambda e: e.tensor_tensor(out=tmp[re_, 0:nq], in0=psO[re_, 0:nq], in1=rec[re_, 0:nq],
                                                      op=ALU.mult), R=[pOk, rk], W=[tk])
                S.op("dve", lambda e: e.scalar_tensor_tensor(out=ofm[re_, it.c, 0:nq], in0=tmp[re_, 0:nq],
                                                             scalar=extra["neglam"][re_, 0:1], in1=ofm[re_, it.c, 0:nq],
                                                             op0=ALU.mult, op1=ALU.add), R=[tk, "neglam", ofmk], W=[ofmk])

        def diff_post(c, q0, nq, ob, obk, ofm, ofmk):
            sq, sqk = sqr.next()
            rs, rsk = recr.next()
            S.op("pool", lambda e: e.tensor_tensor(out=sq[:, 0:nq], in0=ofm[:, c, 0:nq], in1=ofm[:, c, 0:nq], op=ALU.mult),
                 R=[ofmk], W=[sqk])
            pw, (ka, kb) = spair.next()
            S.op("pe", lambda e: e.matmul(pw[:, 0:nq], lhsT=extra["bones"][:], rhs=sq[:, 0:nq], start=True, stop=True),
                 R=[sqk, "bones"], W=[ka])
            S.op("act", lambda e: e.activation(out=rs[:, 0:nq], in_=pw[:, 0:nq], func=AF.Sqrt, bias=1e-5, scale=1.0 / 64),
                 R=[ka], W=[rsk])
            S.op("dve", lambda e: e.reciprocal(out=rs[:, 0:nq], in_=rs[:, 0:nq]), R=[rsk], W=[rsk])
            S.op("dve", lambda e: e.tensor_tensor(out=sq[:, 0:nq], in0=ofm[:, c, 0:nq], in1=rs[:, 0:nq], op=ALU.mult),
                 R=[ofmk, rsk, sqk], W=[sqk])
            S.op("dve", lambda e: e.scalar_tensor_tensor(out=ob[:, c, 0:nq], in0=sq[:, 0:nq], scalar=extra["subcol"][:, 0:1],
                                                         in1=GtT[:, c, q0:q0 + nq], op0=ALU.mult, op1=ALU.mult),
                 R=[sqk, "subcol", "GtT"], W=[obk])

        def one_block(q0, nq, kts):
            ob, obk = obr.next()
            ofm, ofmk = ofmr.next() if kind == "diff" else (None, None)
            if masked:
                qz, qzk = qzr.next()
                for n, it in enumerate(masked):
                    S.op("dve", lambda e, n=n, it=it: e.tensor_scalar(
                        out=qz[:, n, 0:nq], in0=it.qsrc(q0, nq), scalar1=it.qmask, scalar2=None, op0=ALU.mult),
                        R=[it.qkey, "qmask"], W=["%s_%d" % (qzk, n)])
                    it.QT = (lambda q0_, nq_, n=n, qz=qz: qz[:, n, 0:nq_])
                    it.qkey_blk = "%s_%d" % (qzk, n)
            else:
                for it in allitems:
                    it.qkey_blk = it.qkey
            for c in range(2):
                for it in chunks[c]:
                    one_item(it, q0, nq, kts, ob, obk, ofm, ofmk)
                if kind == "diff":
                    diff_post(c, q0, nq, ob, obk, ofm, ofmk)
            S.dma("sp", k.OUTS[mname][:, :, q0:q0 + nq], ob[:, :, 0:nq], R=[obk], W=["OUTS_" + mname])

        for (q0, nq, kts) in qblocks(l):
            one_block(q0, nq, kts)
        S.barrier()
        S.flush()


def qblocks(l):
    blocks = []
    if l == 0:
        blocks.append((0, NCTX, [0, 1]))
    for j in range(8):
        blocks.append((NCTX + 512 * j, 512, list(range(NT))))
    return blocks


def mixer_gqa(k, l):
    nc, S, I = k.nc, k.S, k.I
    S.phase = "gqa%d" % l
    with ExitStack() as mx:
        QT = _sb(mx, nc, "gQT", [128, 2, T], BF16)
        KT = _sb(mx, nc, "gKT", [128, T], BF16)
        VA = _sb(mx, nc, "gVA", [128, NT, 4, 128], BF16)
        qmask = _sb(mx, nc, "gqmask", [128, 6], F32)
        S.dma("sp", qmask[:], I["qmask"], W=["qmask"])
        GtT = _sb(mx, nc, "gGtT", [128, 2, T], BF16)
        with ExitStack() as es:
            xmT = XmStream(k, es)
            w = _sb(es, nc, "gw", [128, 8, 768], BF16)
            stg = None
            load_w(k, w, "gw", I["in_w"][l][:, O_GQ:O_GQ + 768], 768, stg)
            rope = _sb(es, nc, "grope", [128, NT, 64], F32)
            S.dma("act", rope[:], I["rope64"].rearrange("(i p) c -> p i c", p=128), W=["rope"])
            gqk = _sb(es, nc, "gqk", [128, 6, 64], F32)
            for h in range(6):
                srcn = I["gqa_q_norm"] if h < 4 else I["gqa_k_norm"]
                S.dma("act", gqk[:, h, :], srcn[l:l + 1, :].to_broadcast([128, 64]), W=["gqk"])
            S.op("pool", lambda e: e.memset(VA[:], 1.0), W=["Vones"])
            sqr = ring(es, nc, "gsq", 2, [128, 6, 64], F32)
            ssr = ring(es, nc, "gss", 2, [128, 6], F32)
            qnr = ring(es, nc, "gqn", 2, [128, 6, 64], F32)
            tar = ring(es, nc, "gta", 2, [128, 6, 32], F32)
            tbr = ring(es, nc, "gtb", 2, [128, 6, 32], F32)
            tcr = ring(es, nc, "gtc", 2, [128, 6, 32], F32)
            tdr = ring(es, nc, "gtd", 2, [128, 6, 32], F32)
            qbr = ring(es, nc, "gqb", 2, [128, 6, 64], BF16)
            psA, psB, psT = psring(k, [0, 1]), psring(k, [2, 3]), psring(k, [4, 5])
            def tile_body(i):
                pa, pak = psA.next()
                pb, pbk = psB.next()
                pt, ptk = psT.next()
                if i % 4 == 0:
                    gate_fm_chunk(k, xmT, i // 4, w, "gw", 512, GtT, psB)
                inproj_tm(k, pa, pak, xmT, i, w, "gw", 0, 512)
                sq, sqk = sqr.next()
                ss, ssk = ssr.next()
                qn, qnk = qnr.next()
                ta, tak = tar.next()
                tb, tbk = tbr.next()
                tc, tck = tcr.next()
                td, tdk = tdr.next()
                qb, qbk = qbr.next()
                pa3 = pa[:, 0:384].rearrange("p (h d) -> p h d", d=64)
                S.op("act", lambda e, sq=sq, pa3=pa3: e.activation(out=sq[:], in_=pa3, func=AF.Square), R=[pak], W=[sqk])
                S.op("dve", lambda e, sq=sq, ss=ss: e.tensor_reduce(out=ss[:], in_=sq[:], axis=AX.X, op=ALU.add),
                     R=[sqk], W=[ssk])
                S.op("act", lambda e, ss=ss: e.activation(out=ss[:], in_=ss[:], func=AF.Sqrt, bias=1e-6, scale=1.0 / 64),
                     R=[ssk], W=[ssk])
                S.op("dve", lambda e, ss=ss: e.reciprocal(out=ss[:], in_=ss[:]), R=[ssk], W=[ssk])
                S.op("dve", lambda e, qn=qn, pa3=pa3, ss=ss: e.tensor_tensor(
                    out=qn[:], in0=pa3, in1=ss[:].unsqueeze(2).to_broadcast([128, 6, 64]), op=ALU.mult),
                    R=[pak, ssk], W=[qnk])
                S.op("pool", lambda e, qn=qn: e.tensor_tensor(out=qn[:], in0=qn[:], in1=gqk[:], op=ALU.mult),
                     R=[qnk, "gqk"], W=[qnk])
                cB = rope[:, i, 0:32].unsqueeze(1).to_broadcast([128, 6, 32])
                sB = rope[:, i, 32:64].unsqueeze(1).to_broadcast([128, 6, 32])
                t1, t2 = qn[:, :, 0:32], qn[:, :, 32:64]
                S.op("dve", lambda e, ta=ta, t1=t1, cB=cB: e.tensor_tensor(out=ta[:], in0=t1, in1=cB, op=ALU.mult),
                     R=[qnk, "rope"], W=[tak])
                S.op("pool", lambda e, tb=tb, t2=t2, sB=sB: e.tensor_tensor(out=tb[:], in0=t2, in1=sB, op=ALU.mult),
                     R=[qnk, "rope"], W=[tbk])
                S.op("dve", lambda e, tc=tc, t2=t2, cB=cB: e.tensor_tensor(out=tc[:], in0=t2, in1=cB, op=ALU.mult),
                     R=[qnk, "rope"], W=[tck])
                S.op("pool", lambda e, td=td, t1=t1, sB=sB: e.tensor_tensor(out=td[:], in0=t1, in1=sB, op=ALU.mult),
                     R=[qnk, "rope"], W=[tdk])
                def perm_q(ap):
                    return ap.rearrange("p (c e) d -> p e c d", c=2)

                def src_q(ap):
                    return ap.rearrange("p (e c) d -> p e c d", c=2)
                S.op("dve", lambda e, qb=qb, ta=ta, tb=tb: e.tensor_tensor(
                    out=perm_q(qb[:, 0:4, 0:32]), in0=src_q(ta[:, 0:4, :]), in1=src_q(tb[:, 0:4, :]), op=ALU.subtract),
                    R=[tak, tbk], W=[qbk])
                S.op("dve", lambda e, qb=qb, tc=tc, td=td: e.tensor_tensor(
                    out=perm_q(qb[:, 0:4, 32:64]), in0=src_q(tc[:, 0:4, :]), in1=src_q(td[:, 0:4, :]), op=ALU.add),
                    R=[tck, tdk], W=[qbk])
                S.op("pool", lambda e, qb=qb, ta=ta, tb=tb: e.tensor_tensor(out=qb[:, 4:6, 0:32], in0=ta[:, 4:6, :],
                                                                            in1=tb[:, 4:6, :], op=ALU.subtract),
                     R=[tak, tbk], W=[qbk])
                S.op("pool", lambda e, qb=qb, tc=tc, td=td: e.tensor_tensor(out=qb[:, 4:6, 32:64], in0=tc[:, 4:6, :],
                                                                            in1=td[:, 4:6, :], op=ALU.add),
                     R=[tck, tdk], W=[qbk])
                ptb = pt[:, :].bitcast(BF16)
                qb2 = qb[:].rearrange("p h d -> p (h d)")
                for c in range(3):
                    S.op("pe", lambda e, c=c, ptb=ptb, qb2=qb2: e.transpose(out=ptb[:, c * 128:(c + 1) * 128],
                                                                             in_=qb2[:, c * 128:(c + 1) * 128],
                                                                             identity=k.ident[:]),
                         R=[qbk, "ident"], W=[ptk])
                S.op("act", lambda e, i=i, ptb=ptb: e.activation(
                    out=QT[:, :, i * 128:(i + 1) * 128], in_=ptb[:, 0:256].rearrange("p (c t) -> p c t", c=2),
                    func=AF.Copy), R=[ptk], W=["QT"])
                S.op("act", lambda e, i=i, ptb=ptb: e.activation(out=KT[:, i * 128:(i + 1) * 128], in_=ptb[:, 256:384],
                                                                  func=AF.Copy), R=[ptk], W=["KT"])
                vsrc = pa[:, 384:512].rearrange("p (h d) -> p h d", d=64)
                S.op("act", lambda e, i=i, vsrc=vsrc: e.activation(out=VA[:, i, 0:4:2, 0:64], in_=vsrc, func=AF.Copy),
                     R=[pak, "Vones"], W=["V"])
                S.op("act", lambda e, i=i, vsrc=vsrc: e.activation(out=VA[:, i, 1:4:2, 64:128], in_=vsrc, func=AF.Copy),
                     R=[pak, "Vones"], W=["V"])

            for i in range(NT):
                tile_body(i)
            S.barrier()
            S.flush()
        chunks = [[], []]
        for h in range(4):
            it = Item()
            c, e2, hk = h % 2, h // 2, h // 2
            it.KT = (lambda kt: KT[:, kt * 128:(kt + 1) * 128])
            it.qsrc = (lambda q0, nq, c=c: QT[:, c, q0:q0 + nq])
            it.qmask = qmask[:, e2:e2 + 1]
            it.VA = (lambda kt, h=h: VA[:, kt, h, :])
            it.kkey, it.qkey, it.vkey = "KT", "QT", "V"
            it.c, it.e = h // 2, h % 2
            chunks[it.c].append(it)
        attention_phase(k, l, "gqa", chunks, GQA_SCALE, GtT)


def mixer_mla(k, l):
    nc, S, I = k.nc, k.S, k.I
    S.phase = "mla%d" % l
    with ExitStack() as mx:
        QT = _sb(mx, nc, "mQT", [128, 4, T], BF16)
        KT = _sb(mx, nc, "mKT", [128, 4, T], BF16)
        VA = _sb(mx, nc, "mVA", [128, NT, 4, 128], BF16)
        GtT = _sb(mx, nc, "mGtT", [128, 2, T], BF16)
        with ExitStack() as es:
            xs = XmStream(k, es)
            w = _sb(es, nc, "mw", [128, 8, 672], BF16)
            wuq = _sb(es, nc, "wuq", [128, 2, 384], BF16)
            wukv = _sb(es, nc, "wukv", [128, 1, 512], BF16)
            stg = None
            load_w(k, w, "mw", I["in_w"][l][:, 0:672], 672, stg)
            load_w(k, wuq, "wuq", I["mla_w_uq"][l], 384, stg, rows=2)
            load_w(k, wukv, "wukv", I["mla_w_ukv"][l], 512, stg, rows=1)
            rope = _sb(es, nc, "mrope", [128, NT, 32], F32)
            S.dma("act", rope[:], I["rope32"].rearrange("(i p) c -> p i c", p=128), W=["rope"])
            gB = _sb(es, nc, "mgB", [128, 384], F32)
            S.dma("act", gB[:, 0:256], I["mla_q_norm"][l:l + 1, :].to_broadcast([128, 256]), W=["gB"])
            S.dma("act", gB[:, 256:384], I["mla_kv_norm"][l:l + 1, :].to_broadcast([128, 128]), W=["gB"])
            S.op("pool", lambda e: e.memset(VA[:], 1.0), W=["Vones"])
            psg = psring(k, [2])
            junkr = ring(es, nc, "mjunk", 2, [128, 256], F32)
            ssr = ring(es, nc, "mss", 2, [128, 2], F32)
            nbr = ring(es, nc, "mnb", 2, [128, 384], BF16)
            nTr = ring(es, nc, "mnT", 2, [128, 3, 128], BF16)
            qtr = ring(es, nc, "mqt", 2, [128, 4, 96], BF16)
            ktr = ring(es, nc, "mkt", 2, [128, 4, 96], BF16)
            r16 = [ring(es, nc, "mr%d" % n, 2, [128, 4, 16], F32) for n in range(4)]
            k16 = [ring(es, nc, "mk%d" % n, 2, [128, 16], F32) for n in range(4)]
            kper = ring(es, nc, "mkpe", 2, [128, 32], F32)
            psA = psring(k, [0, 1])

            def tile_body(i):
                pa, pak = psA.next()
                pb, pbk = k.ps[2], "ps2"
                pt, ptk = k.ps[3], "ps3"
                pq, pqk = k.ps[4], "ps4"
                pk_, pkk = k.ps[5], "ps5"
                pt2, pt2k = (k.ps[6], "ps6") if i % 2 == 0 else (k.ps[7], "ps7")
                if i % 4 == 0:
                    gate_fm_chunk(k, xs, i // 4, w, "mw", 416, GtT, psg)
                inproj_tm(k, pa, pak, xs, i, w, "mw", 0, 416)
                junk, jk = junkr.next()
                ss, ssk = ssr.next()
                nb, nbk = nbr.next()
                nT, nTk = nTr.next()
                S.op("act", lambda e: e.activation(out=junk[:, 0:256], in_=pa[:, 0:256], func=AF.Square,
                                                   accum_out=ss[:, 0:1]), R=[pak], W=[jk, ssk])
                S.op("act", lambda e: e.activation(out=junk[:, 0:128], in_=pa[:, 256:384], func=AF.Square,
                                                   accum_out=ss[:, 1:2]), R=[pak], W=[jk, ssk])
                S.op("act", lambda e: e.activation(out=ss[:, 0:1], in_=ss[:, 0:1], func=AF.Sqrt, bias=1e-6,
                                                   scale=1.0 / 256), R=[ssk], W=[ssk])
                S.op("act", lambda e: e.activation(out=ss[:, 1:2], in_=ss[:, 1:2], func=AF.Sqrt, bias=1e-6,
                                                   scale=1.0 / 128), R=[ssk], W=[ssk])
                S.op("dve", lambda e: e.reciprocal(out=ss[:], in_=ss[:]), R=[ssk], W=[ssk])
                S.op("dve", lambda e: e.scalar_tensor_tensor(out=nb[:, 0:256], in0=pa[:, 0:256], scalar=ss[:, 0:1],
                                                             in1=gB[:, 0:256], op0=ALU.mult, op1=ALU.mult),
                     R=[pak, ssk, "gB"], W=[nbk])
                S.op("dve", lambda e: e.scalar_tensor_tensor(out=nb[:, 256:384], in0=pa[:, 256:384], scalar=ss[:, 1:2],
                                                             in1=gB[:, 256:384], op0=ALU.mult, op1=ALU.mult),
                     R=[pak, ssk, "gB"], W=[nbk])
                if CUT <= 1:
                    return
                ptb = pt[:, :].bitcast(BF16)
                for c in range(3):
                    S.op("pe", lambda e, c=c: e.transpose(out=ptb[:, c * 128:(c + 1) * 128],
                                                          in_=nb[:, c * 128:(c + 1) * 128], identity=k.ident[:]),
                         R=[nbk, "ident"], W=[ptk])
                S.op("act", lambda e: e.activation(out=nT[:].rearrange("p c t -> p (c t)"), in_=ptb[:, 0:384],
                                                   func=AF.Copy), R=[ptk], W=[nTk])
                if CUT <= 2:
                    return
                for c in range(2):
                    S.op("pe", lambda e, c=c: e.matmul(pq[:, 0:384], lhsT=nT[:, c, :], rhs=wuq[:, c, :],
                                                       start=(c == 0), stop=(c == 1)),
                         R=[nTk] + wkeys("wuq", 0, 384), W=[pqk])
                S.op("pe", lambda e: e.matmul(pk_[:, 0:512], lhsT=nT[:, 2, :], rhs=wukv[:, 0, :], start=True, stop=True),
                     R=[nTk] + wkeys("wukv", 0, 512), W=[pkk])
                if CUT <= 3:
                    return
                qt, qtk = qtr.next()
                kt_, ktk = ktr.next()
                pq3 = pq[:, 0:384].rearrange("p (h d) -> p h d", d=96)
                pk3 = pk_[:, 0:512].rearrange("p (h d) -> p h d", d=128)
                S.op("dve", lambda e: e.tensor_copy(out=qt[:, :, 0:64], in_=pq3[:, :, 0:64]), R=[pqk], W=[qtk])
                S.op("act", lambda e: e.activation(out=kt_[:, :, 0:64], in_=pk3[:, :, 0:64], func=AF.Copy),
                     R=[pkk], W=[ktk])
                S.op("act", lambda e: e.activation(out=VA[:, i, 0:4:2, 0:64], in_=pk3[:, 0:4:2, 64:128], func=AF.Copy),
                     R=[pkk, "Vones"], W=["V"])
                S.op("dve", lambda e: e.tensor_copy(out=VA[:, i, 1:4:2, 64:128], in_=pk3[:, 1:4:2, 64:128]),
                     R=[pkk, "Vones"], W=["V"])
                if CUT <= 4:
                    return
                cB = rope[:, i, 0:16].unsqueeze(1).to_broadcast([128, 4, 16])
                sB = rope[:, i, 16:32].unsqueeze(1).to_broadcast([128, 4, 16])
                t1, t2 = pq3[:, :, 64:80], pq3[:, :, 80:96]
                (a, ak), (b, bk), (c_, ck), (d_, dk) = [r.next() for r in r16]
                S.op("dve", lambda e: e.tensor_tensor(out=a[:], in0=t1, in1=cB, op=ALU.mult), R=[pqk, "rope"], W=[ak])
                S.op("dve", lambda e: e.tensor_tensor(out=b[:], in0=t2, in1=sB, op=ALU.mult), R=[pqk, "rope"], W=[bk])
                S.op("dve", lambda e: e.tensor_tensor(out=c_[:], in0=t2, in1=cB, op=ALU.mult), R=[pqk, "rope"], W=[ck])
                S.op("dve", lambda e: e.tensor_tensor(out=d_[:], in0=t1, in1=sB, op=ALU.mult), R=[pqk, "rope"], W=[dk])
                S.op("pool", lambda e: e.tensor_tensor(out=qt[:, :, 64:80], in0=a[:], in1=b[:], op=ALU.subtract),
                     R=[ak, bk], W=[qtk])
                S.op("pool", lambda e: e.tensor_tensor(out=qt[:, :, 80:96], in0=c_[:], in1=d_[:], op=ALU.add),
                     R=[ck, dk], W=[qtk])
                if CUT <= 5:
                    return
                c1, s1 = rope[:, i, 0:16], rope[:, i, 16:32]
                u1, u2 = pa[:, 384:400], pa[:, 400:416]
                (a2, a2k), (b2, b2k), (c2, c2k), (d2, d2k) = [r.next() for r in k16]
                kpe, kpek = kper.next()
                S.op("dve", lambda e: e.tensor_tensor(out=a2[:], in0=u1, in1=c1, op=ALU.mult), R=[pak, "rope"], W=[a2k])
                S.op("dve", lambda e: e.tensor_tensor(out=b2[:], in0=u2, in1=s1, op=ALU.mult), R=[pak, "rope"], W=[b2k])
                S.op("dve", lambda e: e.tensor_tensor(out=c2[:], in0=u2, in1=c1, op=ALU.mult), R=[pak, "rope"], W=[c2k])
                S.op("dve", lambda e: e.tensor_tensor(out=d2[:], in0=u1, in1=s1, op=ALU.mult), R=[pak, "rope"], W=[d2k])
                S.op("pool", lambda e: e.tensor_tensor(out=kpe[:, 0:16], in0=a2[:], in1=b2[:], op=ALU.subtract),
                     R=[a2k, b2k], W=[kpek])
                S.op("pool", lambda e: e.tensor_tensor(out=kpe[:, 16:32], in0=c2[:], in1=d2[:], op=ALU.add),
                     R=[c2k, d2k], W=[kpek])
                S.op("pool", lambda e: e.tensor_copy(out=kt_[:, :, 64:96],
                                                     in_=kpe[:].unsqueeze(1).to_broadcast([128, 4, 32])),
                     R=[kpek], W=[ktk])
                if CUT <= 6:
                    return
                pt2b = pt2[:, :].bitcast(BF16)
                for h in range(4):
                    S.op("pe", lambda e, h=h: e.transpose(out=pt2b[0:96, h * 128:(h + 1) * 128], in_=qt[:, h, :],
                                                          identity=k.ident[:]), R=[qtk, "ident"], W=[pt2k])
                    S.op("pe", lambda e, h=h: e.transpose(out=pt2b[0:96, 512 + h * 128:512 + (h + 1) * 128],
                                                          in_=kt_[:, h, :], identity=k.ident[:]),
                         R=[ktk, "ident"], W=[pt2k])
                S.op("act", lambda e: e.activation(out=QT[0:96, :, i * 128:(i + 1) * 128],
                                                   in_=pt2b[0:96, 0:512].rearrange("p (h t) -> p h t", h=4),
                                                   func=AF.Copy), R=[pt2k], W=["QT"])
                S.op("act", lambda e: e.activation(out=KT[0:96, :, i * 128:(i + 1) * 128],
                                                   in_=pt2b[0:96, 512:1024].rearrange("p (h t) -> p h t", h=4),
                                                   func=AF.Copy), R=[pt2k], W=["KT"])

            for i in range(NT):
                tile_body(i)
            S.barrier()
            S.flush()
        chunks = [[], []]
        for h in range(4):
            it = Item()
            it.KT = (lambda kt, h=h: KT[0:96, h, kt * 128:(kt + 1) * 128])
            it.QT = (lambda q0, nq, h=h: QT[0:96, h, q0:q0 + nq])
            it.VA = (lambda kt, h=h: VA[:, kt, h, :])
            it.kkey, it.qkey, it.vkey = "KT", "QT", "V"
            it.c, it.e = h // 2, h % 2
            chunks[it.c].append(it)
        if not SKIP_ATT:
            attention_phase(k, l, "mla", chunks, MLA_SCALE, GtT)


def mixer_diff(k, l):
    nc, S, I = k.nc, k.S, k.I
    S.phase = "diff%d" % l
    lam_init = 0.8 - 0.6 * math.exp(-0.3 * l)
    with ExitStack() as mx:
        QT = _sb(mx, nc, "dQT", [128, 2, T], BF16)
        KT = _sb(mx, nc, "dKT", [128, 2, T], BF16)
        qmask = _sb(mx, nc, "dqmask", [128, 6], F32)
        S.dma("sp", qmask[:], I["qmask"], W=["qmask"])
        VA = _sb(mx, nc, "dVA", [128, NT, 4, 128], BF16)
        GtT = _sb(mx, nc, "dGtT", [128, 2, T], BF16)
        neglam = _sb(mx, nc, "neglam", [128, 2], F32)
        subcol = _sb(mx, nc, "subcol", [128, 1], F32)
        bones = _sb(mx, nc, "dbones", [128, 128], F32)
        S.dma("sp", bones[:], I["bones"], W=["bones"])
        with ExitStack() as es:
            xs = XmStream(k, es)
            w = _sb(es, nc, "dw", [128, 8, 1024], BF16)
            stg = None
            load_w(k, w, "dw", I["in_w"][l][:, O_DQ:O_DQ + 1024], 1024, stg)
            rope = _sb(es, nc, "drope", [128, NT, 32], F32)
            S.dma("act", rope[:], I["rope32"].rearrange("(i p) c -> p i c", p=128), W=["rope"])
            S.dma("act", subcol[:], I["subcol"][l], W=["subcol"])
            S.op("pool", lambda e: e.tensor_scalar(out=subcol[:], in0=subcol[:], scalar1=float(1.0 - lam_init),
                                                   scalar2=None, op0=ALU.mult), R=["subcol"], W=["subcol"])
            S.op("pool", lambda e: e.memset(VA[:], 1.0), W=["Vones"])
            psg = psring(k, [6, 7])
            dl = _sb(es, nc, "dl", [1, 128], F32)
            pr = _sb(es, nc, "dpr", [1, 2, 32], F32)
            sm = _sb(es, nc, "dsm", [1, 2], F32)
            nl = _sb(es, nc, "dnl", [1, 2], F32)
            one = _sb(es, nc, "done", [1, 128], F32)
            S.dma("sp", dl[:], I["diff_lambda"][l:l + 1].rearrange("o a b -> o (a b)"), W=["dl"])
            S.op("pool", lambda e: e.memset(one[:], 1.0), W=["one"])
            S.op("dve", lambda e: e.tensor_tensor(out=pr[:, 0, :], in0=dl[:, 0:32], in1=dl[:, 32:64], op=ALU.mult),
                 R=["dl"], W=["pr"])
            S.op("dve", lambda e: e.tensor_tensor(out=pr[:, 1, :], in0=dl[:, 64:96], in1=dl[:, 96:128], op=ALU.mult),
                 R=["dl"], W=["pr"])
            S.op("dve", lambda e: e.tensor_reduce(out=sm[:], in_=pr[:], axis=AX.X, op=ALU.add), R=["pr"], W=["sm"])
            S.op("act", lambda e: e.activation(out=sm[:], in_=sm[:], func=AF.Exp), R=["sm"], W=["sm"])
            S.op("dve", lambda e: e.tensor_tensor(out=nl[:, 0:1], in0=sm[:, 1:2], in1=sm[:, 0:1], op=ALU.subtract),
                 R=["sm"], W=["nl"])
            S.op("dve", lambda e: e.tensor_scalar(out=nl[:, 0:1], in0=nl[:, 0:1], scalar1=float(-lam_init), scalar2=None,
                                                  op0=ALU.add), R=["nl"], W=["nl"])
            S.op("dve", lambda e: e.tensor_copy(out=nl[:, 1:2], in_=nl[:, 0:1]), R=["nl"], W=["nl"])
            S.op("pe", lambda e: e.matmul(k.ps[6][:, 0:2], lhsT=one[:], rhs=nl[:], start=True, stop=True),
                 R=["one", "nl"], W=["ps6"])
            S.op("dve", lambda e: e.tensor_copy(out=neglam[:], in_=k.ps[6][:, 0:2]), R=["ps6"], W=["neglam"])
            r16 = [ring(es, nc, "dr%d" % n, 2, [128, 16, 16], F32) for n in range(4)]
            qbr = ring(es, nc, "dqb", 2, [128, 16, 32], BF16)
            psA, psB, psT = psring(k, [0, 1]), psring(k, [2, 3]), psring(k, [4, 5])

            def tile_body(i):
                pa, pak = psA.next()
                pb, pbk = psB.next()
                pt, ptk = psT.next()
                if i % 4 == 0:
                    gate_fm_chunk(k, xs, i // 4, w, "dw", 768, GtT, psg)
                inproj_tm(k, pa, pak, xs, i, w, "dw", 0, 512)
                inproj_tm(k, pb, pbk, xs, i, w, "dw", 512, 256)
                pa3 = pa[:, :].rearrange("p (u d) -> p u d", d=32)
                cB = rope[:, i, 0:16].unsqueeze(1).to_broadcast([128, 16, 16])
                sB = rope[:, i, 16:32].unsqueeze(1).to_broadcast([128, 16, 16])
                t1, t2 = pa3[:, :, 0:16], pa3[:, :, 16:32]
                (a, ak), (b, bk), (c_, ck), (d_, dk) = [r.next() for r in r16]
                qb, qbk = qbr.next()
                S.op("dve", lambda e: e.tensor_tensor(out=a[:], in0=t1, in1=cB, op=ALU.mult), R=[pak, "rope"], W=[ak])
                S.op("dve", lambda e: e.tensor_tensor(out=b[:], in0=t2, in1=sB, op=ALU.mult), R=[pak, "rope"], W=[bk])
                S.op("dve", lambda e: e.tensor_tensor(out=c_[:], in0=t2, in1=cB, op=ALU.mult), R=[pak, "rope"], W=[ck])
                S.op("dve", lambda e: e.tensor_tensor(out=d_[:], in0=t1, in1=sB, op=ALU.mult), R=[pak, "rope"], W=[dk])
                S.op("pool", lambda e: e.tensor_tensor(out=qb[:, :, 0:16], in0=a[:], in1=b[:], op=ALU.subtract),
                     R=[ak, bk], W=[qbk])
                S.op("pool", lambda e: e.tensor_tensor(out=qb[:, :, 16:32], in0=c_[:], in1=d_[:], op=ALU.add),
                     R=[ck, dk], W=[qbk])
                ptb = pt[:, :].bitcast(BF16)
                qb2 = qb[:].rearrange("p u d -> p (u d)")
                for c in range(4):
                    S.op("pe", lambda e, c=c: e.transpose(out=ptb[:, c * 128:(c + 1) * 128],
                                                          in_=qb2[:, c * 128:(c + 1) * 128], identity=k.ident[:]),
                         R=[qbk, "ident"], W=[ptk])
                S.op("act", lambda e: e.activation(out=QT[:, :, i * 128:(i + 1) * 128],
                                                   in_=ptb[:, 0:256].rearrange("p (c t) -> p c t", c=2), func=AF.Copy),
                     R=[ptk], W=["QT"])
                S.op("act", lambda e: e.activation(out=KT[:, :, i * 128:(i + 1) * 128],
                                                   in_=ptb[:, 256:512].rearrange("p (c t) -> p c t", c=2), func=AF.Copy),
                     R=[ptk], W=["KT"])
                vsrc = pb[:, 0:256].rearrange("p (h d) -> p h d", d=64)
                S.op("act", lambda e: e.activation(out=VA[:, i, 0:4:2, 0:64], in_=vsrc[:, 0:4:2, :], func=AF.Copy),
                     R=[pbk, "Vones"], W=["V"])
                S.op("act", lambda e: e.activation(out=VA[:, i, 1:4:2, 64:128], in_=vsrc[:, 1:4:2, :], func=AF.Copy),
                     R=[pbk, "Vones"], W=["V"])

            for i in range(NT):
                tile_body(i)
            S.barrier()
            S.flush()

        chunks = [[], []]
        for h in range(4):
            for which in range(2):
                it = Item()
                hp, ul = h // 2, (h % 2) * 2 + which
                it.KT = (lambda kt, hp=hp: KT[:, hp, kt * 128:(kt + 1) * 128])
                it.qsrc = (lambda q0, nq, hp=hp: QT[:, hp, q0:q0 + nq])
                it.qmask = qmask[:, 2 + ul:3 + ul]
                it.VA = (lambda kt, h=h: VA[:, kt, h, :])
                it.kkey, it.qkey, it.vkey = "KT", "QT", "V"
                it.c, it.e, it.which = h // 2, h % 2, which
                chunks[it.c].append(it)
        attention_phase(k, l, "diff", chunks, DIFF_SCALE, GtT, kind="diff",
                        extra={"neglam": neglam, "subcol": subcol, "bones": bones})


def xm_chunks():
    return [(ci, ci * 512, min(512, T - ci * 512)) for ci in range((T + 511) // 512)]


def mixer_rwkv(k, l):
    nc, S, I = k.nc, k.S, k.I
    S.phase = "rwkv%d" % l
    SH, SC, BT = k.SH, k.SC, k.BT
    with ExitStack() as mx:
        Vtm = _sb(mx, nc, "rVtm", [128, NT, 256], BF16)
        Gt = _sb(mx, nc, "rGt", [128, NT, 256], BF16)
        PCs = _sb(mx, nc, "rPC", [128, 2, 2, NT], F32)
        cols = _sb(mx, nc, "rcols", [128, 2, 7], F32)
        omk = _sb(mx, nc, "romk", [128, 2], F32)
        p1 = ExitStack()
        tw = _sb(p1, nc, "rtw", [128, T], BF16)
        adb = _sb(p1, nc, "radb", [128, T], BF16)
        S.dma("sp", cols[:], I["rw_cols"][l], W=["cols"])
        for hp in range(2):
            S.op("dve", lambda e, hp=hp: e.tensor_scalar(out=omk[:, hp:hp + 1], in0=cols[:, hp, 1:2], scalar1=-1.0,
                                                        scalar2=1.0, op0=ALU.mult, op1=ALU.add), R=["cols"], W=["omk"])
        with ExitStack() as es:
            xs = XmStream(k, es)
            w = _sb(es, nc, "rw", [128, 8, 1280], BF16)
            load_w(k, w, "rw", I["in_w"][l][:, O_RW:O_RW + 1280], 1280)
            mu = _sb(es, nc, "rmu", [128, 2, 8], F32)
            mu0 = _sb(es, nc, "rmu0", [128, 8], F32)
            S.dma("sp", mu[:], I["rw_mu"][l], W=["mu"])
            S.op("dve", lambda e: e.tensor_tensor(out=mu0[:], in0=mu[:, 0, :], in1=mu[:, 1, :], op=ALU.add),
                 R=["mu"], W=["mu0"])
            S.op("dve", lambda e: e.tensor_scalar(out=mu0[:], in0=mu0[:], scalar1=-1.0, scalar2=1.0, op0=ALU.mult,
                                                  op1=ALU.add), R=["mu0"], W=["mu0"])
            rawr = ring(es, nc, "rraw", 2, [128, T], F32)
            shfr = ring(es, nc, "rshf", 2, [128, T], F32)
            psr = psring(k, [0, 1, 2, 3])

            def fpair(fs):
                bufs = []
                for f in fs:
                    raw, rawk = rawr.next()
                    shf, shfk = shfr.next()
                    bufs.append((f, raw, rawk, shf, shfk))
                for (ci, q0, nq) in xm_chunks():
                    xb, xbk = xs.chunk(ci)
                    for (f, raw, rawk, shf, shfk) in bufs:
                        ps, pk = psr.next()
                        for kc in range(8):
                            S.op("pe", lambda e, kc=kc, ps=ps, xb=xb, nq=nq, f=f: e.matmul(
                                ps[:, 0:nq], lhsT=w[:, kc, f * 128:(f + 1) * 128], rhs=xb[:, kc, 0:nq],
                                start=(kc == 0), stop=(kc == 7)),
                                R=["%s_%d" % (xbk, kc)] + wkeys("rw", f * 128, 128), W=[pk], c=0.23)
                        S.op("act", lambda e, ps=ps, q0=q0, nq=nq, raw=raw: e.activation(
                            out=raw[:, q0:q0 + nq], in_=ps[:, 0:nq], func=AF.Copy), R=[pk], W=[rawk])
                for (f, raw, rawk, shf, shfk) in bufs:
                    finish_f(f, raw, rawk, shf, shfk)

            def finish_f(f, raw, rawk, shf, shfk):
                S.op("dve", lambda e: e.tensor_scalar(out=shf[:], in0=raw[:], scalar1=mu0[:, f:f + 1], scalar2=None,
                                                      op0=ALU.mult), R=[rawk, "mu0"], W=[shfk])
                for (a, b) in ((0, NCTX), (NCTX, T)):
                    S.op("dve", lambda e, a=a, b=b: e.scalar_tensor_tensor(
                        out=shf[:, a + 1:b], in0=raw[:, a:b - 1], scalar=mu[:, 0, f:f + 1], in1=shf[:, a + 1:b],
                        op0=ALU.mult, op1=ALU.add), R=[rawk, "mu", shfk], W=[shfk])
                    S.op("dve", lambda e, a=a, b=b: e.scalar_tensor_tensor(
                        out=shf[:, a:b - 1], in0=raw[:, a + 1:b], scalar=mu[:, 1, f:f + 1], in1=shf[:, a:b - 1],
                        op0=ALU.mult, op1=ALU.add), R=[rawk, "mu", shfk], W=[shfk])
                if f == 6:
                    S.op("act", lambda e: e.activation(out=tw[:], in_=shf[:], func=AF.Tanh), R=[shfk], W=["tw"])
                elif f == 7:
                    S.op("act", lambda e: e.activation(out=adb[:], in_=shf[:], func=AF.Copy), R=[shfk], W=["adb"])
                else:
                    S.dma("sp", SH[f], shf[:], R=[shfk], W=["SH%d" % f])

            for fs in ((6, 7), (0, 1), (2, 3), (4, 5)):
                fpair(fs)
            psg = psring(k, [4, 5])

            def gate_tile(i):
                pg, pgk = psg.next()
                inproj_tm(k, pg, pgk, xs, i, w, "rw", 1024, 256)
                S.op("act", lambda e: e.activation(out=Gt[:, i, :], in_=pg[:, 0:256], func=AF.Silu), R=[pgk],
                     W=["Gt%d" % i])

            for i in range(NT):
                gate_tile(i)
            S.barrier()
            S.flush()
        if CUT <= 1:
            p1.close()
            return
        with ExitStack() as es:
            wup = _sb(es, nc, "rwup", [128, 1, 256], BF16)
            aup = _sb(es, nc, "raup", [128, 1, 256], BF16)
            bones = _sb(es, nc, "rbones", [128, 128], F32)
            rm4 = _sb(es, nc, "rrm4", [128, 512], F32)
            load_w(k, wup, "wup", I["rwkv_w_up"][l].rearrange("d j n -> (d j) n"), 256, rows=1)
            load_w(k, aup, "aup", I["rwkv_a_up"][l].rearrange("d j n -> (d j) n"), 256, rows=1)
            S.dma("sp", bones[:], I["bones"], W=["bones"])
            S.op("pool", lambda e: e.memset(rm4[:], 1.0), W=["rm4"])
            for j in range(4):
                S.op("pool", lambda e, j=j: e.memset(rm4[:, j * 128:j * 128 + 1], 0.0), R=["rm4"], W=["rm4"])
            names = ["rr", "kr", "vr", "kk0", "sq", "rinv", "kap", "sg", "av", "lw", "cpre", "cc", "e1", "E1", "E2", "E3",
                     "e4", "E4", "tt", "kd", "be", "ks", "bt0"]
            R_ = {n: ring(es, nc, "r_" + n, 2, [128, 512], F32) for n in names}
            vbr = ring(es, nc, "r_vb", 2, [128, 512], BF16)
            btr = ring(es, nc, "r_btb", 2, [128, 512], BF16)
            str_ = ring(es, nc, "r_st", 2, [128, 6, 512], BF16)
            psr = psring(k, [0, 1, 2, 3, 4, 5])
            pst = psring(k, [6, 7])

            def blk(hp, ci, q0, nq):
                nt = nq // 128
                g = {n: R_[n].next() for n in names}

                def A(n):
                    return g[n][0][:, 0:nq]

                def Kk(n):
                    return g[n][1]
                S.dma("sp", A("rr"), SH[hp][:, q0:q0 + nq], W=[Kk("rr")])
                S.dma("sp", A("kr"), SH[2 + hp][:, q0:q0 + nq], W=[Kk("kr")])
                S.dma("act", A("vr"), SH[4 + hp][:, q0:q0 + nq], W=[Kk("vr")])
                S.op("dve", lambda e: e.tensor_scalar(out=A("kk0"), in0=A("kr"), scalar1=cols[:, hp, 0:1], scalar2=None,
                                                      op0=ALU.mult), R=[Kk("kr"), "cols"], W=[Kk("kk0")])
                S.op("pool", lambda e: e.tensor_tensor(out=A("sq"), in0=A("kk0"), in1=A("kk0"), op=ALU.mult),
                     R=[Kk("kk0")], W=[Kk("sq")])
                ps, pk = psr.next()
                S.op("pe", lambda e: e.matmul(ps[:, 0:nq], lhsT=bones[:], rhs=A("sq"), start=True, stop=True),
                     R=["bones", Kk("sq")], W=[pk])
                S.op("act", lambda e: e.activation(out=A("rinv"), in_=ps[:, 0:nq], func=AF.Sqrt, bias=1e-12, scale=1.0),
                     R=[pk], W=[Kk("rinv")])
                S.op("dve", lambda e: e.reciprocal(out=A("rinv"), in_=A("rinv")), R=[Kk("rinv")], W=[Kk("rinv")])
                S.op("dve", lambda e: e.tensor_tensor(out=A("kap"), in0=A("kk0"), in1=A("rinv"), op=ALU.mult),
                     R=[Kk("kk0"), Kk("rinv")], W=[Kk("kap")])
                vb, vbk = vbr.next()
                S.op("pool", lambda e: e.tensor_copy(out=vb[:, 0:nq], in_=A("vr")), R=[Kk("vr")], W=[vbk])
                pt, ptk = pst.next()
                ptb = pt[:, :].bitcast(BF16)
                for j in range(nt):
                    S.op("pe", lambda e, j=j: e.transpose(out=ptb[:, j * 128:(j + 1) * 128],
                                                          in_=vb[:, j * 128:(j + 1) * 128], identity=k.ident[:]),
                         R=[vbk, "ident"], W=[ptk])
                t0 = q0 // 128
                S.op("act", lambda e: e.activation(out=Vtm[:, t0:t0 + nt, hp * 128:(hp + 1) * 128],
                                                   in_=ptb[:, 0:nq].rearrange("p (j c) -> p j c", c=128), func=AF.Copy),
                     R=[ptk], W=["Vtm"])
                for d in range(2):
                    rows = slice(64 * d, 64 * d + 64)
                    pz, pzk = psr.next()
                    pa, pak = psr.next()
                    S.op("pe", lambda e, pz=pz, rows=rows: e.matmul(
                        pz[:, 0:nq], lhsT=wup[rows, 0, hp * 128:(hp + 1) * 128], rhs=tw[rows, q0:q0 + nq],
                        start=True, stop=True), R=["tw"] + wkeys("wup", 0, 256), W=[pzk])
                    S.op("pe", lambda e, pa=pa, rows=rows: e.matmul(
                        pa[:, 0:nq], lhsT=aup[rows, 0, hp * 128:(hp + 1) * 128], rhs=adb[rows, q0:q0 + nq],
                        start=True, stop=True), R=["adb"] + wkeys("aup", 0, 256), W=[pak])
                    S.op("act", lambda e, pz=pz, d=d: e.activation(out=A("sg"), in_=pz[:, 0:nq], func=AF.Sigmoid,
                                                                   bias=cols[:, hp, 3 + d:4 + d], scale=1.0),
                         R=[pzk, "cols"], W=[Kk("sg")])
                    S.op("act", lambda e, pa=pa, d=d: e.activation(out=A("av"), in_=pa[:, 0:nq], func=AF.Sigmoid,
                                                                   bias=cols[:, hp, 5 + d:6 + d], scale=1.0),
                         R=[pak, "cols"], W=[Kk("av")])
                    S.op("dve", lambda e: e.tensor_scalar(out=A("lw"), in0=A("sg"), scalar1=float(-DECAY_SCALE),
                                                          scalar2=None, op0=ALU.mult), R=[Kk("sg")], W=[Kk("lw")])
                    S.op("dve", lambda e: e.tensor_tensor_scan(out=A("cpre"), data0=rm4[:, 0:nq], data1=A("lw"),
                                                               initial=0.0, op0=ALU.mult, op1=ALU.add),
                         R=["rm4", Kk("lw")], W=[Kk("cpre")])
                    cp3 = A("cpre").rearrange("p (j c) -> p j c", c=128)
                    cC = cp3[:, :, 127:128]
                    cCb = cC.to_broadcast([128, nt, 128])
                    if d == 0:
                        cn = "cpre"
                    else:
                        cn = "cc"
                        c3 = A("cc").rearrange("p (j c) -> p j c", c=128)
                        S.op("dve", lambda e, c3=c3, cCb=cCb, cp3=cp3: e.tensor_tensor(out=c3, in0=cCb, in1=cp3,
                                                                                      op=ALU.subtract),
                             R=[Kk("cpre")], W=[Kk("cc")])
                        S.op("dve", lambda e: e.tensor_tensor(out=A("cc"), in0=A("cc"), in1=A("lw"), op=ALU.add),
                             R=[Kk("cc"), Kk("lw")], W=[Kk("cc")])
                    cA = A(cn)
                    cK = Kk(cn)
                    cA3 = cA.rearrange("p (j c) -> p j c", c=128)
                    S.op("pool", lambda e, cA=cA: e.tensor_tensor(out=A("e1"), in0=cA, in1=A("lw"), op=ALU.subtract),
                         R=[cK, Kk("lw")], W=[Kk("e1")])
                    S.op("pool", lambda e, cA3=cA3, cCb=cCb: e.tensor_tensor(
                        out=A("e4").rearrange("p (j c) -> p j c", c=128), in0=cCb, in1=cA3, op=ALU.subtract),
                        R=[cK, Kk("cpre")], W=[Kk("e4")])
                    S.op("act", lambda e: e.activation(out=A("E1"), in_=A("e1"), func=AF.Exp), R=[Kk("e1")], W=[Kk("E1")])
                    S.op("act", lambda e, cA=cA: e.activation(out=A("E2"), in_=cA, func=AF.Exp), R=[cK], W=[Kk("E2")])
                    S.op("act", lambda e, cA=cA: e.activation(out=A("E3"), in_=cA, func=AF.Exp, scale=-1.0),
                         R=[cK], W=[Kk("E3")])
                    S.op("act", lambda e: e.activation(out=A("E4"), in_=A("e4"), func=AF.Exp), R=[Kk("e4")], W=[Kk("E4")])
                    S.op("act", lambda e, d=d, cC=cC: e.activation(
                        out=PCs[:, d, hp, t0:t0 + nt], in_=cC.rearrange("p j o -> p (j o)"), func=AF.Exp),
                        R=[Kk("cpre")], W=["PC"])
                    S.op("dve", lambda e: e.tensor_scalar(out=A("tt"), in0=A("av"), scalar1=cols[:, hp, 1:2],
                                                          scalar2=omk[:, hp:hp + 1], op0=ALU.mult, op1=ALU.add),
                         R=[Kk("av"), "cols", "omk"], W=[Kk("tt")])
                    S.op("pool", lambda e: e.tensor_tensor(out=A("kd"), in0=A("kr"), in1=A("tt"), op=ALU.mult),
                         R=[Kk("kr"), Kk("tt")], W=[Kk("kd")])
                    S.op("pool", lambda e: e.tensor_tensor(out=A("be"), in0=A("av"), in1=A("kap"), op=ALU.mult),
                         R=[Kk("av"), Kk("kap")], W=[Kk("be")])
                    st, stk = str_.next()
                    S.op("dve", lambda e, st=st: e.tensor_tensor(out=st[:, 0, 0:nq], in0=A("kap"), in1=A("E1"), op=ALU.mult),
                         R=[Kk("kap"), Kk("E1")], W=[stk])
                    S.op("pool", lambda e, st=st: e.tensor_tensor(out=st[:, 1, 0:nq], in0=A("rr"), in1=A("E2"), op=ALU.mult),
                         R=[Kk("rr"), Kk("E2")], W=[stk])
                    S.op("dve", lambda e, st=st: e.tensor_tensor(out=st[:, 2, 0:nq], in0=A("kd"), in1=A("E3"), op=ALU.mult),
                         R=[Kk("kd"), Kk("E3")], W=[stk])
                    S.op("pool", lambda e, st=st: e.tensor_tensor(out=st[:, 3, 0:nq], in0=A("be"), in1=A("E3"), op=ALU.mult),
                         R=[Kk("be"), Kk("E3")], W=[stk])
                    S.op("dve", lambda e, st=st: e.tensor_tensor(out=st[:, 4, 0:nq], in0=A("kd"), in1=A("E4"), op=ALU.mult),
                         R=[Kk("kd"), Kk("E4")], W=[stk])
                    S.op("dve", lambda e, st=st: e.scalar_tensor_tensor(out=st[:, 5, 0:nq], in0=A("be"), scalar=-1.0,
                                                                        in1=A("E4"), op0=ALU.mult, op1=ALU.mult),
                         R=[Kk("be"), Kk("E4")], W=[stk])
                    S.dma("sp", SC[d][:, hp, :, q0:q0 + nq], st[:, :, 0:nq], R=[stk], W=["SC%d" % d])
                    if d == 0:
                        S.op("pool", lambda e: e.tensor_copy(out=A("ks"), in_=A("kd")), R=[Kk("kd")], W=[Kk("ks")])
                    else:
                        S.op("pool", lambda e: e.tensor_tensor(out=A("ks"), in0=A("ks"), in1=A("kd"), op=ALU.add),
                             R=[Kk("kd"), Kk("ks")], W=[Kk("ks")])
                btb, btk = btr.next()
                S.op("dve", lambda e: e.scalar_tensor_tensor(out=btb[:, 0:nq], in0=A("rr"), scalar=cols[:, hp, 2:3],
                                                             in1=A("ks"), op0=ALU.mult, op1=ALU.mult),
                     R=[Kk("rr"), "cols", Kk("ks")], W=[btk])
                S.dma("sp", BT[:, hp, q0:q0 + nq], btb[:, 0:nq], R=[btk], W=["BT"])

            for hp in range(2):
                for (ci, q0, nq) in xm_chunks():
                    blk(hp, ci, q0, nq)
            S.barrier()
            S.flush()
        p1.close()
        if CUT <= 2:
            return
        with ExitStack() as es:
            Yacc = _sb(es, nc, "rYacc", [128, NT, 256], F32)
            Hb = [[_sb(es, nc, "rH%d%d" % (hp, j), [128, 64], F32) for j in range(2)] for hp in range(2)]
            Hh = [[_sb(es, nc, "rHh%d%d" % (hp, j), [128, 64], BF16) for j in range(2)] for hp in range(2)]
            masks = _sb(es, nc, "rmask", [128, 2, 640], F32)
            identf = _sb(es, nc, "ridentf", [128, 128], F32)
            bdm = _sb(es, nc, "rbdm", [128, 128], F32)
            hsel = _sb(es, nc, "rhsel", [128, 2], BF16)
            gnB = _sb(es, nc, "rgnB", [128, 2, 256], F32)
            S.dma("sp", masks[:], I["rmask"].rearrange("d p c -> p d c"), W=["masks"])
            S.dma("sp", identf[:], I["identf"], W=["identf"])
            S.dma("sp", bdm[:], I["bones"], W=["bdm"])
            S.dma("sp", hsel[:], I["hsel"], W=["hsel"])
            S.dma("act", gnB[:, 0, :], I["rwkv_gn_w"][l:l + 1, :].to_broadcast([128, 256]), W=["gnB"])
            S.dma("act", gnB[:, 1, :], I["rwkv_gn_b"][l:l + 1, :].to_broadcast([128, 256]), W=["gnB"])
            NB = int(os.environ.get("RW_NB", "4"))
            Xr = ring(es, nc, "rX", NB + 1, [128, 2, 6, 128], BF16)
            A1r = ring(es, nc, "rA1", NB, [128, 4, 256], BF16)
            A2r = ring(es, nc, "rA2", NB, [128, 4, 256], BF16)
            N0r = ring(es, nc, "rN0", NB, [128, 4, 128], BF16)
            Mr = ring(es, nc, "rM", NB, [128, 4, 128], BF16)
            MTr = ring(es, nc, "rMT", NB, [128, 4, 128], BF16)
            Mpr = ring(es, nc, "rMp", NB, [128, 4, 128], BF16)
            Ppr = ring(es, nc, "rPp", NB, [128, 4, 128], BF16)
            lvm = _sb(es, nc, "rlvm", [128, 7, 4, 128], U16)
            S.dma("sp", lvm[:], I["lvmask"], W=["lvm"])
            W0r = ring(es, nc, "rW0", NB, [128, 256], BF16)
            Xtr = ring(es, nc, "rXt", NB, [128, 3, 256], BF16)
            UGr = ring(es, nc, "rUG", NB, [128, 2, 256], BF16)
            Phr = ring(es, nc, "rPh", NB, [128, 2, 128], F32)
            Psr = ring(es, nc, "rPs", NB, [128, 2, 64], F32)
            Rgr = ring(es, nc, "rRg", NB, [128, 2, 128], BF16)
            btlr = ring(es, nc, "rbtl", 2, [128, 2, 128], BF16)
            ysr = ring(es, nc, "rys", 2, [128, 4, 64], F32)
            ycr = ring(es, nc, "ryc", 2, [128, 4, 64], F32)
            sqr = ring(es, nc, "rsq", 2, [128, 4, 64], F32)
            s4r = ring(es, nc, "rs4", 4, [128, 4], F32)
            bsr = ring(es, nc, "rbs", 2, [128, 4], F32)
            obr = ring(es, nc, "rob", 2, [128, 256], BF16)
            oTr = ring(es, nc, "roT", 2, [128, 2, 128], BF16)
            pall = psring(k, [0, 1, 2, 3, 4, 5, 6, 7])
            hcur = [0, 0]

            def hd(h):
                return h // 2, slice(64 * (h % 2), 64 * (h % 2) + 64)

            class U:
                pass

            def pre1(u):
                d, i = u.d, u.i
                u.X, u.Xk = Xr.next()
                X = u.X
                S.dma("sp" if i % 2 == 0 else "act", X[:], SC[d][:, :, :, i * 128:(i + 1) * 128], W=[u.Xk])
                u.A1, u.A1k = A1r.next()
                u.A2, u.A2k = A2r.next()
                u.N0, u.N0k = N0r.next()
                if SUB <= 1:
                    return
                pA = [pall.next() for _ in range(2)]
                pB = [pall.next() for _ in range(2)]
                pC = [pall.next() for _ in range(2)]
                for h in range(4):
                    hp, rows = hd(h)
                    e2 = h % 2
                    rkr = X[rows, hp, 0:2, :].rearrange("p a t -> p (a t)")
                    cs = slice(hp * 256, hp * 256 + 256)
                    S.op("pe", lambda e, hp=hp, rows=rows, rkr=rkr, cs=cs, e2=e2: e.matmul(
                        pA[e2][0][:, cs], lhsT=X[rows, hp, 2, :], rhs=rkr, start=True, stop=True),
                        R=[u.Xk], W=[pA[e2][1]])
                    S.op("pe", lambda e, hp=hp, rows=rows, rkr=rkr, cs=cs, e2=e2: e.matmul(
                        pB[e2][0][:, cs], lhsT=X[rows, hp, 3, :], rhs=rkr, start=True, stop=True),
                        R=[u.Xk], W=[pB[e2][1]])
                    S.op("pe", lambda e, hp=hp, rows=rows, e2=e2: e.matmul(
                        pC[e2][0][:, hp * 128:(hp + 1) * 128], lhsT=X[rows, hp, 0, :], rhs=X[rows, hp, 3, :],
                        start=True, stop=True), R=[u.Xk], W=[pC[e2][1]])
                if SUB <= 2:
                    return
                mA = masks[:, d, 0:256].unsqueeze(1).to_broadcast([128, 2, 256])
                mB = masks[:, d, 256:512].unsqueeze(1).to_broadcast([128, 2, 256])
                mC = masks[:, d, 512:640].unsqueeze(1).to_broadcast([128, 2, 128])
                for e2 in range(2):
                    S.op("dve", lambda e, e2=e2: e.tensor_tensor(
                        out=u.A1[:, e2:4:2, :], in0=pA[e2][0][:, :].rearrange("p (a c) -> p a c", a=2), in1=mA,
                        op=ALU.mult), R=[pA[e2][1], "masks"], W=[u.A1k])
                    S.op("dve", lambda e, e2=e2: e.tensor_tensor(
                        out=u.A2[:, e2:4:2, :], in0=pB[e2][0][:, :].rearrange("p (a c) -> p a c", a=2), in1=mB,
                        op=ALU.mult), R=[pB[e2][1], "masks"], W=[u.A2k])
                    S.op("dve", lambda e, e2=e2: e.tensor_tensor(
                        out=u.N0[:, e2:4:2, :], in0=pC[e2][0][:, 0:256].rearrange("p (a c) -> p a c", a=2), in1=mC,
                        op=ALU.mult), R=[pC[e2][1], "masks"], W=[u.N0k])
                D0, D0k = Mr.next()
                DT0, DT0k = MTr.next()
                idb = identf[:].unsqueeze(1).to_broadcast([128, 4, 128])
                S.op("pool", lambda e: e.tensor_copy(out=D0[:], in_=idb), R=["identf"], W=[D0k])
                S.op("pool", lambda e: e.tensor_copy(out=DT0[:], in_=idb), R=["identf"], W=[DT0k])
                S.op("dve", lambda e: e.copy_predicated(out=D0[:], mask=lvm[:, 0, :, :], data=u.N0[:]),
                     R=[u.N0k, "lvm", D0k], W=[D0k])
                S.op("dve", lambda e: e.copy_predicated(out=DT0[:], mask=lvm[:, 0, :, :], data=u.A2[:, :, 0:128]),
                     R=[u.A2k, "lvm", DT0k], W=[DT0k])
                u.D, u.Dk, u.DT, u.DTk = D0, D0k, DT0, DT0k
                u.TT, u.TTk = DT0, DT0k
                u.lvl = 1

            def merge_level(u):
                lv = u.lvl
                u.lvl += 1
                last = (lv == 6)
                D, Dk, DT, DTk = u.D, u.Dk, u.DT, u.DTk
                pQ = pall.next()
                for h in range(4):
                    S.op("pe", lambda e, h=h: e.matmul(pQ[0][:, h * 128:(h + 1) * 128], lhsT=u.N0[:, h, :], rhs=DT[:, h, :],
                                                       start=True, stop=True), R=[u.N0k, DTk], W=[pQ[1]])
                Q, Qk = Mpr.next()
                S.op("act", lambda e: e.activation(out=Q[:], in_=pQ[0][:, :].rearrange("p (a c) -> p a c", a=4),
                                                   func=AF.Copy), R=[pQ[1]], W=[Qk])
                if not last:
                    pP = pall.next()
                    for h in range(4):
                        S.op("pe", lambda e, h=h: e.matmul(pP[0][:, h * 128:(h + 1) * 128], lhsT=u.A2[:, h, 0:128],
                                                           rhs=D[:, h, :], start=True, stop=True),
                             R=[u.A2k, Dk], W=[pP[1]])
                    P, Pk = Ppr.next()
                    S.op("act", lambda e: e.activation(out=P[:], in_=pP[0][:, :].rearrange("p (a c) -> p a c", a=4),
                                                       func=AF.Copy), R=[pP[1]], W=[Pk])
                pT = pall.next()
                for h in range(4):
                    S.op("pe", lambda e, h=h: e.matmul(pT[0][:, h * 128:(h + 1) * 128], lhsT=D[:, h, :], rhs=Q[:, h, :],
                                                       start=True, stop=True), R=[Dk, Qk], W=[pT[1]])
                if not last:
                    pD = pall.next()
                    for h in range(4):
                        S.op("pe", lambda e, h=h: e.matmul(pD[0][:, h * 128:(h + 1) * 128], lhsT=DT[:, h, :],
                                                           rhs=P[:, h, :], start=True, stop=True),
                             R=[DTk, Pk], W=[pD[1]])
                S.op("dve", lambda e: e.copy_predicated(out=DT[:], mask=lvm[:, lv, :, :],
                                                        data=pT[0][:, :].rearrange("p (a c) -> p a c", a=4)),
                     R=[pT[1], "lvm", DTk], W=[DTk])
                if not last:
                    S.op("dve", lambda e: e.copy_predicated(out=D[:], mask=lvm[:, lv, :, :],
                                                            data=pD[0][:, :].rearrange("p (a c) -> p a c", a=4)),
                         R=[pD[1], "lvm", Dk], W=[Dk])

            def pre2(u):
                d, i, X = u.d, u.i, u.X
                TTf = u.TT
                u.W0, u.W0k = W0r.next()
                u.Xt, u.Xtk = Xtr.next()
                u.UG, u.UGk = UGr.next()
                pW = pall.next()
                for h in range(4):
                    S.op("pe", lambda e, h=h: e.matmul(pW[0][:, h * 64:(h + 1) * 64], lhsT=u.A1[:, h, 0:128],
                                                       rhs=Vtm[:, i, h * 64:(h + 1) * 64], start=True, stop=True),
                         R=[u.A1k, "Vtm"], W=[pW[1]])
                S.op("act", lambda e: e.activation(out=u.W0[:], in_=pW[0][:, 0:256], func=AF.Copy), R=[pW[1]], W=[u.W0k])
                pX = pall.next()
                pXb = pX[0][:, :].bitcast(BF16)
                for a, arr in enumerate((0, 4, 5)):
                    for hp in range(2):
                        S.op("pe", lambda e, a=a, arr=arr, hp=hp: e.transpose(
                            out=pXb[:, (a * 2 + hp) * 128:(a * 2 + hp + 1) * 128], in_=X[:, hp, arr, :],
                            identity=k.ident[:]), R=[u.Xk, "ident"], W=[pX[1]])
                S.op("dve", lambda e: e.tensor_copy(out=u.Xt[:].rearrange("p a c -> p (a c)"), in_=pXb[:, 0:768]),
                     R=[pX[1]], W=[u.Xtk])
                pU = pall.next()
                for h in range(4):
                    S.op("pe", lambda e, h=h: e.matmul(pU[0][:, h * 64:(h + 1) * 64], lhsT=TTf[:, h, :],
                                                       rhs=u.W0[:, h * 64:(h + 1) * 64], start=True, stop=True),
                         R=[u.TTk, u.W0k], W=[pU[1]])
                    S.op("pe", lambda e, h=h: e.matmul(pU[0][:, 256 + h * 64:256 + (h + 1) * 64], lhsT=TTf[:, h, :],
                                                       rhs=u.Xt[:, 0, h * 64:(h + 1) * 64], start=True, stop=True),
                         R=[u.TTk, u.Xtk], W=[pU[1]])
                S.op("act", lambda e: e.activation(out=u.UG[:].rearrange("p a c -> p (a c)"), in_=pU[0][:, :],
                                                   func=AF.Copy), R=[pU[1]], W=[u.UGk])

            def pre3(u):
                d, i, X = u.d, u.i, u.X
                u.Ph, u.Phk = Phr.next()
                u.Ps, u.Psk = Psr.next()
                u.Rg, u.Rgk = Rgr.next()
                pP = pall.next()
                for hp in range(2):
                    S.op("pe", lambda e, hp=hp: e.matmul(pP[0][:, hp * 128:(hp + 1) * 128],
                                                         lhsT=u.UG[:, 1, hp * 128:(hp + 1) * 128],
                                                         rhs=u.Xt[:, 2, hp * 128:(hp + 1) * 128], start=True, stop=True),
                         R=[u.UGk, u.Xtk], W=[pP[1]])
                S.op("dve", lambda e: e.tensor_tensor(out=u.Ph[:], in0=pP[0][:, 0:256].rearrange("p (a c) -> p a c", a=2),
                                                      in1=bdm[:].unsqueeze(1).to_broadcast([128, 2, 128]), op=ALU.mult),
                     R=[pP[1], "bdm"], W=[u.Phk])
                pS = pall.next()
                pR = pall.next()
                for h in range(4):
                    hp, rows = hd(h)
                    S.op("pe", lambda e, h=h, hp=hp, rows=rows: e.matmul(
                        pS[0][rows, hp * 64:(hp + 1) * 64], lhsT=u.Xt[:, 1, h * 64:(h + 1) * 64],
                        rhs=Vtm[:, i, h * 64:(h + 1) * 64], start=True, stop=False),
                        R=[u.Xtk, "Vtm"], W=[pS[1]])
                    S.op("pe", lambda e, h=h, hp=hp, rows=rows: e.matmul(
                        pS[0][rows, hp * 64:(hp + 1) * 64], lhsT=u.Xt[:, 2, h * 64:(h + 1) * 64],
                        rhs=u.UG[:, 0, h * 64:(h + 1) * 64], start=False, stop=True),
                        R=[u.Xtk, u.UGk], W=[pS[1]])
                    S.op("pe", lambda e, h=h, hp=hp, rows=rows: e.matmul(
                        pR[0][rows, hp * 128:(hp + 1) * 128], lhsT=u.UG[:, 1, h * 64:(h + 1) * 64],
                        rhs=u.A2[:, h, 128:256], start=True, stop=True), R=[u.UGk, u.A2k], W=[pR[1]])
                S.op("act", lambda e: e.activation(out=u.Ps[:].rearrange("p a c -> p (a c)"), in_=pS[0][:, 0:128],
                                                   func=AF.Copy), R=[pS[1]], W=[u.Psk])
                S.op("dve", lambda e: e.tensor_tensor(out=u.Rg[:], in0=pR[0][:, 0:256].rearrange("p (a c) -> p a c", a=2),
                                                      in1=X[:, :, 1, :], op=ALU.add), R=[pR[1], u.Xk], W=[u.Rgk])

            def chain(u):
                d, i = u.d, u.i
                pY = pall.next()
                pH = pall.next()
                for h in range(4):
                    hp, rows = hd(h)
                    Hc = Hb[hp][hcur[hp]]
                    Hk = "H%d%d" % (hp, hcur[hp])
                    S.op("pe", lambda e, h=h: e.matmul(pY[0][:, h * 64:(h + 1) * 64], lhsT=u.A1[:, h, 128:256],
                                                       rhs=Vtm[:, i, h * 64:(h + 1) * 64], start=True, stop=False),
                         R=[u.A1k, "Vtm"], W=[pY[1]])
                    S.op("pe", lambda e, h=h: e.matmul(pY[0][:, h * 64:(h + 1) * 64], lhsT=u.A2[:, h, 128:256],
                                                       rhs=u.UG[:, 0, h * 64:(h + 1) * 64], start=False, stop=False),
                         R=[u.A2k, u.UGk], W=[pY[1]])
                    Hcb = Hh[hp][hcur[hp]]
                    S.op("pe", lambda e, h=h, hp=hp, rows=rows, Hcb=Hcb: e.matmul(
                        pY[0][:, h * 64:(h + 1) * 64], lhsT=u.Rg[rows, hp, :], rhs=Hcb[rows, :], start=False, stop=True),
                        R=[u.Rgk, Hk + "b"], W=[pY[1]])
                for hp in range(2):
                    Hc = Hb[hp][hcur[hp]]
                    Hk = "H%d%d" % (hp, hcur[hp])
                    S.op("pe", lambda e, hp=hp, Hc=Hc: e.matmul(pH[0][:, hp * 64:(hp + 1) * 64], lhsT=u.Ph[:, hp, :],
                                                                rhs=Hc[:], start=True, stop=True),
                         R=[u.Phk, Hk], W=[pH[1]])
                for hp in range(2):
                    Hc = Hb[hp][hcur[hp]]
                    Hk = "H%d%d" % (hp, hcur[hp])
                    Hn = Hb[hp][1 - hcur[hp]]
                    Hnk = "H%d%d" % (hp, 1 - hcur[hp])
                    Hnb = Hh[hp][1 - hcur[hp]]
                    S.op("dve", lambda e, hp=hp, Hc=Hc, Hn=Hn: e.scalar_tensor_tensor(
                        out=Hn[:], in0=Hc[:], scalar=PCs[:, d, hp, i:i + 1], in1=pH[0][:, hp * 64:(hp + 1) * 64],
                        op0=ALU.mult, op1=ALU.add), R=[Hk, "PC", pH[1]], W=[Hnk])
                    S.op("dve", lambda e, hp=hp, Hn=Hn: e.tensor_tensor(out=Hn[:], in0=Hn[:], in1=u.Ps[:, hp, :], op=ALU.add),
                         R=[Hnk, u.Psk], W=[Hnk])
                    S.op("act", lambda e, Hn=Hn, Hnb=Hnb: e.activation(out=Hnb[:], in_=Hn[:], func=AF.Copy),
                         R=[Hnk], W=[Hnk + "b"])
                    hcur[hp] = 1 - hcur[hp]
                if d == 0:
                    S.op("act", lambda e: e.activation(out=Yacc[:, i, :], in_=pY[0][:, 0:256], func=AF.Copy),
                         R=[pY[1]], W=["Yacc%d" % i])
                elif not (l == DEPTH - 1 and i < 2):
                    epilogue(i, pY)

            def epilogue(i, pY):
                ys, ysk = ysr.next()
                yc, yck = ycr.next()
                sq, sqk = sqr.next()
                sm, smk = s4r.next()
                vr_, vrk = s4r.next()
                bs, bsk = bsr.next()
                btl, btlk = btlr.next()
                ob, obk = obr.next()
                oT, oTk = oTr.next()
                ys2 = ys[:].rearrange("p a c -> p (a c)")
                yc2 = yc[:].rearrange("p a c -> p (a c)")
                S.op("dve", lambda e: e.tensor_tensor(out=ys2, in0=Yacc[:, i, :], in1=pY[0][:, 0:256], op=ALU.add),
                     R=["Yacc%d" % i, pY[1]], W=[ysk])
                S.op("dve", lambda e: e.tensor_reduce(out=sm[:], in_=ys[:], axis=AX.X, op=ALU.add), R=[ysk], W=[smk])
                S.op("dve", lambda e: e.tensor_scalar(out=sm[:], in0=sm[:], scalar1=-1.0 / 64, scalar2=None, op0=ALU.mult),
                     R=[smk], W=[smk])
                S.op("dve", lambda e: e.tensor_tensor(out=yc[:], in0=ys[:], in1=sm[:].unsqueeze(2).to_broadcast([128, 4, 64]),
                                                      op=ALU.add), R=[ysk, smk], W=[yck])
                S.op("pool", lambda e: e.tensor_tensor(out=sq[:], in0=yc[:], in1=yc[:], op=ALU.mult), R=[yck], W=[sqk])
                S.op("dve", lambda e: e.tensor_reduce(out=vr_[:], in_=sq[:], axis=AX.X, op=ALU.add), R=[sqk], W=[vrk])
                S.op("act", lambda e: e.activation(out=vr_[:], in_=vr_[:], func=AF.Sqrt, bias=64e-5, scale=1.0 / 64),
                     R=[vrk], W=[vrk])
                S.op("dve", lambda e: e.reciprocal(out=vr_[:], in_=vr_[:]), R=[vrk], W=[vrk])
                S.op("dve", lambda e: e.tensor_tensor(out=yc[:], in0=yc[:], in1=vr_[:].unsqueeze(2).to_broadcast([128, 4, 64]),
                                                      op=ALU.mult), R=[yck, vrk], W=[yck])
                S.op("pool", lambda e: e.tensor_tensor(out=yc2, in0=yc2, in1=gnB[:, 0, :], op=ALU.mult),
                     R=[yck, "gnB"], W=[yck])
                S.op("pool", lambda e: e.tensor_tensor(out=yc2, in0=yc2, in1=gnB[:, 1, :], op=ALU.add),
                     R=[yck, "gnB"], W=[yck])
                S.dma("sp", btl[:], BT[:, :, i * 128:(i + 1) * 128], W=[btlk])
                pBo = pall.next()
                for hp in range(2):
                    S.op("pe", lambda e, hp=hp: e.matmul(pBo[0][:, hp * 2:hp * 2 + 2], lhsT=btl[:, hp, :], rhs=hsel[:],
                                                         start=True, stop=True), R=[btlk, "hsel"], W=[pBo[1]])
                S.op("dve", lambda e: e.tensor_copy(out=bs[:], in_=pBo[0][:, 0:4]), R=[pBo[1]], W=[bsk])
                S.op("dve", lambda e: e.tensor_tensor(out=sq[:], in0=Vtm[:, i, :].rearrange("p (a c) -> p a c", a=4),
                                                      in1=bs[:].unsqueeze(2).to_broadcast([128, 4, 64]), op=ALU.mult),
                     R=["Vtm", bsk, sqk], W=[sqk])
                S.op("pool", lambda e: e.tensor_tensor(out=yc[:], in0=yc[:], in1=sq[:], op=ALU.add), R=[yck, sqk], W=[yck])
                S.op("dve", lambda e: e.tensor_tensor(out=ob[:], in0=yc2, in1=Gt[:, i, :], op=ALU.mult),
                     R=[yck, "Gt%d" % i], W=[obk])
                pO = pall.next()
                pOb = pO[0][:, :].bitcast(BF16)
                for c in range(2):
                    S.op("pe", lambda e, c=c: e.transpose(out=pOb[:, c * 128:(c + 1) * 128],
                                                          in_=ob[:, c * 128:(c + 1) * 128], identity=k.ident[:]),
                         R=[obk, "ident"], W=[pO[1]])
                S.op("act", lambda e: e.activation(out=oT[:].rearrange("p a c -> p (a c)"), in_=pOb[:, 0:256],
                                                   func=AF.Copy), R=[pO[1]], W=[oTk])
                S.dma("sp", k.OUTS["rwkv"][:, :, i * 128:(i + 1) * 128], oT[:], R=[oTk], W=["OUTS_rwkv"])

            for d in range(2):
                order = list(range(NT)) if d == 0 else [1, 0] + list(range(NT - 1, 1, -1))
                for hp in range(2):
                    Hc = Hb[hp][hcur[hp]]
                    S.op("pool", lambda e, Hc=Hc: e.memset(Hc[:], 0.0), W=["H%d%d" % (hp, hcur[hp])])
                    Hcb0 = Hh[hp][hcur[hp]]
                    S.op("pool", lambda e, Hcb0=Hcb0: e.memset(Hcb0[:], 0.0), W=["H%d%db" % (hp, hcur[hp])])
                for b0 in range(0, NT, NB):
                    units = []
                    for i in order[b0:b0 + NB]:
                        u = U()
                        u.d, u.i = d, i
                        units.append(u)
                    for u in units:
                        pre1(u)
                    if CUT <= 3:
                        continue
                    for _ in range(6):
                        for u in units:
                            merge_level(u)
                    if CUT <= 4:
                        continue
                    for u in units:
                        pre2(u)
                    if CUT <= 5:
                        continue
                    for u in units:
                        pre3(u)
                    if CUT <= 6:
                        continue
                    for u in units:
                        chain(u)
            S.barrier()
            S.flush()


def phase_merge(k, l):
    nc, S, I = k.nc, k.S, k.I
    S.phase = "merge%d" % l
    act = [m for m in MIX if m in k.active]
    src = I["xin"] if l == 0 else k.X1
    with ExitStack() as es:
        wg = _sb(es, nc, "wg", [128, 8, 4096], BF16)
        bw = _sb(es, nc, "bw", [128, 8, D], BF16)
        ow = _sb(es, nc, "ow", [128, 8, D], BF16)
        mb = _sb(es, nc, "mb", [128, 32], F32)
        lnB = _sb(es, nc, "lnB", [128, 2, D], F32)
        load_w(k, bw, "bw", I["branch_w"][l].rearrange("i k n -> (i k) n"), D)
        load_w(k, wg, "wg", I["in_w"][l][:, O_MERGE:O_MERGE + 4096], 4096)
        load_w(k, ow, "ow", I["out_w"][l], D)
        S.dma("act", mb[:], I["merge_bt"][l], W=["mb"])
        S.dma("act", lnB[:, 0, :], I["ln_g"][l:l + 1, :].to_broadcast([128, D]), W=["lnB"])
        S.dma("act", lnB[:, 1, :], I["ln_b"][l:l + 1, :].to_broadcast([128, D]), W=["lnB"])
        S.barrier()
        S.flush()
        xbr = ring(es, nc, "xb", 2, [128, 8, 512], BF16)
        obr = {m: ring(es, nc, "mob" + m, 2, [128, 2, 512], BF16) for m in act}
        yTr = ring(es, nc, "yT", 1, [128, 8, 512], BF16)
        sgr = ring(es, nc, "sg", 2, [128, 512], F32)
        tmr = ring(es, nc, "tm", 2, [128, 512], F32)
        yar = ring(es, nc, "ya", 2, [128, 512], F32)
        xrr = ring(es, nc, "xr", 2, [128, D], F32)
        tAr = ring(es, nc, "tA", 2, [128, D], F32)
        tBr = ring(es, nc, "tB", 2, [128, D], F32)
        str_ = ring(es, nc, "mst", 2, [128, 2, 6], F32)
        mvr = ring(es, nc, "mmv", 2, [128, 2], F32)
        rsr = ring(es, nc, "mrs", 2, [128, 1], F32)
        psG, psZ = psring(k, [0, 1]), psring(k, [2, 3])
        psY = Ring([((k.ps[4], k.ps[5]), ("ps4", "ps5")), ((k.ps[6], k.ps[7]), ("ps6", "ps7"))])
        blocks = [(0, NCTX)] if l == 0 else []
        blocks += [(NCTX + 512 * j, 512) for j in range(8)]
        def block_body(q0, nq):
            nsub = nq // 128
            which = 1 if q0 < NCTX else 0
            xb, xbk = xbr.next()
            for kc in range(8):
                S.dma("sp" if kc % 2 == 0 else "act", xb[:, kc, 0:nq], k.XM[:, kc, q0:q0 + nq], W=[xbk + "_%d" % kc])
            obs = {}
            for m in act:
                ob, obk = obr[m].next()
                S.dma("sp", ob[:, :, 0:nq], k.OUTS[m][:, :, q0:q0 + nq], W=[obk])
                obs[m] = (ob, obk)
            yT, yTk = yTr.next()
            for fc in range(8):
                ya, yak = yar.next()
                first = True
                for m in act:
                    i = MIX.index(m)
                    pg, pgk = psG.next()
                    pz, pzk = psZ.next()
                    col = i * 1024 + fc * 128
                    for kc in range(8):
                        S.op("pe", lambda e, pg=pg, kc=kc, col=col, xb=xb: e.matmul(
                            pg[:, 0:nq], lhsT=wg[:, kc, col:col + 128], rhs=xb[:, kc, 0:nq],
                            start=(kc == 0), stop=(kc == 7)), R=[xbk + "_%d" % kc] + wkeys("wg", col, 128), W=[pgk])
                    ob, obk = obs[m]
                    for c in range(2):
                        S.op("pe", lambda e, pz=pz, c=c, i=i, fc=fc, ob=ob: e.matmul(
                            pz[:, 0:nq], lhsT=bw[:, i * 2 + c, fc * 128:(fc + 1) * 128], rhs=ob[:, c, 0:nq],
                            start=(c == 0), stop=(c == 1)), R=[obk] + wkeys("bw", fc * 128, 128), W=[pzk])
                    sg, sgk = sgr.next()
                    S.op("act", lambda e, sg=sg, pg=pg, i=i, fc=fc: e.activation(
                        out=sg[:, 0:nq], in_=pg[:, 0:nq], func=AF.Sigmoid, bias=mb[:, i * 8 + fc:i * 8 + fc + 1],
                        scale=1.0), R=[pgk, "mb"], W=[sgk])
                    last = (m == act[-1])
                    dst, dstk = ((yT[:, fc, 0:nq], yTk) if (last and first) else (ya[:, 0:nq], yak))
                    if first:
                        S.op("dve", lambda e, dst=dst, sg=sg, pz=pz: e.tensor_tensor(out=dst, in0=sg[:, 0:nq],
                                                                                    in1=pz[:, 0:nq], op=ALU.mult),
                             R=[sgk, pzk], W=[dstk])
                    else:
                        tm, tmk = tmr.next()
                        S.op("dve", lambda e, tm=tm, sg=sg, pz=pz: e.tensor_tensor(out=tm[:, 0:nq], in0=sg[:, 0:nq],
                                                                                  in1=pz[:, 0:nq], op=ALU.mult),
                             R=[sgk, pzk], W=[tmk])
                        if last:
                            S.op("pool", lambda e, tm=tm, ya=ya, yT=yT, fc=fc: e.tensor_tensor(
                                out=yT[:, fc, 0:nq], in0=ya[:, 0:nq], in1=tm[:, 0:nq], op=ALU.add),
                                R=[yak, tmk], W=[yTk])
                        else:
                            S.op("pool", lambda e, tm=tm, ya=ya: e.tensor_tensor(out=ya[:, 0:nq], in0=ya[:, 0:nq],
                                                                                in1=tm[:, 0:nq], op=ALU.add),
                                 R=[yak, tmk], W=[yak])
                    first = False
            for j in range(nsub):
                t0 = q0 + j * 128
                (py0, py1), (pyk0, pyk1) = psY.next()
                for hb, (py, pyk) in enumerate(((py0, pyk0), (py1, pyk1))):
                    for fc in range(8):
                        S.op("pe", lambda e, py=py, fc=fc, hb=hb, j=j, yT=yT: e.matmul(
                            py[:, :], lhsT=yT[:, fc, j * 128:(j + 1) * 128], rhs=ow[:, fc, hb * 512:(hb + 1) * 512],
                            start=(fc == 0), stop=(fc == 7)), R=[yTk] + wkeys("ow", hb * 512, 512), W=[pyk])
                xr, xrk = xrr.next()
                tA, tAk = tAr.next()
                tB, tBk = tBr.next()
                S.dma("sp", xr[:], src[t0:t0 + 128, :], W=[xrk])
                for hb, (py, pyk) in enumerate(((py0, pyk0), (py1, pyk1))):
                    S.op("dve", lambda e, tA=tA, py=py, hb=hb, which=which: e.tensor_tensor(
                        out=tA[:, hb * 512:(hb + 1) * 512], in0=py[:, :], in1=k.modB[:, which, 2, hb * 512:(hb + 1) * 512],
                        op=ALU.mult), R=[pyk, "modB"], W=[tAk])
                S.op("dve", lambda e, tA=tA, xr=xr, tB=tB: e.scalar_tensor_tensor(
                    out=tB[:], in0=xr[:], scalar=float(ALPHA), in1=tA[:], op0=ALU.mult, op1=ALU.add),
                    R=[xrk, tAk], W=[tBk])
                stt, mvt, rst = str_.next(), mvr.next(), rsr.next()
                ln_rows(k, stt, mvt, rst, tB, tBk, 1e-5)
                mv, mvk = mvt
                rs, rsk = rst
                S.op("dve", lambda e, tA=tA, tB=tB, mv=mv, rs=rs: e.tensor_scalar(
                    out=tA[:], in0=tB[:], scalar1=mv[:, 0:1], scalar2=rs[:, 0:1], op0=ALU.subtract, op1=ALU.mult),
                    R=[tBk, mvk, rsk], W=[tAk])
                S.op("pool", lambda e, tA=tA: e.tensor_tensor(out=tA[:], in0=tA[:], in1=lnB[:, 0, :], op=ALU.mult),
                     R=[tAk, "lnB"], W=[tAk])
                S.op("pool", lambda e, tA=tA, tB=tB: e.tensor_tensor(out=tB[:], in0=tA[:], in1=lnB[:, 1, :], op=ALU.add),
                     R=[tAk, "lnB"], W=[tBk])
                if l == DEPTH - 1:
                    S.dma("sp", k.out[t0 - NCTX:t0 - NCTX + 128, :], tB[:], R=[tBk], W=["out"])
                else:
                    S.dma("sp", k.X1[t0:t0 + 128, :], tB[:], R=[tBk], W=["X1_%d" % (t0 // 128)])

        for (q0, nq) in blocks:
            block_body(q0, nq)
        S.barrier()
        S.flush()


def rope_table(rot):
    L = SEQ
    rows = L // 64
    row = np.repeat(np.arange(rows), 64).astype(np.float32)
    col = np.tile(np.arange(64), rows).astype(np.float32)
    q = rot // 4
    inv = (np.float32(10000.0) ** (-np.arange(q, dtype=np.float32) / np.float32(q))).astype(np.float32)
    ang = np.concatenate([row[:, None] * inv, col[:, None] * inv], axis=-1).astype(np.float32)
    cos = np.concatenate([np.ones((NCTX, rot // 2), np.float32), np.cos(ang)], 0)
    sin = np.concatenate([np.zeros((NCTX, rot // 2), np.float32), np.sin(ang)], 0)
    return np.ascontiguousarray(np.concatenate([cos, sin], axis=1).astype(np.float32))


def input_specs():
    return [
        ("xin", [T, D], F32), ("cvec", [128, 8, 2], F32),
        ("ada_w", [DEPTH, D, 3 * D], F32), ("ada_b", [DEPTH, 3 * D], F32), ("in_w", [DEPTH, D, IN_W], F32),
        ("ident", [128, 128], BF16), ("sel2", [2, 2, 128], F32),
        ("rope64", [T, 64], F32), ("rope32", [T, 32], F32),
        ("gqa_q_norm", [DEPTH, 64], F32), ("gqa_k_norm", [DEPTH, 64], F32),
        ("mla_q_norm", [DEPTH, 256], F32), ("mla_w_uq", [DEPTH, 256, 384], F32),
        ("mla_kv_norm", [DEPTH, 128], F32), ("mla_w_ukv", [DEPTH, 128, 512], F32),
        ("diff_lambda", [DEPTH, 4, 32], F32), ("subcol", [DEPTH, 128, 1], F32),
        ("rw_cols", [DEPTH, 128, 2, 7], F32), ("rw_mu", [DEPTH, 128, 2, 8], F32),
        ("rwkv_w_up", [DEPTH, 2, 64, 256], F32), ("rwkv_a_up", [DEPTH, 2, 64, 256], F32),
        ("rwkv_gn_w", [DEPTH, 256], F32), ("rwkv_gn_b", [DEPTH, 256], F32),
        ("bones", [128, 128], F32), ("identf", [128, 128], F32), ("hsel", [128, 2], BF16), ("rmask", [2, 128, 640], F32),
        ("lvmask", [128, 7, 4, 128], U16), ("qmask", [128, 6], F32),
        ("merge_bt", [DEPTH, 128, 32], F32), ("branch_w", [DEPTH, 4, 256, D], F32), ("out_w", [DEPTH, D, D], F32),
        ("ln_g", [DEPTH, D], F32), ("ln_b", [DEPTH, D], F32),
    ]


def host_consts():
    c = {}
    c["ident"] = np.eye(128, dtype=np.float32).astype(ml_dtypes.bfloat16)
    sel = np.zeros((2, 2, 128), np.float32)
    sel[0, 0, :] = 1.0
    sel[1, 1, :] = 1.0
    c["sel2"] = sel
    bo = np.zeros((128, 128), np.float32)
    bo[:64, :64] = 1.0
    bo[64:, 64:] = 1.0
    c["bones"] = bo
    c["identf"] = np.eye(128, dtype=np.float32)
    hs = np.zeros((128, 2), np.float32)
    hs[:64, 0] = 1.0
    hs[64:, 1] = 1.0
    c["hsel"] = hs.astype(ml_dtypes.bfloat16)
    r_, c_ = np.meshgrid(np.arange(128), np.arange(128), indexing="ij")
    U_ = (r_ < c_).astype(np.float32)
    Ue = (r_ <= c_).astype(np.float32)
    L_ = (r_ > c_).astype(np.float32)
    Le = (r_ >= c_).astype(np.float32)
    rm = np.zeros((2, 128, 640), np.float32)
    rm[0] = np.concatenate([U_, Ue, -U_, -Ue, -L_], axis=1)
    rm[1] = np.concatenate([L_, Le, -L_, -Le, -U_], axis=1)
    c["rmask"] = rm
    qm = np.zeros((128, 6), np.float32)
    for j in range(2):
        qm[64 * j:64 * j + 64, j] = 1.0
    for j in range(4):
        qm[32 * j:32 * j + 32, 2 + j] = 1.0
    c["qmask"] = qm
    lv = np.zeros((128, 7, 4, 128), np.uint16)
    for j in range(7):
        bsz = 1 << j
        lv[:, j, :, :] = ((r_ // (2 * bsz) == c_ // (2 * bsz)) & (r_ // bsz != c_ // bsz)).astype(np.uint16)[:, None, :]
    c["lvmask"] = lv
    c["rope64"] = rope_table(64)
    c["rope32"] = rope_table(32)
    return c


def make_in_maps(inputs):
    consts = host_consts()
    shared = dict(consts)
    for nm in ["ada_w", "ada_b", "in_w", "gqa_q_norm", "gqa_k_norm", "branch_w", "out_w", "ln_g", "ln_b",
               "mla_q_norm", "mla_w_uq", "mla_kv_norm", "mla_w_ukv", "diff_lambda",
               "rwkv_w_up", "rwkv_a_up", "rwkv_gn_w", "rwkv_gn_b"]:
        shared[nm] = np.ascontiguousarray(inputs[nm], dtype=np.float32)
    def colsplit(a):
        return a.reshape(DEPTH, 2, 128).transpose(0, 2, 1)
    rc = np.stack([colsplit(inputs["rwkv_k_k"]), colsplit(inputs["rwkv_k_a"]), colsplit(inputs["rwkv_r_k"]),
                   colsplit(inputs["rwkv_w0"][:, 0]), colsplit(inputs["rwkv_w0"][:, 1]),
                   colsplit(inputs["rwkv_a0"][:, 0]), colsplit(inputs["rwkv_a0"][:, 1])], axis=-1)
    shared["rw_cols"] = np.ascontiguousarray(rc.astype(np.float32))
    shared["subcol"] = np.ascontiguousarray(np.tile(inputs["diff_subln"], (1, 2)).reshape(DEPTH, 128, 1).astype(np.float32))
    shared["rw_mu"] = np.ascontiguousarray(inputs["rwkv_mu"].reshape(DEPTH, 2, 8, 128).transpose(0, 3, 1, 2))
    shared["merge_bt"] = np.ascontiguousarray(inputs["merge_b"].reshape(DEPTH, 32, 128).transpose(0, 2, 1))
    maps = []
    for b in range(8):
        m = dict(shared)
        m["xin"] = np.ascontiguousarray(np.concatenate([inputs["ctx"][b], inputs["x"][b]], axis=0))
        cv = np.stack([inputs["c"][b].reshape(8, 128).T, inputs["c_ctx"].reshape(8, 128).T], axis=-1)
        m["cvec"] = np.ascontiguousarray(cv.astype(np.float32))
        maps.append(m)
    return maps


def kernel(**inputs):
    inputs = {kk: np.asarray(v) for kk, v in inputs.items()}
    nc = build(active=tuple(MIX))
    maps = make_in_maps(inputs)
    res = run_bass_kernel_spmd(nc, maps, core_ids=list(range(8)))
    return np.stack([np.asarray(r["out"]) for r in res.results], axis=0).astype(np.float32)
```

```python
import math
import os
from contextlib import ExitStack

import numpy as np
import ml_dtypes

import concourse.bass as bass
import concourse.mybir as mybir
from concourse.bass_utils import run_bass_kernel_spmd

F32 = mybir.dt.float32
BF16 = mybir.dt.bfloat16
U16 = mybir.dt.uint16
AF = mybir.ActivationFunctionType
ALU = mybir.AluOpType
AX = mybir.AxisListType

D = 1024
NCTX = 256
SEQ = 4096
T = NCTX + SEQ
NT = T // 128
DEPTH = 2
IN_W = 7840
ALPHA = (2 * DEPTH) ** 0.25
MLA_SCALE = 96 ** -0.5
GQA_SCALE = 64 ** -0.5
DIFF_SCALE = 32 ** -0.5
DECAY_SCALE = math.exp(-0.5)

O_MQ, O_MKV, O_MKR, O_MG = 0, 256, 384, 416
O_RW, O_RG = 672, 1696
O_GQ, O_GK, O_GV, O_GG = 1952, 2208, 2336, 2464
O_DQ, O_DK, O_DV, O_DG = 2720, 2976, 3232, 3488
O_MERGE = 3744

SAME_ENGINE_SYNC = True
NDS = 24


REORDER = int(os.environ.get("REORDER", "1"))
DUR = {"pe": 0.1, "act": 1.3, "dve": 0.5, "pool": 2.0}
if os.environ.get("DUR"):
    DUR = dict(zip(["pe", "act", "dve", "pool"], [float(x) for x in os.environ["DUR"].split(",")]))
DMA_DUR = float(os.environ.get("DMA_DUR", "2.0"))
SW_INFLIGHT = int(os.environ.get("SW_INFLIGHT", "4"))
HINTS = int(os.environ.get("HINTS", "1"))
CP_ALPHA = float(os.environ.get("CP_ALPHA", "0.15"))


class Sched:
    def __init__(self, nc):
        self.nc = nc
        self.names = ["pe", "act", "dve", "pool", "sp"]
        self.ops = []
        self.cnt = {e: 0 for e in self.names}
        self.seen = {e: {} for e in self.names}
        self.csem = {e: nc.alloc_semaphore("c_" + e) for e in ["pe", "act", "dve", "pool"]}
        self.dsem = [nc.alloc_semaphore("d%d" % i) for i in range(NDS)]
        self.dval = [0] * NDS
        self.dnext = 0
        self.nins = 0
        self.phase = "x"
        self.nflush = 0
        self.swq = []

    def op(self, e, fn, R=(), W=(), c=None):
        self.ops.append((e, fn, tuple(R), tuple(W), False, c if HINTS else None))
        self.nins += 1

    def dma(self, q, out, in_, R=(), W=()):
        self.ops.append((q, (lambda eng, out=out, in_=in_: eng.dma_start(out=out, in_=in_)), tuple(R), tuple(W), True, None))
        self.nins += 1

    def barrier(self):
        self.ops.append(None)

    def flush(self, name=None):
        ops = self.ops
        self.ops = []
        seg = []
        for o in ops:
            if o is None:
                self._emit_segment(seg, True, name)
                seg = []
            else:
                seg.append(o)
        if seg:
            self._emit_segment(seg, False, name)

    def _emit_segment(self, ops, barrier, name):
        n = len(ops)
        preds = [[] for _ in range(n)]
        lastw = {}
        readers = {}
        for i, (e, fn, R, W, isd, cst) in enumerate(ops):
            p = preds[i]
            for k in R:
                t = lastw.get(k)
                if t is not None:
                    p.append(t)
                if k.startswith("ps"):
                    for r in readers.get(k, ()):
                        if ops[r][0] != e:
                            p.append(r)
            for k in W:
                t = lastw.get(k)
                if t is not None:
                    p.append(t)
                p.extend(readers.get(k, ()))
            for k in R:
                readers.setdefault(k, []).append(i)
            for k in W:
                lastw[k] = i
                readers[k] = []
        est = [0.0] * n
        fin = [0.0] * n
        for i in range(n):
            e, _, _, _, isd, cst = ops[i]
            t = 0.0
            for p in preds[i]:
                if fin[p] > t:
                    t = fin[p]
            est[i] = t
            fin[i] = t + (cst if cst is not None else (DMA_DUR if isd else DUR[e]))
        if CP_ALPHA > 0.0:
            tail = [0.0] * n
            for i in range(n - 1, -1, -1):
                d_i = fin[i] - est[i]
                if tail[i] < d_i:
                    tail[i] = d_i
                ti = tail[i]
                for p in preds[i]:
                    cand = ti + (fin[p] - est[p])
                    if tail[p] < cand:
                        tail[p] = cand
            keyv = [est[i] - CP_ALPHA * tail[i] for i in range(n)]
        else:
            keyv = est
        order = sorted(range(n), key=lambda i: (keyv[i], i)) if REORDER else list(range(n))
        tok = [None] * n
        prog = {e: [] for e in self.names}
        for i in order:
            e, fn, R, W, isd, cst = ops[i]
            deps = []
            for p in preds[i]:
                tp = tok[p]
                assert tp is not None, "scheduler order violates a dependency"
                if tp[0] == "c" and tp[1] == e and (e == "pe" or not SAME_ENGINE_SYNC):
                    continue
                deps.append(tp)
            if isd:
                j = self.dnext
                self.dnext = (j + 1) % NDS
                if self.dval[j] > 0:
                    deps.append(("d", j, self.dval[j]))
                if e == "pool":
                    self.swq.append(None)
                    if len(self.swq) > SW_INFLIGHT:
                        deps.append(self.swq[-1 - SW_INFLIGHT])
                self.dval[j] += 16
                tok[i] = ("d", j, self.dval[j])
                if e == "pool":
                    self.swq[-1] = tok[i]
            else:
                self.cnt[e] += 1
                tok[i] = ("c", e, self.cnt[e])
            seen = self.seen[e]
            need = {}
            for (kind, s_, v) in deps:
                key = (kind, s_)
                if seen.get(key, 0) >= v:
                    continue
                if need.get(key, 0) < v:
                    need[key] = v
            for key, v in need.items():
                seen[key] = v
            prog[e].append((list(need.items()), fn, (tok[i][0], tok[i][1])))
        if barrier:
            toks = [("c", o, self.cnt[o]) for o in ["pe", "act", "dve", "pool"] if self.cnt[o] > 0]
            toks += [("d", j, self.dval[j]) for j in range(NDS) if self.dval[j] > 0]
            for e in self.names:
                seen = self.seen[e]
                need = {}
                for (kind, s_, v) in toks:
                    if seen.get((kind, s_), 0) < v:
                        need[(kind, s_)] = v
                        seen[(kind, s_)] = v
                if need:
                    prog[e].append((list(need.items()), None, None))
        self._emit(prog, name)

    def _emit(self, progs, name):
        nc = self.nc
        self.nflush += 1
        name = "%02d_%s" % (self.nflush, name or self.phase)
        csem, dsem = self.csem, self.dsem

        def mk(e):
            def body(eng):
                for waits, fn, inc in progs[e]:
                    for (kind, s), v in waits:
                        eng.wait_ge(csem[s] if kind == "c" else dsem[s], v)
                    if fn is None:
                        continue
                    ins = fn(eng)
                    if inc[0] == "c":
                        ins.then_inc(csem[inc[1]], 1)
                    else:
                        ins.then_inc(dsem[inc[1]], 16)
            return body

        with nc.named_scope(name), nc.Block() as blk:
            blk.tensor(mk("pe"))
            blk.scalar(mk("act"))
            blk.vector(mk("dve"))
            blk.gpsimd(mk("pool"))
            blk.sync(mk("sp"))


class Ring:
    def __init__(self, items):
        self.items = items
        self.i = 0

    def next(self):
        it = self.items[self.i]
        self.i = (self.i + 1) % len(self.items)
        return it


class K:
    pass


_uid = [0]


def _sb(es, nc, name, shape, dt):
    _uid[0] += 1
    return es.enter_context(nc.sbuf_tensor("sb_%s_%d" % (name, _uid[0]), list(shape), dt))


MIX = ["mla", "rwkv", "gqa", "diff"]
CUT = int(os.environ.get("CUT", "99"))
SKIP_ATT = int(os.environ.get("SKIP_ATT", "0"))
SUB = int(os.environ.get("SUB", "99"))


def ring(es, nc, name, n, shape, dt):
    return Ring([(_sb(es, nc, "%s%d" % (name, i), shape, dt), "%s%d" % (name, i)) for i in range(n)])


def psring(k, idxs):
    return Ring([(k.ps[i], "ps%d" % i) for i in idxs])


def build(dbg=None, active=("gqa",)):
    nc = bass.Bass("TRN2", target_bir_lowering=False)
    S = Sched(nc)
    k = K()
    k.nc, k.S, k.dbg, k.active = nc, S, dbg, active

    def din(name, shape, dt=F32):
        return nc.dram_tensor(name, list(shape), dt, kind="ExternalInput").ap()

    def dscr(name, shape, dt):
        return nc.dram_tensor(name, list(shape), dt, kind="Internal").ap()

    I = {}
    for name, shape, dt in input_specs():
        I[name] = din(name, shape, dt)
    k.I = I
    k.out = nc.dram_tensor("out", [SEQ, D], F32, kind="ExternalOutput").ap()
    if dbg is not None:
        k.dbg_out = nc.dram_tensor("dbg", list(dbg[1]), dbg[2], kind="ExternalOutput").ap()
    k.XM = dscr("XM", [128, 8, T], BF16)
    k.X1 = dscr("X1", [T, D], F32)
    k.OUTS = {m: dscr("OUTS_" + m, [128, 2, T], BF16) for m in MIX}
    k.SH = [dscr("SH%d" % f, [128, T], F32) for f in range(6)]
    k.SC = [dscr("SC%d" % d, [128, 2, 6, T], BF16) for d in range(2)]
    k.BT = dscr("BT", [128, 2, T], BF16)

    with ExitStack() as top:
        k.psw = [top.enter_context(nc.psum_tensor("psw%d" % i, [128, 1024], F32)) for i in range(4)]
        k.ps = [k.psw[i // 2][:, (i % 2) * 512:(i % 2 + 1) * 512] for i in range(8)]
        k.ident = _sb(top, nc, "ident", [128, 128], BF16)
        k.modB = _sb(top, nc, "modB", [128, 2, 3, D], F32)
        S.dma("sp", k.ident[:], I["ident"], W=["ident"])
        stop = False
        for l in range(DEPTH):
            phase_adaln(k, l)
            if dbg is not None and dbg[0] == "ln%d" % l:
                break
            for m in MIX:
                if m in active:
                    {"gqa": mixer_gqa, "mla": mixer_mla, "diff": mixer_diff, "rwkv": mixer_rwkv}[m](k, l)
                if dbg is not None and dbg[0] == "%s%d" % (m, l):
                    stop = True
                    break
            if stop:
                break
            phase_merge(k, l)
            if dbg is not None and dbg[0] == "x%d" % l:
                break
        if dbg is not None:
            emit_dbg(k)
        S.barrier()
        S.flush()
    return nc


def emit_dbg(k):
    nc, S = k.nc, k.S
    name = k.dbg[0]
    if name.startswith("ln"):
        S.dma("sp", k.dbg_out, k.XM, R=["XM"], W=["dbg"])
    elif name[:-1] in MIX:
        S.dma("sp", k.dbg_out, k.OUTS[name[:-1]], W=["dbg"])
    elif name == "x0":
        S.dma("sp", k.dbg_out, k.X1, W=["dbg"])
    S.barrier()
    S.flush()


def phase_adaln(k, l):
    nc, S, I = k.nc, k.S, k.I
    S.phase = "adaln%d" % l
    with ExitStack() as es:
        cv = _sb(es, nc, "cv", [128, 8, 2], F32)
        sc = _sb(es, nc, "sc", [128, 8, 2], F32)
        sel = _sb(es, nc, "sel", [2, 2, 128], F32)
        ab = _sb(es, nc, "ab", [2, 3 * D], F32)
        mod = _sb(es, nc, "mod", [2, 3 * D], F32)
        wst = [_sb(es, nc, "adaw%d" % i, [128, 8, 512], F32) for i in range(2)]
        S.dma("sp", cv[:], I["cvec"], W=["cv"])
        S.dma("sp", sel[:], I["sel2"], W=["sel"])
        S.dma("sp", ab[:], I["ada_b"][l:l + 1, :].to_broadcast([2, 3 * D]), W=["ab"])
        S.op("act", lambda e: e.activation(out=sc[:], in_=cv[:], func=AF.Silu), R=["cv"], W=["sc"])
        awv = I["ada_w"][l].rearrange("(kc p) n -> p kc n", p=128)
        for j in range(6):
            w = wst[j % 2]
            wk = "adaw%d" % (j % 2)
            S.dma("sp" if j % 2 == 0 else "act", w[:], awv[:, :, j * 512:(j + 1) * 512], W=[wk])
            ps = k.ps[j % 2]
            pk = "ps%d" % (j % 2)
            for kc in range(8):
                S.op("pe", lambda e, ps=ps, w=w, kc=kc: e.matmul(ps[0:2, :], lhsT=sc[:, kc, :], rhs=w[:, kc, :],
                                                                 start=(kc == 0), stop=(kc == 7)),
                     R=["sc", wk], W=[pk])
            S.op("dve", lambda e, ps=ps, j=j: e.tensor_tensor(out=mod[:, j * 512:(j + 1) * 512], in0=ps[0:2, :],
                                                              in1=ab[:, j * 512:(j + 1) * 512], op=ALU.add),
                 R=[pk, "ab"], W=["mod"])
        S.op("dve", lambda e: e.tensor_scalar(out=mod[:, D:2 * D], in0=mod[:, D:2 * D], scalar1=1.0, scalar2=None,
                                              op0=ALU.add), R=["mod"], W=["mod"])
        n = 0
        for which in range(2):
            for part in range(3):
                for hb in range(2):
                    ps = k.ps[2 + n % 2]
                    pk = "ps%d" % (2 + n % 2)
                    n += 1
                    c0 = part * D + hb * 512
                    S.op("pe", lambda e, ps=ps, which=which, c0=c0: e.matmul(ps[:, :], lhsT=sel[:, which, :],
                                                                               rhs=mod[:, c0:c0 + 512],
                                                                               start=True, stop=True),
                         R=["sel", "mod"], W=[pk])
                    S.op("act", lambda e, ps=ps, which=which, part=part, hb=hb: e.activation(
                        out=k.modB[:, which, part, hb * 512:(hb + 1) * 512], in_=ps[:, :], func=AF.Copy),
                        R=[pk], W=["modB"])
        phase_ln(k, l)


def ln_rows(k, stt, mvt, rst, src, sk, eps):
    S = k.S
    st, stk = stt
    mv, mvk = mvt
    rs, rsk = rst
    for h in range(2):
        S.op("dve", lambda e, h=h: e.bn_stats(out=st[:, h, :], in_=src[:, h * 512:(h + 1) * 512]), R=[sk], W=[stk])
    S.op("dve", lambda e: e.bn_aggr(out=mv[:], in_=st[:].rearrange("p a s -> p (a s)")), R=[stk], W=[mvk])
    S.op("act", lambda e: e.activation(out=rs[:], in_=mv[:, 1:2], func=AF.Sqrt, bias=float(eps), scale=1.0),
         R=[mvk], W=[rsk])
    S.op("dve", lambda e: e.reciprocal(out=rs[:], in_=rs[:]), R=[rsk], W=[rsk])


def phase_ln(k, l):
    nc, S, I = k.nc, k.S, k.I
    S.phase = "ln%d" % l
    src = I["xin"] if l == 0 else k.X1
    with ExitStack() as es:
        xtr = ring(es, nc, "xt", 3, [128, D], F32)
        xnr = ring(es, nc, "xn", 2, [128, D], F32)
        xmr = ring(es, nc, "xm", 2, [128, D], BF16)
        xTr = ring(es, nc, "xT", 2, [128, 8, 128], BF16)
        str_ = ring(es, nc, "st", 2, [128, 2, 6], F32)
        mvr = ring(es, nc, "mv", 2, [128, 2], F32)
        rsr = ring(es, nc, "rs", 2, [128, 1], F32)
        psr = psring(k, [4, 5])
        for i in range(NT):
            which = 1 if i < 2 else 0
            xt, xtk = xtr.next()
            xn, xnk = xnr.next()
            xm, xmk = xmr.next()
            xT, xTk = xTr.next()
            stt, mvt, rst = str_.next(), mvr.next(), rsr.next()
            mv, mvk = mvt
            rs, rsk = rst
            S.dma("sp", xt[:], src[i * 128:(i + 1) * 128, :], W=[xtk])
            ln_rows(k, stt, mvt, rst, xt, xtk, 1e-6)
            S.op("dve", lambda e, xn=xn, xt=xt, mv=mv, rs=rs: e.tensor_scalar(
                out=xn[:], in0=xt[:], scalar1=mv[:, 0:1], scalar2=rs[:, 0:1], op0=ALU.subtract, op1=ALU.mult),
                R=[xtk, mvk, rsk], W=[xnk])
            S.op("pool", lambda e, xn=xn, which=which: e.tensor_tensor(out=xn[:], in0=xn[:], in1=k.modB[:, which, 1, :],
                                                                        op=ALU.mult), R=[xnk, "modB"], W=[xnk])
            S.op("dve", lambda e, xn=xn, xm=xm, which=which: e.tensor_tensor(out=xm[:], in0=xn[:],
                                                                              in1=k.modB[:, which, 0, :], op=ALU.add),
                 R=[xnk, "modB"], W=[xmk])
            pT, pk = psr.next()
            pTb = pT[:, :].bitcast(BF16)
            for kc in range(8):
                S.op("pe", lambda e, xm=xm, kc=kc, pTb=pTb: e.transpose(out=pTb[:, kc * 128:(kc + 1) * 128],
                                                                         in_=xm[:, kc * 128:(kc + 1) * 128],
                                                                         identity=k.ident[:]),
                     R=[xmk, "ident"], W=[pk])
            S.op("act", lambda e, xT=xT, pTb=pTb: e.activation(out=xT[:].rearrange("p a t -> p (a t)"), in_=pTb,
                                                                func=AF.Copy), R=[pk], W=[xTk])
            S.dma("act", k.XM[:, :, i * 128:(i + 1) * 128], xT[:], R=[xTk], W=["XM"])
        S.barrier()
        S.flush()


class XmStream:
    def __init__(self, k, es):
        self.k = k
        self.ring = ring(es, k.nc, "xmc", 2, [128, 8, 512], BF16)
        self.cur = None

    def chunk(self, ci):
        self.tile(ci * 4)
        return self.cur[1], self.cur[2]

    def tile(self, i):
        ci = i // 4
        if self.cur is None or self.cur[0] != ci:
            xb, xbk = self.ring.next()
            q0 = ci * 512
            nq = min(512, T - q0)
            for kc in range(8):
                self.k.S.dma("sp" if kc % 2 == 0 else "act", xb[:, kc, 0:nq], self.k.XM[:, kc, q0:q0 + nq],
                             W=["%s_%d" % (xbk, kc)])
            self.cur = (ci, xb, xbk)
        _, xb, xbk = self.cur
        o = (i % 4) * 128
        return (lambda kc: xb[:, kc, o:o + 128]), (lambda kc: "%s_%d" % (xbk, kc))


def load_w(k, w, name, src, n, stg_ring=None, rows=8, group=256, order=None):
    S = k.S
    view = src.rearrange("(kc p) n -> p kc n", p=128)
    starts = list(range(0, n, group)) if order is None else [g * group for g in order]
    for c0 in starts:
        wd = min(group, n - c0)
        S.dma("pool", w[:, :, c0:c0 + wd], view[:, :, c0:c0 + wd], W=["%s_%d" % (name, c0 // group)])


def wkeys(name, c0, n, group=256):
    return ["%s_%d" % (name, g) for g in range(c0 // group, (c0 + n - 1) // group + 1)]


def inproj_tm(k, ps, pk, xs, i, w, wname, c0, n):
    apf, keyf = xs.tile(i)
    for kc in range(8):
        k.S.op("pe", lambda e, kc=kc: e.matmul(ps[:, 0:n], lhsT=apf(kc), rhs=w[:, kc, c0:c0 + n],
                                               start=(kc == 0), stop=(kc == 7)),
               R=[keyf(kc)] + wkeys(wname, c0, n), W=[pk])


class Item:
    pass


def gate_fm_chunk(k, xs, ci, w, wname, gcol0, GtT, psg):
    S = k.S
    q0 = ci * 512
    nq = min(512, T - q0)
    xb, xbk = xs.chunk(ci)
    for c in range(2):
        ps, pk = psg.next()
        for kc in range(8):
            S.op("pe", lambda e, kc=kc, c=c, ps=ps: e.matmul(
                ps[:, 0:nq], lhsT=w[:, kc, gcol0 + c * 128:gcol0 + (c + 1) * 128], rhs=xb[:, kc, 0:nq],
                start=(kc == 0), stop=(kc == 7)),
                R=["%s_%d" % (xbk, kc)] + wkeys(wname, gcol0 + c * 128, 128), W=[pk])
        S.op("act", lambda e, c=c, ps=ps: e.activation(out=GtT[:, c, q0:q0 + nq], in_=ps[:, 0:nq], func=AF.Silu),
             R=[pk], W=["GtT"])


def attention_phase(k, l, mname, chunks, scale, GtT, kind="soft", extra=None):
    nc, S = k.nc, k.S
    with ExitStack() as es:
        ptr = ring(es, nc, "Pt2", 3, [128, 1024], BF16)
        recr = ring(es, nc, "arec", 2, [128, 512], F32)
        tmpr = ring(es, nc, "atmp", 2, [128, 512], F32)
        obr = ring(es, nc, "aob", 2, [128, 2, 512], BF16)
        if kind == "diff":
            ofmr = ring(es, nc, "aofm", 2, [128, 2, 512], F32)
            sqr = ring(es, nc, "asq", 2, [128, 512], F32)
        spair = Ring([(k.psw[i], ("ps%d" % (2 * i), "ps%d" % (2 * i + 1))) for i in range(3)])
        oring = psring(k, [6, 7])
        allitems = chunks[0] + chunks[1]
        masked = [it for it in allitems if getattr(it, "qmask", None) is not None]
        if masked:
            qzr = ring(es, nc, "aqz", 2, [128, len(masked), 512], BF16)

        def one_item(it, q0, nq, kts, ob, obk, ofm, ofmk):
            psO, pOk = oring.next()
            npair = len(kts) // 2
            pend = []

            def do_pv(n, kt0, kt1, Pt, ptk):
                S.op("pe", lambda e: e.matmul(psO[:, 0:nq], lhsT=it.VA(kt0), rhs=Pt[:, 0:nq], start=(n == 0), stop=False),
                     R=[ptk, it.vkey], W=[pOk], c=0.23)
                S.op("pe", lambda e: e.matmul(psO[:, 0:nq], lhsT=it.VA(kt1), rhs=Pt[:, 512:512 + nq], start=False,
                                              stop=(n == npair - 1)), R=[ptk, it.vkey], W=[pOk], c=0.23)

            for n in range(npair):
                kt0, kt1 = kts[2 * n], kts[2 * n + 1]
                pw, (ka, kb) = spair.next()
                qap = it.QT(q0, nq)
                S.op("pe", lambda e, pw=pw, kt0=kt0, qap=qap: e.matmul(pw[:, 0:nq], lhsT=it.KT(kt0), rhs=qap,
                                                              start=True, stop=True), R=[it.kkey, it.qkey_blk], W=[ka], c=0.23)
                S.op("pe", lambda e, pw=pw, kt1=kt1, qap=qap: e.matmul(pw[:, 512:512 + nq], lhsT=it.KT(kt1), rhs=qap,
                                                              start=True, stop=True), R=[it.kkey, it.qkey_blk], W=[kb], c=0.23)
                Pt, ptk = ptr.next()
                if nq == 512:
                    S.op("act", lambda e, pw=pw, Pt=Pt: e.activation(out=Pt[:, :], in_=pw[:, :], func=AF.Exp,
                                                                     scale=float(scale)), R=[ka, kb], W=[ptk], c=0.95)
                else:
                    S.op("act", lambda e, pw=pw, Pt=Pt: e.activation(
                        out=Pt[:, :].rearrange("p (a c) -> p a c", a=2)[:, :, 0:nq],
                        in_=pw[:, :].rearrange("p (a c) -> p a c", a=2)[:, :, 0:nq], func=AF.Exp, scale=float(scale)),
                        R=[ka, kb], W=[ptk])
                pend.append((n, kt0, kt1, Pt, ptk))
                if len(pend) > 1:
                    do_pv(*pend.pop(0))
            while pend:
                do_pv(*pend.pop(0))
            re_ = slice(64 * it.e, 64 * it.e + 64)
            ro_ = slice(64 * (1 - it.e), 64 * (1 - it.e) + 64)
            rec, rk = recr.next()
            S.op("dve", lambda e: e.reciprocal(out=rec[re_, 0:nq], in_=psO[ro_, 0:nq]), R=[pOk], W=[rk])
            if kind == "soft":
                tmp, tk = tmpr.next()
                S.op("dve", lambda e: e.tensor_tensor(out=tmp[re_, 0:nq], in0=psO[re_, 0:nq], in1=rec[re_, 0:nq],
                                                      op=ALU.mult), R=[pOk, rk], W=[tk])
                S.op("pool", lambda e: e.tensor_tensor(out=ob[re_, it.c, 0:nq], in0=tmp[re_, 0:nq],
                                                       in1=GtT[re_, it.c, q0:q0 + nq], op=ALU.mult),
                     R=[tk, "GtT"], W=[obk])
            elif it.which == 0:
                S.op("dve", lambda e: e.tensor_tensor(out=ofm[re_, it.c, 0:nq], in0=psO[re_, 0:nq], in1=rec[re_, 0:nq],
                                                      op=ALU.mult), R=[pOk, rk], W=[ofmk])
            else:
                tmp, tk = tmpr.next()
                S.op("dve", lambda e: e.tensor_tensor(out=tmp[re_, 0:nq], in0=psO[re_, 0:nq], in1=rec[re_, 0:nq],
                                                      op=ALU.mult), R=[pOk, rk], W=[tk])
                S.op("dve", lambda e: e.scalar_tensor_tensor(out=ofm[re_, it.c, 0:nq], in0=tmp[re_, 0:nq],
                                                             scalar=extra["neglam"][re_, 0:1], in1=ofm[re_, it.c, 0:nq],
                                                             op0=ALU.mult, op1=ALU.add), R=[tk, "neglam", ofmk], W=[ofmk])

        def diff_post(c, q0, nq, ob, obk, ofm, ofmk):
            sq, sqk = sqr.next()
            rs, rsk = recr.next()
            S.op("pool", lambda e: e.tensor_tensor(out=sq[:, 0:nq], in0=ofm[:, c, 0:nq], in1=ofm[:, c, 0:nq], op=ALU.mult),
                 R=[ofmk], W=[sqk])
            pw, (ka, kb) = spair.next()
            S.op("pe", lambda e: e.matmul(pw[:, 0:nq], lhsT=extra["bones"][:], rhs=sq[:, 0:nq], start=True, stop=True),
                 R=[sqk, "bones"], W=[ka])
            S.op("act", lambda e: e.activation(out=rs[:, 0:nq], in_=pw[:, 0:nq], func=AF.Sqrt, bias=1e-5, scale=1.0 / 64),
                 R=[ka], W=[rsk])
            S.op("dve", lambda e: e.reciprocal(out=rs[:, 0:nq], in_=rs[:, 0:nq]), R=[rsk], W=[rsk])
            S.op("dve", lambda e: e.tensor_tensor(out=sq[:, 0:nq], in0=ofm[:, c, 0:nq], in1=rs[:, 0:nq], op=ALU.mult),
                 R=[ofmk, rsk, sqk], W=[sqk])
            S.op("dve", lambda e: e.scalar_tensor_tensor(out=ob[:, c, 0:nq], in0=sq[:, 0:nq], scalar=extra["subcol"][:, 0:1],
                                                         in1=GtT[:, c, q0:q0 + nq], op0=ALU.mult, op1=ALU.mult),
                 R=[sqk, "subcol", "GtT"], W=[obk])

        def one_block(q0, nq, kts):
            ob, obk = obr.next()
            ofm, ofmk = ofmr.next() if kind == "diff" else (None, None)
            if masked:
                qz, qzk = qzr.next()
                for n, it in enumerate(masked):
                    S.op("dve", lambda e, n=n, it=it: e.tensor_scalar(
                        out=qz[:, n, 0:nq], in0=it.qsrc(q0, nq), scalar1=it.qmask, scalar2=None, op0=ALU.mult),
                        R=[it.qkey, "qmask"], W=["%s_%d" % (qzk, n)])
                    it.QT = (lambda q0_, nq_, n=n, qz=qz: qz[:, n, 0:nq_])
                    it.qkey_blk = "%s_%d" % (qzk, n)
            else:
                for it in allitems:
                    it.qkey_blk = it.qkey
            for c in range(2):
                for it in chunks[c]:
                    one_item(it, q0, nq, kts, ob, obk, ofm, ofmk)
                if kind == "diff":
                    diff_post(c, q0, nq, ob, obk, ofm, ofmk)
            S.dma("sp", k.OUTS[mname][:, :, q0:q0 + nq], ob[:, :, 0:nq], R=[obk], W=["OUTS_" + mname])

        for (q0, nq, kts) in qblocks(l):
            one_block(q0, nq, kts)
        S.barrier()
        S.flush()


def qblocks(l):
    blocks = []
    if l == 0:
        blocks.append((0, NCTX, [0, 1]))
    for j in range(8):
        blocks.append((NCTX + 512 * j, 512, list(range(NT))))
    return blocks


def mixer_gqa(k, l):
    nc, S, I = k.nc, k.S, k.I
    S.phase = "gqa%d" % l
    with ExitStack() as mx:
        QT = _sb(mx, nc, "gQT", [128, 2, T], BF16)
        KT = _sb(mx, nc, "gKT", [128, T], BF16)
        VA = _sb(mx, nc, "gVA", [128, NT, 4, 128], BF16)
        qmask = _sb(mx, nc, "gqmask", [128, 6], F32)
        S.dma("sp", qmask[:], I["qmask"], W=["qmask"])
        GtT = _sb(mx, nc, "gGtT", [128, 2, T], BF16)
        with ExitStack() as es:
            xmT = XmStream(k, es)
            w = _sb(es, nc, "gw", [128, 8, 768], BF16)
            stg = None
            load_w(k, w, "gw", I["in_w"][l][:, O_GQ:O_GQ + 768], 768, stg)
            rope = _sb(es, nc, "grope", [128, NT, 64], F32)
            S.dma("act", rope[:], I["rope64"].rearrange("(i p) c -> p i c", p=128), W=["rope"])
            gqk = _sb(es, nc, "gqk", [128, 6, 64], F32)
            for h in range(6):
                srcn = I["gqa_q_norm"] if h < 4 else I["gqa_k_norm"]
                S.dma("act", gqk[:, h, :], srcn[l:l + 1, :].to_broadcast([128, 64]), W=["gqk"])
            S.op("pool", lambda e: e.memset(VA[:], 1.0), W=["Vones"])
            sqr = ring(es, nc, "gsq", 2, [128, 6, 64], F32)
            ssr = ring(es, nc, "gss", 2, [128, 6], F32)
            qnr = ring(es, nc, "gqn", 2, [128, 6, 64], F32)
            tar = ring(es, nc, "gta", 2, [128, 6, 32], F32)
            tbr = ring(es, nc, "gtb", 2, [128, 6, 32], F32)
            tcr = ring(es, nc, "gtc", 2, [128, 6, 32], F32)
            tdr = ring(es, nc, "gtd", 2, [128, 6, 32], F32)
            qbr = ring(es, nc, "gqb", 2, [128, 6, 64], BF16)
            psA, psB, psT = psring(k, [0, 1]), psring(k, [2, 3]), psring(k, [4, 5])
            def tile_body(i):
                pa, pak = psA.next()
                pb, pbk = psB.next()
                pt, ptk = psT.next()
                if i % 4 == 0:
                    gate_fm_chunk(k, xmT, i // 4, w, "gw", 512, GtT, psB)
                inproj_tm(k, pa, pak, xmT, i, w, "gw", 0, 512)
                sq, sqk = sqr.next()
                ss, ssk = ssr.next()
                qn, qnk = qnr.next()
                ta, tak = tar.next()
                tb, tbk = tbr.next()
                tc, tck = tcr.next()
                td, tdk = tdr.next()
                qb, qbk = qbr.next()
                pa3 = pa[:, 0:384].rearrange("p (h d) -> p h d", d=64)
                S.op("act", lambda e, sq=sq, pa3=pa3: e.activation(out=sq[:], in_=pa3, func=AF.Square), R=[pak], W=[sqk])
                S.op("dve", lambda e, sq=sq, ss=ss: e.tensor_reduce(out=ss[:], in_=sq[:], axis=AX.X, op=ALU.add),
                     R=[sqk], W=[ssk])
                S.op("act", lambda e, ss=ss: e.activation(out=ss[:], in_=ss[:], func=AF.Sqrt, bias=1e-6, scale=1.0 / 64),
                     R=[ssk], W=[ssk])
                S.op("dve", lambda e, ss=ss: e.reciprocal(out=ss[:], in_=ss[:]), R=[ssk], W=[ssk])
                S.op("dve", lambda e, qn=qn, pa3=pa3, ss=ss: e.tensor_tensor(
                    out=qn[:], in0=pa3, in1=ss[:].unsqueeze(2).to_broadcast([128, 6, 64]), op=ALU.mult),
                    R=[pak, ssk], W=[qnk])
                S.op("pool", lambda e, qn=qn: e.tensor_tensor(out=qn[:], in0=qn[:], in1=gqk[:], op=ALU.mult),
                     R=[qnk, "gqk"], W=[qnk])
                cB = rope[:, i, 0:32].unsqueeze(1).to_broadcast([128, 6, 32])
                sB = rope[:, i, 32:64].unsqueeze(1).to_broadcast([128, 6, 32])
                t1, t2 = qn[:, :, 0:32], qn[:, :, 32:64]
                S.op("dve", lambda e, ta=ta, t1=t1, cB=cB: e.tensor_tensor(out=ta[:], in0=t1, in1=cB, op=ALU.mult),
                     R=[qnk, "rope"], W=[tak])
                S.op("pool", lambda e, tb=tb, t2=t2, sB=sB: e.tensor_tensor(out=tb[:], in0=t2, in1=sB, op=ALU.mult),
                     R=[qnk, "rope"], W=[tbk])
                S.op("dve", lambda e, tc=tc, t2=t2, cB=cB: e.tensor_tensor(out=tc[:], in0=t2, in1=cB, op=ALU.mult),
                     R=[qnk, "rope"], W=[tck])
                S.op("pool", lambda e, td=td, t1=t1, sB=sB: e.tensor_tensor(out=td[:], in0=t1, in1=sB, op=ALU.mult),
                     R=[qnk, "rope"], W=[tdk])
                def perm_q(ap):
                    return ap.rearrange("p (c e) d -> p e c d", c=2)

                def src_q(ap):
                    return ap.rearrange("p (e c) d -> p e c d", c=2)
                S.op("dve", lambda e, qb=qb, ta=ta, tb=tb: e.tensor_tensor(
                    out=perm_q(qb[:, 0:4, 0:32]), in0=src_q(ta[:, 0:4, :]), in1=src_q(tb[:, 0:4, :]), op=ALU.subtract),
                    R=[tak, tbk], W=[qbk])
                S.op("dve", lambda e, qb=qb, tc=tc, td=td: e.tensor_tensor(
                    out=perm_q(qb[:, 0:4, 32:64]), in0=src_q(tc[:, 0:4, :]), in1=src_q(td[:, 0:4, :]), op=ALU.add),
                    R=[tck, tdk], W=[qbk])
                S.op("pool", lambda e, qb=qb, ta=ta, tb=tb: e.tensor_tensor(out=qb[:, 4:6, 0:32], in0=ta[:, 4:6, :],
                                                                            in1=tb[:, 4:6, :], op=ALU.subtract),
                     R=[tak, tbk], W=[qbk])
                S.op("pool", lambda e, qb=qb, tc=tc, td=td: e.tensor_tensor(out=qb[:, 4:6, 32:64], in0=tc[:, 4:6, :],
                                                                            in1=td[:, 4:6, :], op=ALU.add),
                     R=[tck, tdk], W=[qbk])
                ptb = pt[:, :].bitcast(BF16)
                qb2 = qb[:].rearrange("p h d -> p (h d)")
                for c in range(3):
                    S.op("pe", lambda e, c=c, ptb=ptb, qb2=qb2: e.transpose(out=ptb[:, c * 128:(c + 1) * 128],
                                                                             in_=qb2[:, c * 128:(c + 1) * 128],
                                                                             identity=k.ident[:]),
                         R=[qbk, "ident"], W=[ptk])
                S.op("act", lambda e, i=i, ptb=ptb: e.activation(
                    out=QT[:, :, i * 128:(i + 1) * 128], in_=ptb[:, 0:256].rearrange("p (c t) -> p c t", c=2),
                    func=AF.Copy), R=[ptk], W=["QT"])
                S.op("act", lambda e, i=i, ptb=ptb: e.activation(out=KT[:, i * 128:(i + 1) * 128], in_=ptb[:, 256:384],
                                                                  func=AF.Copy), R=[ptk], W=["KT"])
                vsrc = pa[:, 384:512].rearrange("p (h d) -> p h d", d=64)
                S.op("act", lambda e, i=i, vsrc=vsrc: e.activation(out=VA[:, i, 0:4:2, 0:64], in_=vsrc, func=AF.Copy),
                     R=[pak, "Vones"], W=["V"])
                S.op("act", lambda e, i=i, vsrc=vsrc: e.activation(out=VA[:, i, 1:4:2, 64:128], in_=vsrc, func=AF.Copy),
                     R=[pak, "Vones"], W=["V"])

            for i in range(NT):
                tile_body(i)
            S.barrier()
            S.flush()
        chunks = [[], []]
        for h in range(4):
            it = Item()
            c, e2, hk = h % 2, h // 2, h // 2
            it.KT = (lambda kt: KT[:, kt * 128:(kt + 1) * 128])
            it.qsrc = (lambda q0, nq, c=c: QT[:, c, q0:q0 + nq])
            it.qmask = qmask[:, e2:e2 + 1]
            it.VA = (lambda kt, h=h: VA[:, kt, h, :])
            it.kkey, it.qkey, it.vkey = "KT", "QT", "V"
            it.c, it.e = h // 2, h % 2
            chunks[it.c].append(it)
        attention_phase(k, l, "gqa", chunks, GQA_SCALE, GtT)


def mixer_mla(k, l):
    nc, S, I = k.nc, k.S, k.I
    S.phase = "mla%d" % l
    with ExitStack() as mx:
        QT = _sb(mx, nc, "mQT", [128, 4, T], BF16)
        KT = _sb(mx, nc, "mKT", [128, 4, T], BF16)
        VA = _sb(mx, nc, "mVA", [128, NT, 4, 128], BF16)
        GtT = _sb(mx, nc, "mGtT", [128, 2, T], BF16)
        with ExitStack() as es:
            xs = XmStream(k, es)
            w = _sb(es, nc, "mw", [128, 8, 672], BF16)
            wuq = _sb(es, nc, "wuq", [128, 2, 384], BF16)
            wukv = _sb(es, nc, "wukv", [128, 1, 512], BF16)
            stg = None
            load_w(k, w, "mw", I["in_w"][l][:, 0:672], 672, stg)
            load_w(k, wuq, "wuq", I["mla_w_uq"][l], 384, stg, rows=2)
            load_w(k, wukv, "wukv", I["mla_w_ukv"][l], 512, stg, rows=1)
            rope = _sb(es, nc, "mrope", [128, NT, 32], F32)
            S.dma("act", rope[:], I["rope32"].rearrange("(i p) c -> p i c", p=128), W=["rope"])
            gB = _sb(es, nc, "mgB", [128, 384], F32)
            S.dma("act", gB[:, 0:256], I["mla_q_norm"][l:l + 1, :].to_broadcast([128, 256]), W=["gB"])
            S.dma("act", gB[:, 256:384], I["mla_kv_norm"][l:l + 1, :].to_broadcast([128, 128]), W=["gB"])
            S.op("pool", lambda e: e.memset(VA[:], 1.0), W=["Vones"])
            psg = psring(k, [2])
            junkr = ring(es, nc, "mjunk", 2, [128, 256], F32)
            ssr = ring(es, nc, "mss", 2, [128, 2], F32)
            nbr = ring(es, nc, "mnb", 2, [128, 384], BF16)
            nTr = ring(es, nc, "mnT", 2, [128, 3, 128], BF16)
            qtr = ring(es, nc, "mqt", 2, [128, 4, 96], BF16)
            ktr = ring(es, nc, "mkt", 2, [128, 4, 96], BF16)
            r16 = [ring(es, nc, "mr%d" % n, 2, [128, 4, 16], F32) for n in range(4)]
            k16 = [ring(es, nc, "mk%d" % n, 2, [128, 16], F32) for n in range(4)]
            kper = ring(es, nc, "mkpe", 2, [128, 32], F32)
            psA = psring(k, [0, 1])

            def tile_body(i):
                pa, pak = psA.next()
                pb, pbk = k.ps[2], "ps2"
                pt, ptk = k.ps[3], "ps3"
                pq, pqk = k.ps[4], "ps4"
                pk_, pkk = k.ps[5], "ps5"
                pt2, pt2k = (k.ps[6], "ps6") if i % 2 == 0 else (k.ps[7], "ps7")
                if i % 4 == 0:
                    gate_fm_chunk(k, xs, i // 4, w, "mw", 416, GtT, psg)
                inproj_tm(k, pa, pak, xs, i, w, "mw", 0, 416)
                junk, jk = junkr.next()
                ss, ssk = ssr.next()
                nb, nbk = nbr.next()
                nT, nTk = nTr.next()
                S.op("act", lambda e: e.activation(out=junk[:, 0:256], in_=pa[:, 0:256], func=AF.Square,
                                                   accum_out=ss[:, 0:1]), R=[pak], W=[jk, ssk])
                S.op("act", lambda e: e.activation(out=junk[:, 0:128], in_=pa[:, 256:384], func=AF.Square,
                                                   accum_out=ss[:, 1:2]), R=[pak], W=[jk, ssk])
                S.op("act", lambda e: e.activation(out=ss[:, 0:1], in_=ss[:, 0:1], func=AF.Sqrt, bias=1e-6,
                                                   scale=1.0 / 256), R=[ssk], W=[ssk])
                S.op("act", lambda e: e.activation(out=ss[:, 1:2], in_=ss[:, 1:2], func=AF.Sqrt, bias=1e-6,
                                                   scale=1.0 / 128), R=[ssk], W=[ssk])
                S.op("dve", lambda e: e.reciprocal(out=ss[:], in_=ss[:]), R=[ssk], W=[ssk])
                S.op("dve", lambda e: e.scalar_tensor_tensor(out=nb[:, 0:256], in0=pa[:, 0:256], scalar=ss[:, 0:1],
                                                             in1=gB[:, 0:256], op0=ALU.mult, op1=ALU.mult),
                     R=[pak, ssk, "gB"], W=[nbk])
                S.op("dve", lambda e: e.scalar_tensor_tensor(out=nb[:, 256:384], in0=pa[:, 256:384], scalar=ss[:, 1:2],
                                                             in1=gB[:, 256:384], op0=ALU.mult, op1=ALU.mult),
                     R=[pak, ssk, "gB"], W=[nbk])
                if CUT <= 1:
                    return
                ptb = pt[:, :].bitcast(BF16)
                for c in range(3):
                    S.op("pe", lambda e, c=c: e.transpose(out=ptb[:, c * 128:(c + 1) * 128],
                                                          in_=nb[:, c * 128:(c + 1) * 128], identity=k.ident[:]),
                         R=[nbk, "ident"], W=[ptk])
                S.op("act", lambda e: e.activation(out=nT[:].rearrange("p c t -> p (c t)"), in_=ptb[:, 0:384],
                                                   func=AF.Copy), R=[ptk], W=[nTk])
                if CUT <= 2:
                    return
                for c in range(2):
                    S.op("pe", lambda e, c=c: e.matmul(pq[:, 0:384], lhsT=nT[:, c, :], rhs=wuq[:, c, :],
                                                       start=(c == 0), stop=(c == 1)),
                         R=[nTk] + wkeys("wuq", 0, 384), W=[pqk])
                S.op("pe", lambda e: e.matmul(pk_[:, 0:512], lhsT=nT[:, 2, :], rhs=wukv[:, 0, :], start=True, stop=True),
                     R=[nTk] + wkeys("wukv", 0, 512), W=[pkk])
                if CUT <= 3:
                    return
                qt, qtk = qtr.next()
                kt_, ktk = ktr.next()
                pq3 = pq[:, 0:384].rearrange("p (h d) -> p h d", d=96)
                pk3 = pk_[:, 0:512].rearrange("p (h d) -> p h d", d=128)
                S.op("dve", lambda e: e.tensor_copy(out=qt[:, :, 0:64], in_=pq3[:, :, 0:64]), R=[pqk], W=[qtk])
                S.op("act", lambda e: e.activation(out=kt_[:, :, 0:64], in_=pk3[:, :, 0:64], func=AF.Copy),
                     R=[pkk], W=[ktk])
                S.op("act", lambda e: e.activation(out=VA[:, i, 0:4:2, 0:64], in_=pk3[:, 0:4:2, 64:128], func=AF.Copy),
                     R=[pkk, "Vones"], W=["V"])
                S.op("dve", lambda e: e.tensor_copy(out=VA[:, i, 1:4:2, 64:128], in_=pk3[:, 1:4:2, 64:128]),
                     R=[pkk, "Vones"], W=["V"])
                if CUT <= 4:
                    return
                cB = rope[:, i, 0:16].unsqueeze(1).to_broadcast([128, 4, 16])
                sB = rope[:, i, 16:32].unsqueeze(1).to_broadcast([128, 4, 16])
                t1, t2 = pq3[:, :, 64:80], pq3[:, :, 80:96]
                (a, ak), (b, bk), (c_, ck), (d_, dk) = [r.next() for r in r16]
                S.op("dve", lambda e: e.tensor_tensor(out=a[:], in0=t1, in1=cB, op=ALU.mult), R=[pqk, "rope"], W=[ak])
                S.op("dve", lambda e: e.tensor_tensor(out=b[:], in0=t2, in1=sB, op=ALU.mult), R=[pqk, "rope"], W=[bk])
                S.op("dve", lambda e: e.tensor_tensor(out=c_[:], in0=t2, in1=cB, op=ALU.mult), R=[pqk, "rope"], W=[ck])
                S.op("dve", lambda e: e.tensor_tensor(out=d_[:], in0=t1, in1=sB, op=ALU.mult), R=[pqk, "rope"], W=[dk])
                S.op("pool", lambda e: e.tensor_tensor(out=qt[:, :, 64:80], in0=a[:], in1=b[:], op=ALU.subtract),
                     R=[ak, bk], W=[qtk])
                S.op("pool", lambda e: e.tensor_tensor(out=qt[:, :, 80:96], in0=c_[:], in1=d_[:], op=ALU.add),
                     R=[ck, dk], W=[qtk])
                if CUT <= 5:
                    return
                c1, s1 = rope[:, i, 0:16], rope[:, i, 16:32]
                u1, u2 = pa[:, 384:400], pa[:, 400:416]
                (a2, a2k), (b2, b2k), (c2, c2k), (d2, d2k) = [r.next() for r in k16]
                kpe, kpek = kper.next()
                S.op("dve", lambda e: e.tensor_tensor(out=a2[:], in0=u1, in1=c1, op=ALU.mult), R=[pak, "rope"], W=[a2k])
                S.op("dve", lambda e: e.tensor_tensor(out=b2[:], in0=u2, in1=s1, op=ALU.mult), R=[pak, "rope"], W=[b2k])
                S.op("dve", lambda e: e.tensor_tensor(out=c2[:], in0=u2, in1=c1, op=ALU.mult), R=[pak, "rope"], W=[c2k])
                S.op("dve", lambda e: e.tensor_tensor(out=d2[:], in0=u1, in1=s1, op=ALU.mult), R=[pak, "rope"], W=[d2k])
                S.op("pool", lambda e: e.tensor_tensor(out=kpe[:, 0:16], in0=a2[:], in1=b2[:], op=ALU.subtract),
                     R=[a2k, b2k], W=[kpek])
                S.op("pool", lambda e: e.tensor_tensor(out=kpe[:, 16:32], in0=c2[:], in1=d2[:], op=ALU.add),
                     R=[c2k, d2k], W=[kpek])
                S.op("pool", lambda e: e.tensor_copy(out=kt_[:, :, 64:96],
                                                     in_=kpe[:].unsqueeze(1).to_broadcast([128, 4, 32])),
                     R=[kpek], W=[ktk])
                if CUT <= 6:
                    return
                pt2b = pt2[:, :].bitcast(BF16)
                for h in range(4):
                    S.op("pe", lambda e, h=h: e.transpose(out=pt2b[0:96, h * 128:(h + 1) * 128], in_=qt[:, h, :],
                                                          identity=k.ident[:]), R=[qtk, "ident"], W=[pt2k])
                    S.op("pe", lambda e, h=h: e.transpose(out=pt2b[0:96, 512 + h * 128:512 + (h + 1) * 128],
                                                          in_=kt_[:, h, :], identity=k.ident[:]),
                         R=[ktk, "ident"], W=[pt2k])
                S.op("act", lambda e: e.activation(out=QT[0:96, :, i * 128:(i + 1) * 128],
                                                   in_=pt2b[0:96, 0:512].rearrange("p (h t) -> p h t", h=4),
                                                   func=AF.Copy), R=[pt2k], W=["QT"])
                S.op("act", lambda e: e.activation(out=KT[0:96, :, i * 128:(i + 1) * 128],
                                                   in_=pt2b[0:96, 512:1024].rearrange("p (h t) -> p h t", h=4),
                                                   func=AF.Copy), R=[pt2k], W=["KT"])

            for i in range(NT):
                tile_body(i)
            S.barrier()
            S.flush()
        chunks = [[], []]
        for h in range(4):
            it = Item()
            it.KT = (lambda kt, h=h: KT[0:96, h, kt * 128:(kt + 1) * 128])
            it.QT = (lambda q0, nq, h=h: QT[0:96, h, q0:q0 + nq])
            it.VA = (lambda kt, h=h: VA[:, kt, h, :])
            it.kkey, it.qkey, it.vkey = "KT", "QT", "V"
            it.c, it.e = h // 2, h % 2
            chunks[it.c].append(it)
        if not SKIP_ATT:
            attention_phase(k, l, "mla", chunks, MLA_SCALE, GtT)


def mixer_diff(k, l):
    nc, S, I = k.nc, k.S, k.I
    S.phase = "diff%d" % l
    lam_init = 0.8 - 0.6 * math.exp(-0.3 * l)
    with ExitStack() as mx:
        QT = _sb(mx, nc, "dQT", [128, 2, T], BF16)
        KT = _sb(mx, nc, "dKT", [128, 2, T], BF16)
        qmask = _sb(mx, nc, "dqmask", [128, 6], F32)
        S.dma("sp", qmask[:], I["qmask"], W=["qmask"])
        VA = _sb(mx, nc, "dVA", [128, NT, 4, 128], BF16)
        GtT = _sb(mx, nc, "dGtT", [128, 2, T], BF16)
        neglam = _sb(mx, nc, "neglam", [128, 2], F32)
        subcol = _sb(mx, nc, "subcol", [128, 1], F32)
        bones = _sb(mx, nc, "dbones", [128, 128], F32)
        S.dma("sp", bones[:], I["bones"], W=["bones"])
        with ExitStack() as es:
            xs = XmStream(k, es)
            w = _sb(es, nc, "dw", [128, 8, 1024], BF16)
            stg = None
            load_w(k, w, "dw", I["in_w"][l][:, O_DQ:O_DQ + 1024], 1024, stg)
            rope = _sb(es, nc, "drope", [128, NT, 32], F32)
            S.dma("act", rope[:], I["rope32"].rearrange("(i p) c -> p i c", p=128), W=["rope"])
            S.dma("act", subcol[:], I["subcol"][l], W=["subcol"])
            S.op("pool", lambda e: e.tensor_scalar(out=subcol[:], in0=subcol[:], scalar1=float(1.0 - lam_init),
                                                   scalar2=None, op0=ALU.mult), R=["subcol"], W=["subcol"])
            S.op("pool", lambda e: e.memset(VA[:], 1.0), W=["Vones"])
            psg = psring(k, [6, 7])
            dl = _sb(es, nc, "dl", [1, 128], F32)
            pr = _sb(es, nc, "dpr", [1, 2, 32], F32)
            sm = _sb(es, nc, "dsm", [1, 2], F32)
            nl = _sb(es, nc, "dnl", [1, 2], F32)
            one = _sb(es, nc, "done", [1, 128], F32)
            S.dma("sp", dl[:], I["diff_lambda"][l:l + 1].rearrange("o a b -> o (a b)"), W=["dl"])
            S.op("pool", lambda e: e.memset(one[:], 1.0), W=["one"])
            S.op("dve", lambda e: e.tensor_tensor(out=pr[:, 0, :], in0=dl[:, 0:32], in1=dl[:, 32:64], op=ALU.mult),
                 R=["dl"], W=["pr"])
            S.op("dve", lambda e: e.tensor_tensor(out=pr[:, 1, :], in0=dl[:, 64:96], in1=dl[:, 96:128], op=ALU.mult),
                 R=["dl"], W=["pr"])
            S.op("dve", lambda e: e.tensor_reduce(out=sm[:], in_=pr[:], axis=AX.X, op=ALU.add), R=["pr"], W=["sm"])
            S.op("act", lambda e: e.activation(out=sm[:], in_=sm[:], func=AF.Exp), R=["sm"], W=["sm"])
            S.op("dve", lambda e: e.tensor_tensor(out=nl[:, 0:1], in0=sm[:, 1:2], in1=sm[:, 0:1], op=ALU.subtract),
                 R=["sm"], W=["nl"])
            S.op("dve", lambda e: e.tensor_scalar(out=nl[:, 0:1], in0=nl[:, 0:1], scalar1=float(-lam_init), scalar2=None,
                                                  op0=ALU.add), R=["nl"], W=["nl"])
            S.op("dve", lambda e: e.tensor_copy(out=nl[:, 1:2], in_=nl[:, 0:1]), R=["nl"], W=["nl"])
            S.op("pe", lambda e: e.matmul(k.ps[6][:, 0:2], lhsT=one[:], rhs=nl[:], start=True, stop=True),
                 R=["one", "nl"], W=["ps6"])
            S.op("dve", lambda e: e.tensor_copy(out=neglam[:], in_=k.ps[6][:, 0:2]), R=["ps6"], W=["neglam"])
            r16 = [ring(es, nc, "dr%d" % n, 2, [128, 16, 16], F32) for n in range(4)]
            qbr = ring(es, nc, "dqb", 2, [128, 16, 32], BF16)
            psA, psB, psT = psring(k, [0, 1]), psring(k, [2, 3]), psring(k, [4, 5])

            def tile_body(i):
                pa, pak = psA.next()
                pb, pbk = psB.next()
                pt, ptk = psT.next()
                if i % 4 == 0:
                    gate_fm_chunk(k, xs, i // 4, w, "dw", 768, GtT, psg)
                inproj_tm(k, pa, pak, xs, i, w, "dw", 0, 512)
                inproj_tm(k, pb, pbk, xs, i, w, "dw", 512, 256)
                pa3 = pa[:, :].rearrange("p (u d) -> p u d", d=32)
                cB = rope[:, i, 0:16].unsqueeze(1).to_broadcast([128, 16, 16])
                sB = rope[:, i, 16:32].unsqueeze(1).to_broadcast([128, 16, 16])
                t1, t2 = pa3[:, :, 0:16], pa3[:, :, 16:32]
                (a, ak), (b, bk), (c_, ck), (d_, dk) = [r.next() for r in r16]
                qb, qbk = qbr.next()
                S.op("dve", lambda e: e.tensor_tensor(out=a[:], in0=t1, in1=cB, op=ALU.mult), R=[pak, "rope"], W=[ak])
                S.op("dve", lambda e: e.tensor_tensor(out=b[:], in0=t2, in1=sB, op=ALU.mult), R=[pak, "rope"], W=[bk])
                S.op("dve", lambda e: e.tensor_tensor(out=c_[:], in0=t2, in1=cB, op=ALU.mult), R=[pak, "rope"], W=[ck])
                S.op("dve", lambda e: e.tensor_tensor(out=d_[:], in0=t1, in1=sB, op=ALU.mult), R=[pak, "rope"], W=[dk])
                S.op("pool", lambda e: e.tensor_tensor(out=qb[:, :, 0:16], in0=a[:], in1=b[:], op=ALU.subtract),
                     R=[ak, bk], W=[qbk])
                S.op("pool", lambda e: e.tensor_tensor(out=qb[:, :, 16:32], in0=c_[:], in1=d_[:], op=ALU.add),
                     R=[ck, dk], W=[qbk])
                ptb = pt[:, :].bitcast(BF16)
                qb2 = qb[:].rearrange("p u d -> p (u d)")
                for c in range(4):
                    S.op("pe", lambda e, c=c: e.transpose(out=ptb[:, c * 128:(c + 1) * 128],
                                                          in_=qb2[:, c * 128:(c + 1) * 128], identity=k.ident[:]),
                         R=[qbk, "ident"], W=[ptk])
                S.op("act", lambda e: e.activation(out=QT[:, :, i * 128:(i + 1) * 128],
                                                   in_=ptb[:, 0:256].rearrange("p (c t) -> p c t", c=2), func=AF.Copy),
                     R=[ptk], W=["QT"])
                S.op("act", lambda e: e.activation(out=KT[:, :, i * 128:(i + 1) * 128],
                                                   in_=ptb[:, 256:512].rearrange("p (c t) -> p c t", c=2), func=AF.Copy),
                     R=[ptk], W=["KT"])
                vsrc = pb[:, 0:256].rearrange("p (h d) -> p h d", d=64)
                S.op("act", lambda e: e.activation(out=VA[:, i, 0:4:2, 0:64], in_=vsrc[:, 0:4:2, :], func=AF.Copy),
                     R=[pbk, "Vones"], W=["V"])
                S.op("act", lambda e: e.activation(out=VA[:, i, 1:4:2, 64:128], in_=vsrc[:, 1:4:2, :], func=AF.Copy),
                     R=[pbk, "Vones"], W=["V"])

            for i in range(NT):
                tile_body(i)
            S.barrier()
            S.flush()

        chunks = [[], []]
        for h in range(4):
            for which in range(2):
                it = Item()
                hp, ul = h // 2, (h % 2) * 2 + which
                it.KT = (lambda kt, hp=hp: KT[:, hp, kt * 128:(kt + 1) * 128])
                it.qsrc = (lambda q0, nq, hp=hp: QT[:, hp, q0:q0 + nq])
                it.qmask = qmask[:, 2 + ul:3 + ul]
                it.VA = (lambda kt, h=h: VA[:, kt, h, :])
                it.kkey, it.qkey, it.vkey = "KT", "QT", "V"
                it.c, it.e, it.which = h // 2, h % 2, which
                chunks[it.c].append(it)
        attention_phase(k, l, "diff", chunks, DIFF_SCALE, GtT, kind="diff",
                        extra={"neglam": neglam, "subcol": subcol, "bones": bones})


def xm_chunks():
    return [(ci, ci * 512, min(512, T - ci * 512)) for ci in range((T + 511) // 512)]


def mixer_rwkv(k, l):
    nc, S, I = k.nc, k.S, k.I
    S.phase = "rwkv%d" % l
    SH, SC, BT = k.SH, k.SC, k.BT
    with ExitStack() as mx:
        Vtm = _sb(mx, nc, "rVtm", [128, NT, 256], BF16)
        Gt = _sb(mx, nc, "rGt", [128, NT, 256], BF16)
        PCs = _sb(mx, nc, "rPC", [128, 2, 2, NT], F32)
        cols = _sb(mx, nc, "rcols", [128, 2, 7], F32)
        omk = _sb(mx, nc, "romk", [128, 2], F32)
        p1 = ExitStack()
        tw = _sb(p1, nc, "rtw", [128, T], BF16)
        adb = _sb(p1, nc, "radb", [128, T], BF16)
        S.dma("sp", cols[:], I["rw_cols"][l], W=["cols"])
        for hp in range(2):
            S.op("dve", lambda e, hp=hp: e.tensor_scalar(out=omk[:, hp:hp + 1], in0=cols[:, hp, 1:2], scalar1=-1.0,
                                                        scalar2=1.0, op0=ALU.mult, op1=ALU.add), R=["cols"], W=["omk"])
        with ExitStack() as es:
            xs = XmStream(k, es)
            w = _sb(es, nc, "rw", [128, 8, 1280], BF16)
            load_w(k, w, "rw", I["in_w"][l][:, O_RW:O_RW + 1280], 1280)
            mu = _sb(es, nc, "rmu", [128, 2, 8], F32)
            mu0 = _sb(es, nc, "rmu0", [128, 8], F32)
            S.dma("sp", mu[:], I["rw_mu"][l], W=["mu"])
            S.op("dve", lambda e: e.tensor_tensor(out=mu0[:], in0=mu[:, 0, :], in1=mu[:, 1, :], op=ALU.add),
                 R=["mu"], W=["mu0"])
            S.op("dve", lambda e: e.tensor_scalar(out=mu0[:], in0=mu0[:], scalar1=-1.0, scalar2=1.0, op0=ALU.mult,
                                                  op1=ALU.add), R=["mu0"], W=["mu0"])
            rawr = ring(es, nc, "rraw", 2, [128, T], F32)
            shfr = ring(es, nc, "rshf", 2, [128, T], F32)
            psr = psring(k, [0, 1, 2, 3])

            def fpair(fs):
                bufs = []
                for f in fs:
                    raw, rawk = rawr.next()
                    shf, shfk = shfr.next()
                    bufs.append((f, raw, rawk, shf, shfk))
                for (ci, q0, nq) in xm_chunks():
                    xb, xbk = xs.chunk(ci)
                    for (f, raw, rawk, shf, shfk) in bufs:
                        ps, pk = psr.next()
                        for kc in range(8):
                            S.op("pe", lambda e, kc=kc, ps=ps, xb=xb, nq=nq, f=f: e.matmul(
                                ps[:, 0:nq], lhsT=w[:, kc, f * 128:(f + 1) * 128], rhs=xb[:, kc, 0:nq],
                                start=(kc == 0), stop=(kc == 7)),
                                R=["%s_%d" % (xbk, kc)] + wkeys("rw", f * 128, 128), W=[pk], c=0.23)
                        S.op("act", lambda e, ps=ps, q0=q0, nq=nq, raw=raw: e.activation(
                            out=raw[:, q0:q0 + nq], in_=ps[:, 0:nq], func=AF.Copy), R=[pk], W=[rawk])
                for (f, raw, rawk, shf, shfk) in bufs:
                    finish_f(f, raw, rawk, shf, shfk)

            def finish_f(f, raw, rawk, shf, shfk):
                S.op("dve", lambda e: e.tensor_scalar(out=shf[:], in0=raw[:], scalar1=mu0[:, f:f + 1], scalar2=None,
                                                      op0=ALU.mult), R=[rawk, "mu0"], W=[shfk])
                for (a, b) in ((0, NCTX), (NCTX, T)):
                    S.op("dve", lambda e, a=a, b=b: e.scalar_tensor_tensor(
                        out=shf[:, a + 1:b], in0=raw[:, a:b - 1], scalar=mu[:, 0, f:f + 1], in1=shf[:, a + 1:b],
                        op0=ALU.mult, op1=ALU.add), R=[rawk, "mu", shfk], W=[shfk])
                    S.op("dve", lambda e, a=a, b=b: e.scalar_tensor_tensor(
                        out=shf[:, a:b - 1], in0=raw[:, a + 1:b], scalar=mu[:, 1, f:f + 1], in1=shf[:, a:b - 1],
                        op0=ALU.mult, op1=ALU.add), R=[rawk, "mu", shfk], W=[shfk])
                if f == 6:
                    S.op("act", lambda e: e.activation(out=tw[:], in_=shf[:], func=AF.Tanh), R=[shfk], W=["tw"])
                elif f == 7:
                    S.op("act", lambda e: e.activation(out=adb[:], in_=shf[:], func=AF.Copy), R=[shfk], W=["adb"])
                else:
                    S.dma("sp", SH[f], shf[:], R=[shfk], W=["SH%d" % f])

            for fs in ((6, 7), (0, 1), (2, 3), (4, 5)):
                fpair(fs)
            psg = psring(k, [4, 5])

            def gate_tile(i):
                pg, pgk = psg.next()
                inproj_tm(k, pg, pgk, xs, i, w, "rw", 1024, 256)
                S.op("act", lambda e: e.activation(out=Gt[:, i, :], in_=pg[:, 0:256], func=AF.Silu), R=[pgk],
                     W=["Gt%d" % i])

            for i in range(NT):
                gate_tile(i)
            S.barrier()
            S.flush()
        if CUT <= 1:
            p1.close()
            return
        with ExitStack() as es:
            wup = _sb(es, nc, "rwup", [128, 1, 256], BF16)
            aup = _sb(es, nc, "raup", [128, 1, 256], BF16)
            bones = _sb(es, nc, "rbones", [128, 128], F32)
            rm4 = _sb(es, nc, "rrm4", [128, 512], F32)
            load_w(k, wup, "wup", I["rwkv_w_up"][l].rearrange("d j n -> (d j) n"), 256, rows=1)
            load_w(k, aup, "aup", I["rwkv_a_up"][l].rearrange("d j n -> (d j) n"), 256, rows=1)
            S.dma("sp", bones[:], I["bones"], W=["bones"])
            S.op("pool", lambda e: e.memset(rm4[:], 1.0), W=["rm4"])
            for j in range(4):
                S.op("pool", lambda e, j=j: e.memset(rm4[:, j * 128:j * 128 + 1], 0.0), R=["rm4"], W=["rm4"])
            names = ["rr", "kr", "vr", "kk0", "sq", "rinv", "kap", "sg", "av", "lw", "cpre", "cc", "e1", "E1", "E2", "E3",
                     "e4", "E4", "tt", "kd", "be", "ks", "bt0"]
            R_ = {n: ring(es, nc, "r_" + n, 2, [128, 512], F32) for n in names}
            vbr = ring(es, nc, "r_vb", 2, [128, 512], BF16)
            btr = ring(es, nc, "r_btb", 2, [128, 512], BF16)
            str_ = ring(es, nc, "r_st", 2, [128, 6, 512], BF16)
            psr = psring(k, [0, 1, 2, 3, 4, 5])
            pst = psring(k, [6, 7])

            def blk(hp, ci, q0, nq):
                nt = nq // 128
                g = {n: R_[n].next() for n in names}

                def A(n):
                    return g[n][0][:, 0:nq]

                def Kk(n):
                    return g[n][1]
                S.dma("sp", A("rr"), SH[hp][:, q0:q0 + nq], W=[Kk("rr")])
                S.dma("sp", A("kr"), SH[2 + hp][:, q0:q0 + nq], W=[Kk("kr")])
                S.dma("act", A("vr"), SH[4 + hp][:, q0:q0 + nq], W=[Kk("vr")])
                S.op("dve", lambda e: e.tensor_scalar(out=A("kk0"), in0=A("kr"), scalar1=cols[:, hp, 0:1], scalar2=None,
                                                      op0=ALU.mult), R=[Kk("kr"), "cols"], W=[Kk("kk0")])
                S.op("pool", lambda e: e.tensor_tensor(out=A("sq"), in0=A("kk0"), in1=A("kk0"), op=ALU.mult),
                     R=[Kk("kk0")], W=[Kk("sq")])
                ps, pk = psr.next()
                S.op("pe", lambda e: e.matmul(ps[:, 0:nq], lhsT=bones[:], rhs=A("sq"), start=True, stop=True),
                     R=["bones", Kk("sq")], W=[pk])
                S.op("act", lambda e: e.activation(out=A("rinv"), in_=ps[:, 0:nq], func=AF.Sqrt, bias=1e-12, scale=1.0),
                     R=[pk], W=[Kk("rinv")])
                S.op("dve", lambda e: e.reciprocal(out=A("rinv"), in_=A("rinv")), R=[Kk("rinv")], W=[Kk("rinv")])
                S.op("dve", lambda e: e.tensor_tensor(out=A("kap"), in0=A("kk0"), in1=A("rinv"), op=ALU.mult),
                     R=[Kk("kk0"), Kk("rinv")], W=[Kk("kap")])
                vb, vbk = vbr.next()
                S.op("pool", lambda e: e.tensor_copy(out=vb[:, 0:nq], in_=A("vr")), R=[Kk("vr")], W=[vbk])
                pt, ptk = pst.next()
                ptb = pt[:, :].bitcast(BF16)
                for j in range(nt):
                    S.op("pe", lambda e, j=j: e.transpose(out=ptb[:, j * 128:(j + 1) * 128],
                                                          in_=vb[:, j * 128:(j + 1) * 128], identity=k.ident[:]),
                         R=[vbk, "ident"], W=[ptk])
                t0 = q0 // 128
                S.op("act", lambda e: e.activation(out=Vtm[:, t0:t0 + nt, hp * 128:(hp + 1) * 128],
                                                   in_=ptb[:, 0:nq].rearrange("p (j c) -> p j c", c=128), func=AF.Copy),
                     R=[ptk], W=["Vtm"])
                for d in range(2):
                    rows = slice(64 * d, 64 * d + 64)
                    pz, pzk = psr.next()
                    pa, pak = psr.next()
                    S.op("pe", lambda e, pz=pz, rows=rows: e.matmul(
                        pz[:, 0:nq], lhsT=wup[rows, 0, hp * 128:(hp + 1) * 128], rhs=tw[rows, q0:q0 + nq],
                        start=True, stop=True), R=["tw"] + wkeys("wup", 0, 256), W=[pzk])
                    S.op("pe", lambda e, pa=pa, rows=rows: e.matmul(
                        pa[:, 0:nq], lhsT=aup[rows, 0, hp * 128:(hp + 1) * 128], rhs=adb[rows, q0:q0 + nq],
                        start=True, stop=True), R=["adb"] + wkeys("aup", 0, 256), W=[pak])
                    S.op("act", lambda e, pz=pz, d=d: e.activation(out=A("sg"), in_=pz[:, 0:nq], func=AF.Sigmoid,
                                                                   bias=cols[:, hp, 3 + d:4 + d], scale=1.0),
                         R=[pzk, "cols"], W=[Kk("sg")])
                    S.op("act", lambda e, pa=pa, d=d: e.activation(out=A("av"), in_=pa[:, 0:nq], func=AF.Sigmoid,
                                                                   bias=cols[:, hp, 5 + d:6 + d], scale=1.0),
                         R=[pak, "cols"], W=[Kk("av")])
                    S.op("dve", lambda e: e.tensor_scalar(out=A("lw"), in0=A("sg"), scalar1=float(-DECAY_SCALE),
                                                          scalar2=None, op0=ALU.mult), R=[Kk("sg")], W=[Kk("lw")])
                    S.op("dve", lambda e: e.tensor_tensor_scan(out=A("cpre"), data0=rm4[:, 0:nq], data1=A("lw"),
                                                               initial=0.0, op0=ALU.mult, op1=ALU.add),
                         R=["rm4", Kk("lw")], W=[Kk("cpre")])
                    cp3 = A("cpre").rearrange("p (j c) -> p j c", c=128)
                    cC = cp3[:, :, 127:128]
                    cCb = cC.to_broadcast([128, nt, 128])
                    if d == 0:
                        cn = "cpre"
                    else:
                        cn = "cc"
                        c3 = A("cc").rearrange("p (j c) -> p j c", c=128)
                        S.op("dve", lambda e, c3=c3, cCb=cCb, cp3=cp3: e.tensor_tensor(out=c3, in0=cCb, in1=cp3,
                                                                                      op=ALU.subtract),
                             R=[Kk("cpre")], W=[Kk("cc")])
                        S.op("dve", lambda e: e.tensor_tensor(out=A("cc"), in0=A("cc"), in1=A("lw"), op=ALU.add),
                             R=[Kk("cc"), Kk("lw")], W=[Kk("cc")])
                    cA = A(cn)
                    cK = Kk(cn)
                    cA3 = cA.rearrange("p (j c) -> p j c", c=128)
                    S.op("pool", lambda e, cA=cA: e.tensor_tensor(out=A("e1"), in0=cA, in1=A("lw"), op=ALU.subtract),
                         R=[cK, Kk("lw")], W=[Kk("e1")])
                    S.op("pool", lambda e, cA3=cA3, cCb=cCb: e.tensor_tensor(
                        out=A("e4").rearrange("p (j c) -> p j c", c=128), in0=cCb, in1=cA3, op=ALU.subtract),
                        R=[cK, Kk("cpre")], W=[Kk("e4")])
                    S.op("act", lambda e: e.activation(out=A("E1"), in_=A("e1"), func=AF.Exp), R=[Kk("e1")], W=[Kk("E1")])
                    S.op("act", lambda e, cA=cA: e.activation(out=A("E2"), in_=cA, func=AF.Exp), R=[cK], W=[Kk("E2")])
                    S.op("act", lambda e, cA=cA: e.activation(out=A("E3"), in_=cA, func=AF.Exp, scale=-1.0),
                         R=[cK], W=[Kk("E3")])
                    S.op("act", lambda e: e.activation(out=A("E4"), in_=A("e4"), func=AF.Exp), R=[Kk("e4")], W=[Kk("E4")])
                    S.op("act", lambda e, d=d, cC=cC: e.activation(
                        out=PCs[:, d, hp, t0:t0 + nt], in_=cC.rearrange("p j o -> p (j o)"), func=AF.Exp),
                        R=[Kk("cpre")], W=["PC"])
                    S.op("dve", lambda e: e.tensor_scalar(out=A("tt"), in0=A("av"), scalar1=cols[:, hp, 1:2],
                                                          scalar2=omk[:, hp:hp + 1], op0=ALU.mult, op1=ALU.add),
                         R=[Kk("av"), "cols", "omk"], W=[Kk("tt")])
                    S.op("pool", lambda e: e.tensor_tensor(out=A("kd"), in0=A("kr"), in1=A("tt"), op=ALU.mult),
                         R=[Kk("kr"), Kk("tt")], W=[Kk("kd")])
                    S.op("pool", lambda e: e.tensor_tensor(out=A("be"), in0=A("av"), in1=A("kap"), op=ALU.mult),
                         R=[Kk("av"), Kk("kap")], W=[Kk("be")])
                    st, stk = str_.next()
                    S.op("dve", lambda e, st=st: e.tensor_tensor(out=st[:, 0, 0:nq], in0=A("kap"), in1=A("E1"), op=ALU.mult),
                         R=[Kk("kap"), Kk("E1")], W=[stk])
                    S.op("pool", lambda e, st=st: e.tensor_tensor(out=st[:, 1, 0:nq], in0=A("rr"), in1=A("E2"), op=ALU.mult),
                         R=[Kk("rr"), Kk("E2")], W=[stk])
                    S.op("dve", lambda e, st=st: e.tensor_tensor(out=st[:, 2, 0:nq], in0=A("kd"), in1=A("E3"), op=ALU.mult),
                         R=[Kk("kd"), Kk("E3")], W=[stk])
                    S.op("pool", lambda e, st=st: e.tensor_tensor(out=st[:, 3, 0:nq], in0=A("be"), in1=A("E3"), op=ALU.mult),
                         R=[Kk("be"), Kk("E3")], W=[stk])
                    S.op("dve", lambda e, st=st: e.tensor_tensor(out=st[:, 4, 0:nq], in0=A("kd"), in1=A("E4"), op=ALU.mult),
                         R=[Kk("kd"), Kk("E4")], W=[stk])
                    S.op("dve", lambda e, st=st: e.scalar_tensor_tensor(out=st[:, 5, 0:nq], in0=A("be"), scalar=-1.0,
                                                                        in1=A("E4"), op0=ALU.mult, op1=ALU.mult),
                         R=[Kk("be"), Kk("E4")], W=[stk])
                    S.dma("sp", SC[d][:, hp, :, q0:q0 + nq], st[:, :, 0:nq], R=[stk], W=["SC%d" % d])
                    if d == 0:
                        S.op("pool", lambda e: e.tensor_copy(out=A("ks"), in_=A("kd")), R=[Kk("kd")], W=[Kk("ks")])
                    else:
                        S.op("pool", lambda e: e.tensor_tensor(out=A("ks"), in0=A("ks"), in1=A("kd"), op=ALU.add),
                             R=[Kk("kd"), Kk("ks")], W=[Kk("ks")])
                btb, btk = btr.next()
                S.op("dve", lambda e: e.scalar_tensor_tensor(out=btb[:, 0:nq], in0=A("rr"), scalar=cols[:, hp, 2:3],
                                                             in1=A("ks"), op0=ALU.mult, op1=ALU.mult),
                     R=[Kk("rr"), "cols", Kk("ks")], W=[btk])
                S.dma("sp", BT[:, hp, q0:q0 + nq], btb[:, 0:nq], R=[btk], W=["BT"])

            for hp in range(2):
                for (ci, q0, nq) in xm_chunks():
                    blk(hp, ci, q0, nq)
            S.barrier()
            S.flush()
        p1.close()
        if CUT <= 2:
            return
        with ExitStack() as es:
            Yacc = _sb(es, nc, "rYacc", [128, NT, 256], F32)
            Hb = [[_sb(es, nc, "rH%d%d" % (hp, j), [128, 64], F32) for j in range(2)] for hp in range(2)]
            Hh = [[_sb(es, nc, "rHh%d%d" % (hp, j), [128, 64], BF16) for j in range(2)] for hp in range(2)]
            masks = _sb(es, nc, "rmask", [128, 2, 640], F32)
            identf = _sb(es, nc, "ridentf", [128, 128], F32)
            bdm = _sb(es, nc, "rbdm", [128, 128], F32)
            hsel = _sb(es, nc, "rhsel", [128, 2], BF16)
            gnB = _sb(es, nc, "rgnB", [128, 2, 256], F32)
            S.dma("sp", masks[:], I["rmask"].rearrange("d p c -> p d c"), W=["masks"])
            S.dma("sp", identf[:], I["identf"], W=["identf"])
            S.dma("sp", bdm[:], I["bones"], W=["bdm"])
            S.dma("sp", hsel[:], I["hsel"], W=["hsel"])
            S.dma("act", gnB[:, 0, :], I["rwkv_gn_w"][l:l + 1, :].to_broadcast([128, 256]), W=["gnB"])
            S.dma("act", gnB[:, 1, :], I["rwkv_gn_b"][l:l + 1, :].to_broadcast([128, 256]), W=["gnB"])
            NB = int(os.environ.get("RW_NB", "4"))
            Xr = ring(es, nc, "rX", NB + 1, [128, 2, 6, 128], BF16)
            A1r = ring(es, nc, "rA1", NB, [128, 4, 256], BF16)
            A2r = ring(es, nc, "rA2", NB, [128, 4, 256], BF16)
            N0r = ring(es, nc, "rN0", NB, [128, 4, 128], BF16)
            Mr = ring(es, nc, "rM", NB, [128, 4, 128], BF16)
            MTr = ring(es, nc, "rMT", NB, [128, 4, 128], BF16)
            Mpr = ring(es, nc, "rMp", NB, [128, 4, 128], BF16)
            Ppr = ring(es, nc, "rPp", NB, [128, 4, 128], BF16)
            lvm = _sb(es, nc, "rlvm", [128, 7, 4, 128], U16)
            S.dma("sp", lvm[:], I["lvmask"], W=["lvm"])
            W0r = ring(es, nc, "rW0", NB, [128, 256], BF16)
            Xtr = ring(es, nc, "rXt", NB, [128, 3, 256], BF16)
            UGr = ring(es, nc, "rUG", NB, [128, 2, 256], BF16)
            Phr = ring(es, nc, "rPh", NB, [128, 2, 128], F32)
            Psr = ring(es, nc, "rPs", NB, [128, 2, 64], F32)
            Rgr = ring(es, nc, "rRg", NB, [128, 2, 128], BF16)
            btlr = ring(es, nc, "rbtl", 2, [128, 2, 128], BF16)
            ysr = ring(es, nc, "rys", 2, [128, 4, 64], F32)
            ycr = ring(es, nc, "ryc", 2, [128, 4, 64], F32)
            sqr = ring(es, nc, "rsq", 2, [128, 4, 64], F32)
            s4r = ring(es, nc, "rs4", 4, [128, 4], F32)
            bsr = ring(es, nc, "rbs", 2, [128, 4], F32)
            obr = ring(es, nc, "rob", 2, [128, 256], BF16)
            oTr = ring(es, nc, "roT", 2, [128, 2, 128], BF16)
            pall = psring(k, [0, 1, 2, 3, 4, 5, 6, 7])
            hcur = [0, 0]

            def hd(h):
                return h // 2, slice(64 * (h % 2), 64 * (h % 2) + 64)

            class U:
                pass

            def pre1(u):
                d, i = u.d, u.i
                u.X, u.Xk = Xr.next()
                X = u.X
                S.dma("sp" if i % 2 == 0 else "act", X[:], SC[d][:, :, :, i * 128:(i + 1) * 128], W=[u.Xk])
                u.A1, u.A1k = A1r.next()
                u.A2, u.A2k = A2r.next()
                u.N0, u.N0k = N0r.next()
                if SUB <= 1:
                    return
                pA = [pall.next() for _ in range(2)]
                pB = [pall.next() for _ in range(2)]
                pC = [pall.next() for _ in range(2)]
                for h in range(4):
                    hp, rows = hd(h)
                    e2 = h % 2
                    rkr = X[rows, hp, 0:2, :].rearrange("p a t -> p (a t)")
                    cs = slice(hp * 256, hp * 256 + 256)
                    S.op("pe", lambda e, hp=hp, rows=rows, rkr=rkr, cs=cs, e2=e2: e.matmul(
                        pA[e2][0][:, cs], lhsT=X[rows, hp, 2, :], rhs=rkr, start=True, stop=True),
                        R=[u.Xk], W=[pA[e2][1]])
                    S.op("pe", lambda e, hp=hp, rows=rows, rkr=rkr, cs=cs, e2=e2: e.matmul(
                        pB[e2][0][:, cs], lhsT=X[rows, hp, 3, :], rhs=rkr, start=True, stop=True),
                        R=[u.Xk], W=[pB[e2][1]])
                    S.op("pe", lambda e, hp=hp, rows=rows, e2=e2: e.matmul(
                        pC[e2][0][:, hp * 128:(hp + 1) * 128], lhsT=X[rows, hp, 0, :], rhs=X[rows, hp, 3, :],
                        start=True, stop=True), R=[u.Xk], W=[pC[e2][1]])
                if SUB <= 2:
                    return
                mA = masks[:, d, 0:256].unsqueeze(1).to_broadcast([128, 2, 256])
                mB = masks[:, d, 256:512].unsqueeze(1).to_broadcast([128, 2, 256])
                mC = masks[:, d, 512:640].unsqueeze(1).to_broadcast([128, 2, 128])
                for e2 in range(2):
                    S.op("dve", lambda e, e2=e2: e.tensor_tensor(
                        out=u.A1[:, e2:4:2, :], in0=pA[e2][0][:, :].rearrange("p (a c) -> p a c", a=2), in1=mA,
                        op=ALU.mult), R=[pA[e2][1], "masks"], W=[u.A1k])
                    S.op("dve", lambda e, e2=e2: e.tensor_tensor(
                        out=u.A2[:, e2:4:2, :], in0=pB[e2][0][:, :].rearrange("p (a c) -> p a c", a=2), in1=mB,
                        op=ALU.mult), R=[pB[e2][1], "masks"], W=[u.A2k])
                    S.op("dve", lambda e, e2=e2: e.tensor_tensor(
                        out=u.N0[:, e2:4:2, :], in0=pC[e2][0][:, 0:256].rearrange("p (a c) -> p a c", a=2), in1=mC,
                        op=ALU.mult), R=[pC[e2][1], "masks"], W=[u.N0k])
                D0, D0k = Mr.next()
                DT0, DT0k = MTr.next()
                idb = identf[:].unsqueeze(1).to_broadcast([128, 4, 128])
                S.op("pool", lambda e: e.tensor_copy(out=D0[:], in_=idb), R=["identf"], W=[D0k])
                S.op("pool", lambda e: e.tensor_copy(out=DT0[:], in_=idb), R=["identf"], W=[DT0k])
                S.op("dve", lambda e: e.copy_predicated(out=D0[:], mask=lvm[:, 0, :, :], data=u.N0[:]),
                     R=[u.N0k, "lvm", D0k], W=[D0k])
                S.op("dve", lambda e: e.copy_predicated(out=DT0[:], mask=lvm[:, 0, :, :], data=u.A2[:, :, 0:128]),
                     R=[u.A2k, "lvm", DT0k], W=[DT0k])
                u.D, u.Dk, u.DT, u.DTk = D0, D0k, DT0, DT0k
                u.TT, u.TTk = DT0, DT0k
                u.lvl = 1

            def merge_level(u):
                lv = u.lvl
                u.lvl += 1
                last = (lv == 6)
                D, Dk, DT, DTk = u.D, u.Dk, u.DT, u.DTk
                pQ = pall.next()
                for h in range(4):
                    S.op("pe", lambda e, h=h: e.matmul(pQ[0][:, h * 128:(h + 1) * 128], lhsT=u.N0[:, h, :], rhs=DT[:, h, :],
                                                       start=True, stop=True), R=[u.N0k, DTk], W=[pQ[1]])
                Q, Qk = Mpr.next()
                S.op("act", lambda e: e.activation(out=Q[:], in_=pQ[0][:, :].rearrange("p (a c) -> p a c", a=4),
                                                   func=AF.Copy), R=[pQ[1]], W=[Qk])
                if not last:
                    pP = pall.next()
                    for h in range(4):
                        S.op("pe", lambda e, h=h: e.matmul(pP[0][:, h * 128:(h + 1) * 128], lhsT=u.A2[:, h, 0:128],
                                                           rhs=D[:, h, :], start=True, stop=True),
                             R=[u.A2k, Dk], W=[pP[1]])
                    P, Pk = Ppr.next()
                    S.op("act", lambda e: e.activation(out=P[:], in_=pP[0][:, :].rearrange("p (a c) -> p a c", a=4),
                                                       func=AF.Copy), R=[pP[1]], W=[Pk])
                pT = pall.next()
                for h in range(4):
                    S.op("pe", lambda e, h=h: e.matmul(pT[0][:, h * 128:(h + 1) * 128], lhsT=D[:, h, :], rhs=Q[:, h, :],
                                                       start=True, stop=True), R=[Dk, Qk], W=[pT[1]])
                if not last:
                    pD = pall.next()
                    for h in range(4):
                        S.op("pe", lambda e, h=h: e.matmul(pD[0][:, h * 128:(h + 1) * 128], lhsT=DT[:, h, :],
                                                           rhs=P[:, h, :], start=True, stop=True),
                             R=[DTk, Pk], W=[pD[1]])
                S.op("dve", lambda e: e.copy_predicated(out=DT[:], mask=lvm[:, lv, :, :],
                                                        data=pT[0][:, :].rearrange("p (a c) -> p a c", a=4)),
                     R=[pT[1], "lvm", DTk], W=[DTk])
                if not last:
                    S.op("dve", lambda e: e.copy_predicated(out=D[:], mask=lvm[:, lv, :, :],
                                                            data=pD[0][:, :].rearrange("p (a c) -> p a c", a=4)),
                         R=[pD[1], "lvm", Dk], W=[Dk])

            def pre2(u):
                d, i, X = u.d, u.i, u.X
                TTf = u.TT
                u.W0, u.W0k = W0r.next()
                u.Xt, u.Xtk = Xtr.next()
                u.UG, u.UGk = UGr.next()
                pW = pall.next()
                for h in range(4):
                    S.op("pe", lambda e, h=h: e.matmul(pW[0][:, h * 64:(h + 1) * 64], lhsT=u.A1[:, h, 0:128],
                                                       rhs=Vtm[:, i, h * 64:(h + 1) * 64], start=True, stop=True),
                         R=[u.A1k, "Vtm"], W=[pW[1]])
                S.op("act", lambda e: e.activation(out=u.W0[:], in_=pW[0][:, 0:256], func=AF.Copy), R=[pW[1]], W=[u.W0k])
                pX = pall.next()
                pXb = pX[0][:, :].bitcast(BF16)
                for a, arr in enumerate((0, 4, 5)):
                    for hp in range(2):
                        S.op("pe", lambda e, a=a, arr=arr, hp=hp: e.transpose(
                            out=pXb[:, (a * 2 + hp) * 128:(a * 2 + hp + 1) * 128], in_=X[:, hp, arr, :],
                            identity=k.ident[:]), R=[u.Xk, "ident"], W=[pX[1]])
                S.op("dve", lambda e: e.tensor_copy(out=u.Xt[:].rearrange("p a c -> p (a c)"), in_=pXb[:, 0:768]),
                     R=[pX[1]], W=[u.Xtk])
                pU = pall.next()
                for h in range(4):
                    S.op("pe", lambda e, h=h: e.matmul(pU[0][:, h * 64:(h + 1) * 64], lhsT=TTf[:, h, :],
                                                       rhs=u.W0[:, h * 64:(h + 1) * 64], start=True, stop=True),
                         R=[u.TTk, u.W0k], W=[pU[1]])
                    S.op("pe", lambda e, h=h: e.matmul(pU[0][:, 256 + h * 64:256 + (h + 1) * 64], lhsT=TTf[:, h, :],
                                                       rhs=u.Xt[:, 0, h * 64:(h + 1) * 64], start=True, stop=True),
                         R=[u.TTk, u.Xtk], W=[pU[1]])
                S.op("act", lambda e: e.activation(out=u.UG[:].rearrange("p a c -> p (a c)"), in_=pU[0][:, :],
                                                   func=AF.Copy), R=[pU[1]], W=[u.UGk])

            def pre3(u):
                d, i, X = u.d, u.i, u.X
                u.Ph, u.Phk = Phr.next()
                u.Ps, u.Psk = Psr.next()
                u.Rg, u.Rgk = Rgr.next()
                pP = pall.next()
                for hp in range(2):
                    S.op("pe", lambda e, hp=hp: e.matmul(pP[0][:, hp * 128:(hp + 1) * 128],
                                                         lhsT=u.UG[:, 1, hp * 128:(hp + 1) * 128],
                                                         rhs=u.Xt[:, 2, hp * 128:(hp + 1) * 128], start=True, stop=True),
                         R=[u.UGk, u.Xtk], W=[pP[1]])
                S.op("dve", lambda e: e.tensor_tensor(out=u.Ph[:], in0=pP[0][:, 0:256].rearrange("p (a c) -> p a c", a=2),
                                                      in1=bdm[:].unsqueeze(1).to_broadcast([128, 2, 128]), op=ALU.mult),
                     R=[pP[1], "bdm"], W=[u.Phk])
                pS = pall.next()
                pR = pall.next()
                for h in range(4):
                    hp, rows = hd(h)
                    S.op("pe", lambda e, h=h, hp=hp, rows=rows: e.matmul(
                        pS[0][rows, hp * 64:(hp + 1) * 64], lhsT=u.Xt[:, 1, h * 64:(h + 1) * 64],
                        rhs=Vtm[:, i, h * 64:(h + 1) * 64], start=True, stop=False),
                        R=[u.Xtk, "Vtm"], W=[pS[1]])
                    S.op("pe", lambda e, h=h, hp=hp, rows=rows: e.matmul(
                        pS[0][rows, hp * 64:(hp + 1) * 64], lhsT=u.Xt[:, 2, h * 64:(h + 1) * 64],
                        rhs=u.UG[:, 0, h * 64:(h + 1) * 64], start=False, stop=True),
                        R=[u.Xtk, u.UGk], W=[pS[1]])
                    S.op("pe", lambda e, h=h, hp=hp, rows=rows: e.matmul(
                        pR[0][rows, hp * 128:(hp + 1) * 128], lhsT=u.UG[:, 1, h * 64:(h + 1) * 64],
                        rhs=u.A2[:, h, 128:256], start=True, stop=True), R=[u.UGk, u.A2k], W=[pR[1]])
                S.op("act", lambda e: e.activation(out=u.Ps[:].rearrange("p a c -> p (a c)"), in_=pS[0][:, 0:128],
                                                   func=AF.Copy), R=[pS[1]], W=[u.Psk])
                S.op("dve", lambda e: e.tensor_tensor(out=u.Rg[:], in0=pR[0][:, 0:256].rearrange("p (a c) -> p a c", a=2),
                                                      in1=X[:, :, 1, :], op=ALU.add), R=[pR[1], u.Xk], W=[u.Rgk])

            def chain(u):
                d, i = u.d, u.i
                pY = pall.next()
                pH = pall.next()
                for h in range(4):
                    hp, rows = hd(h)
                    Hc = Hb[hp][hcur[hp]]
                    Hk = "H%d%d" % (hp, hcur[hp])
                    S.op("pe", lambda e, h=h: e.matmul(pY[0][:, h * 64:(h + 1) * 64], lhsT=u.A1[:, h, 128:256],
                                                       rhs=Vtm[:, i, h * 64:(h + 1) * 64], start=True, stop=False),
                         R=[u.A1k, "Vtm"], W=[pY[1]])
                    S.op("pe", lambda e, h=h: e.matmul(pY[0][:, h * 64:(h + 1) * 64], lhsT=u.A2[:, h, 128:256],
                                                       rhs=u.UG[:, 0, h * 64:(h + 1) * 64], start=False, stop=False),
                         R=[u.A2k, u.UGk], W=[pY[1]])
                    Hcb = Hh[hp][hcur[hp]]
                    S.op("pe", lambda e, h=h, hp=hp, rows=rows, Hcb=Hcb: e.matmul(
                        pY[0][:, h * 64:(h + 1) * 64], lhsT=u.Rg[rows, hp, :], rhs=Hcb[rows, :], start=False, stop=True),
                        R=[u.Rgk, Hk + "b"], W=[pY[1]])
                for hp in range(2):
                    Hc = Hb[hp][hcur[hp]]
                    Hk = "H%d%d" % (hp, hcur[hp])
                    S.op("pe", lambda e, hp=hp, Hc=Hc: e.matmul(pH[0][:, hp * 64:(hp + 1) * 64], lhsT=u.Ph[:, hp, :],
                                                                rhs=Hc[:], start=True, stop=True),
                         R=[u.Phk, Hk], W=[pH[1]])
                for hp in range(2):
                    Hc = Hb[hp][hcur[hp]]
                    Hk = "H%d%d" % (hp, hcur[hp])
                    Hn = Hb[hp][1 - hcur[hp]]
                    Hnk = "H%d%d" % (hp, 1 - hcur[hp])
                    Hnb = Hh[hp][1 - hcur[hp]]
                    S.op("dve", lambda e, hp=hp, Hc=Hc, Hn=Hn: e.scalar_tensor_tensor(
                        out=Hn[:], in0=Hc[:], scalar=PCs[:, d, hp, i:i + 1], in1=pH[0][:, hp * 64:(hp + 1) * 64],
                        op0=ALU.mult, op1=ALU.add), R=[Hk, "PC", pH[1]], W=[Hnk])
                    S.op("dve", lambda e, hp=hp, Hn=Hn: e.tensor_tensor(out=Hn[:], in0=Hn[:], in1=u.Ps[:, hp, :], op=ALU.add),
                         R=[Hnk, u.Psk], W=[Hnk])
                    S.op("act", lambda e, Hn=Hn, Hnb=Hnb: e.activation(out=Hnb[:], in_=Hn[:], func=AF.Copy),
                         R=[Hnk], W=[Hnk + "b"])
                    hcur[hp] = 1 - hcur[hp]
                if d == 0:
                    S.op("act", lambda e: e.activation(out=Yacc[:, i, :], in_=pY[0][:, 0:256], func=AF.Copy),
                         R=[pY[1]], W=["Yacc%d" % i])
                elif not (l == DEPTH - 1 and i < 2):
                    epilogue(i, pY)

            def epilogue(i, pY):
                ys, ysk = ysr.next()
                yc, yck = ycr.next()
                sq, sqk = sqr.next()
                sm, smk = s4r.next()
                vr_, vrk = s4r.next()
                bs, bsk = bsr.next()
                btl, btlk = btlr.next()
                ob, obk = obr.next()
                oT, oTk = oTr.next()
                ys2 = ys[:].rearrange("p a c -> p (a c)")
                yc2 = yc[:].rearrange("p a c -> p (a c)")
                S.op("dve", lambda e: e.tensor_tensor(out=ys2, in0=Yacc[:, i, :], in1=pY[0][:, 0:256], op=ALU.add),
                     R=["Yacc%d" % i, pY[1]], W=[ysk])
                S.op("dve", lambda e: e.tensor_reduce(out=sm[:], in_=ys[:], axis=AX.X, op=ALU.add), R=[ysk], W=[smk])
                S.op("dve", lambda e: e.tensor_scalar(out=sm[:], in0=sm[:], scalar1=-1.0 / 64, scalar2=None, op0=ALU.mult),
                     R=[smk], W=[smk])
                S.op("dve", lambda e: e.tensor_tensor(out=yc[:], in0=ys[:], in1=sm[:].unsqueeze(2).to_broadcast([128, 4, 64]),
                                                      op=ALU.add), R=[ysk, smk], W=[yck])
                S.op("pool", lambda e: e.tensor_tensor(out=sq[:], in0=yc[:], in1=yc[:], op=ALU.mult), R=[yck], W=[sqk])
                S.op("dve", lambda e: e.tensor_reduce(out=vr_[:], in_=sq[:], axis=AX.X, op=ALU.add), R=[sqk], W=[vrk])
                S.op("act", lambda e: e.activation(out=vr_[:], in_=vr_[:], func=AF.Sqrt, bias=64e-5, scale=1.0 / 64),
                     R=[vrk], W=[vrk])
                S.op("dve", lambda e: e.reciprocal(out=vr_[:], in_=vr_[:]), R=[vrk], W=[vrk])
                S.op("dve", lambda e: e.tensor_tensor(out=yc[:], in0=yc[:], in1=vr_[:].unsqueeze(2).to_broadcast([128, 4, 64]),
                                                      op=ALU.mult), R=[yck, vrk], W=[yck])
                S.op("pool", lambda e: e.tensor_tensor(out=yc2, in0=yc2, in1=gnB[:, 0, :], op=ALU.mult),
                     R=[yck, "gnB"], W=[yck])
                S.op("pool", lambda e: e.tensor_tensor(out=yc2, in0=yc2, in1=gnB[:, 1, :], op=ALU.add),
                     R=[yck, "gnB"], W=[yck])
                S.dma("sp", btl[:], BT[:, :, i * 128:(i + 1) * 128], W=[btlk])
                pBo = pall.next()
                for hp in range(2):
                    S.op("pe", lambda e, hp=hp: e.matmul(pBo[0][:, hp * 2:hp * 2 + 2], lhsT=btl[:, hp, :], rhs=hsel[:],
                                                         start=True, stop=True), R=[btlk, "hsel"], W=[pBo[1]])
                S.op("dve", lambda e: e.tensor_copy(out=bs[:], in_=pBo[0][:, 0:4]), R=[pBo[1]], W=[bsk])
                S.op("dve", lambda e: e.tensor_tensor(out=sq[:], in0=Vtm[:, i, :].rearrange("p (a c) -> p a c", a=4),
                                                      in1=bs[:].unsqueeze(2).to_broadcast([128, 4, 64]), op=ALU.mult),
                     R=["Vtm", bsk, sqk], W=[sqk])
                S.op("pool", lambda e: e.tensor_tensor(out=yc[:], in0=yc[:], in1=sq[:], op=ALU.add), R=[yck, sqk], W=[yck])
                S.op("dve", lambda e: e.tensor_tensor(out=ob[:], in0=yc2, in1=Gt[:, i, :], op=ALU.mult),
                     R=[yck, "Gt%d" % i], W=[obk])
                pO = pall.next()
                pOb = pO[0][:, :].bitcast(BF16)
                for c in range(2):
                    S.op("pe", lambda e, c=c: e.transpose(out=pOb[:, c * 128:(c + 1) * 128],
                                                          in_=ob[:, c * 128:(c + 1) * 128], identity=k.ident[:]),
                         R=[obk, "ident"], W=[pO[1]])
                S.op("act", lambda e: e.activation(out=oT[:].rearrange("p a c -> p (a c)"), in_=pOb[:, 0:256],
                                                   func=AF.Copy), R=[pO[1]], W=[oTk])
                S.dma("sp", k.OUTS["rwkv"][:, :, i * 128:(i + 1) * 128], oT[:], R=[oTk], W=["OUTS_rwkv"])

            for d in range(2):
                order = list(range(NT)) if d == 0 else [1, 0] + list(range(NT - 1, 1, -1))
                for hp in range(2):
                    Hc = Hb[hp][hcur[hp]]
                    S.op("pool", lambda e, Hc=Hc: e.memset(Hc[:], 0.0), W=["H%d%d" % (hp, hcur[hp])])
                    Hcb0 = Hh[hp][hcur[hp]]
                    S.op("pool", lambda e, Hcb0=Hcb0: e.memset(Hcb0[:], 0.0), W=["H%d%db" % (hp, hcur[hp])])
                for b0 in range(0, NT, NB):
                    units = []
                    for i in order[b0:b0 + NB]:
                        u = U()
                        u.d, u.i = d, i
                        units.append(u)
                    for u in units:
                        pre1(u)
                    if CUT <= 3:
                        continue
                    for _ in range(6):
                        for u in units:
                            merge_level(u)
                    if CUT <= 4:
                        continue
                    for u in units:
                        pre2(u)
                    if CUT <= 5:
                        continue
                    for u in units:
                        pre3(u)
                    if CUT <= 6:
                        continue
                    for u in units:
                        chain(u)
            S.barrier()
            S.flush()


def phase_merge(k, l):
    nc, S, I = k.nc, k.S, k.I
    S.phase = "merge%d" % l
    act = [m for m in MIX if m in k.active]
    src = I["xin"] if l == 0 else k.X1
    with ExitStack() as es:
        wg = _sb(es, nc, "wg", [128, 8, 4096], BF16)
        bw = _sb(es, nc, "bw", [128, 8, D], BF16)
        ow = _sb(es, nc, "ow", [128, 8, D], BF16)
        mb = _sb(es, nc, "mb", [128, 32], F32)
        lnB = _sb(es, nc, "lnB", [128, 2, D], F32)
        load_w(k, bw, "bw", I["branch_w"][l].rearrange("i k n -> (i k) n"), D)
        load_w(k, wg, "wg", I["in_w"][l][:, O_MERGE:O_MERGE + 4096], 4096)
        load_w(k, ow, "ow", I["out_w"][l], D)
        S.dma("act", mb[:], I["merge_bt"][l], W=["mb"])
        S.dma("act", lnB[:, 0, :], I["ln_g"][l:l + 1, :].to_broadcast([128, D]), W=["lnB"])
        S.dma("act", lnB[:, 1, :], I["ln_b"][l:l + 1, :].to_broadcast([128, D]), W=["lnB"])
        S.barrier()
        S.flush()
        xbr = ring(es, nc, "xb", 2, [128, 8, 512], BF16)
        obr = {m: ring(es, nc, "mob" + m, 2, [128, 2, 512], BF16) for m in act}
        yTr = ring(es, nc, "yT", 1, [128, 8, 512], BF16)
        sgr = ring(es, nc, "sg", 2, [128, 512], F32)
        tmr = ring(es, nc, "tm", 2, [128, 512], F32)
        yar = ring(es, nc, "ya", 2, [128, 512], F32)
        xrr = ring(es, nc, "xr", 2, [128, D], F32)
        tAr = ring(es, nc, "tA", 2, [128, D], F32)
        tBr = ring(es, nc, "tB", 2, [128, D], F32)
        str_ = ring(es, nc, "mst", 2, [128, 2, 6], F32)
        mvr = ring(es, nc, "mmv", 2, [128, 2], F32)
        rsr = ring(es, nc, "mrs", 2, [128, 1], F32)
        psG, psZ = psring(k, [0, 1]), psring(k, [2, 3])
        psY = Ring([((k.ps[4], k.ps[5]), ("ps4", "ps5")), ((k.ps[6], k.ps[7]), ("ps6", "ps7"))])
        blocks = [(0, NCTX)] if l == 0 else []
        blocks += [(NCTX + 512 * j, 512) for j in range(8)]
        def block_body(q0, nq):
            nsub = nq // 128
            which = 1 if q0 < NCTX else 0
            xb, xbk = xbr.next()
            for kc in range(8):
                S.dma("sp" if kc % 2 == 0 else "act", xb[:, kc, 0:nq], k.XM[:, kc, q0:q0 + nq], W=[xbk + "_%d" % kc])
            obs = {}
            for m in act:
                ob, obk = obr[m].next()
                S.dma("sp", ob[:, :, 0:nq], k.OUTS[m][:, :, q0:q0 + nq], W=[obk])
                obs[m] = (ob, obk)
            yT, yTk = yTr.next()
            for fc in range(8):
                ya, yak = yar.next()
                first = True
                for m in act:
                    i = MIX.index(m)
                    pg, pgk = psG.next()
                    pz, pzk = psZ.next()
                    col = i * 1024 + fc * 128
                    for kc in range(8):
                        S.op("pe", lambda e, pg=pg, kc=kc, col=col, xb=xb: e.matmul(
                            pg[:, 0:nq], lhsT=wg[:, kc, col:col + 128], rhs=xb[:, kc, 0:nq],
                            start=(kc == 0), stop=(kc == 7)), R=[xbk + "_%d" % kc] + wkeys("wg", col, 128), W=[pgk])
                    ob, obk = obs[m]
                    for c in range(2):
                        S.op("pe", lambda e, pz=pz, c=c, i=i, fc=fc, ob=ob: e.matmul(
                            pz[:, 0:nq], lhsT=bw[:, i * 2 + c, fc * 128:(fc + 1) * 128], rhs=ob[:, c, 0:nq],
                            start=(c == 0), stop=(c == 1)), R=[obk] + wkeys("bw", fc * 128, 128), W=[pzk])
                    sg, sgk = sgr.next()
                    S.op("act", lambda e, sg=sg, pg=pg, i=i, fc=fc: e.activation(
                        out=sg[:, 0:nq], in_=pg[:, 0:nq], func=AF.Sigmoid, bias=mb[:, i * 8 + fc:i * 8 + fc + 1],
                        scale=1.0), R=[pgk, "mb"], W=[sgk])
                    last = (m == act[-1])
                    dst, dstk = ((yT[:, fc, 0:nq], yTk) if (last and first) else (ya[:, 0:nq], yak))
                    if first:
                        S.op("dve", lambda e, dst=dst, sg=sg, pz=pz: e.tensor_tensor(out=dst, in0=sg[:, 0:nq],
                                                                                    in1=pz[:, 0:nq], op=ALU.mult),
                             R=[sgk, pzk], W=[dstk])
                    else:
                        tm, tmk = tmr.next()
                        S.op("dve", lambda e, tm=tm, sg=sg, pz=pz: e.tensor_tensor(out=tm[:, 0:nq], in0=sg[:, 0:nq],
                                                                                  in1=pz[:, 0:nq], op=ALU.mult),
                             R=[sgk, pzk], W=[tmk])
                        if last:
                            S.op("pool", lambda e, tm=tm, ya=ya, yT=yT, fc=fc: e.tensor_tensor(
                                out=yT[:, fc, 0:nq], in0=ya[:, 0:nq], in1=tm[:, 0:nq], op=ALU.add),
                                R=[yak, tmk], W=[yTk])
                        else:
                            S.op("pool", lambda e, tm=tm, ya=ya: e.tensor_tensor(out=ya[:, 0:nq], in0=ya[:, 0:nq],
                                                                                in1=tm[:, 0:nq], op=ALU.add),
                                 R=[yak, tmk], W=[yak])
                    first = False
            for j in range(nsub):
                t0 = q0 + j * 128
                (py0, py1), (pyk0, pyk1) = psY.next()
                for hb, (py, pyk) in enumerate(((py0, pyk0), (py1, pyk1))):
                    for fc in range(8):
                        S.op("pe", lambda e, py=py, fc=fc, hb=hb, j=j, yT=yT: e.matmul(
                            py[:, :], lhsT=yT[:, fc, j * 128:(j + 1) * 128], rhs=ow[:, fc, hb * 512:(hb + 1) * 512],
                            start=(fc == 0), stop=(fc == 7)), R=[yTk] + wkeys("ow", hb * 512, 512), W=[pyk])
                xr, xrk = xrr.next()
                tA, tAk = tAr.next()
                tB, tBk = tBr.next()
                S.dma("sp", xr[:], src[t0:t0 + 128, :], W=[xrk])
                for hb, (py, pyk) in enumerate(((py0, pyk0), (py1, pyk1))):
                    S.op("dve", lambda e, tA=tA, py=py, hb=hb, which=which: e.tensor_tensor(
                        out=tA[:, hb * 512:(hb + 1) * 512], in0=py[:, :], in1=k.modB[:, which, 2, hb * 512:(hb + 1) * 512],
                        op=ALU.mult), R=[pyk, "modB"], W=[tAk])
                S.op("dve", lambda e, tA=tA, xr=xr, tB=tB: e.scalar_tensor_tensor(
                    out=tB[:], in0=xr[:], scalar=float(ALPHA), in1=tA[:], op0=ALU.mult, op1=ALU.add),
                    R=[xrk, tAk], W=[tBk])
                stt, mvt, rst = str_.next(), mvr.next(), rsr.next()
                ln_rows(k, stt, mvt, rst, tB, tBk, 1e-5)
                mv, mvk = mvt
                rs, rsk = rst
                S.op("dve", lambda e, tA=tA, tB=tB, mv=mv, rs=rs: e.tensor_scalar(
                    out=tA[:], in0=tB[:], scalar1=mv[:, 0:1], scalar2=rs[:, 0:1], op0=ALU.subtract, op1=ALU.mult),
                    R=[tBk, mvk, rsk], W=[tAk])
                S.op("pool", lambda e, tA=tA: e.tensor_tensor(out=tA[:], in0=tA[:], in1=lnB[:, 0, :], op=ALU.mult),
                     R=[tAk, "lnB"], W=[tAk])
                S.op("pool", lambda e, tA=tA, tB=tB: e.tensor_tensor(out=tB[:], in0=tA[:], in1=lnB[:, 1, :], op=ALU.add),
                     R=[tAk, "lnB"], W=[tBk])
                if l == DEPTH - 1:
                    S.dma("sp", k.out[t0 - NCTX:t0 - NCTX + 128, :], tB[:], R=[tBk], W=["out"])
                else:
                    S.dma("sp", k.X1[t0:t0 + 128, :], tB[:], R=[tBk], W=["X1_%d" % (t0 // 128)])

        for (q0, nq) in blocks:
            block_body(q0, nq)
        S.barrier()
        S.flush()


def rope_table(rot):
    L = SEQ
    rows = L // 64
    row = np.repeat(np.arange(rows), 64).astype(np.float32)
    col = np.tile(np.arange(64), rows).astype(np.float32)
    q = rot // 4
    inv = (np.float32(10000.0) ** (-np.arange(q, dtype=np.float32) / np.float32(q))).astype(np.float32)
    ang = np.concatenate([row[:, None] * inv, col[:, None] * inv], axis=-1).astype(np.float32)
    cos = np.concatenate([np.ones((NCTX, rot // 2), np.float32), np.cos(ang)], 0)
    sin = np.concatenate([np.zeros((NCTX, rot // 2), np.float32), np.sin(ang)], 0)
    return np.ascontiguousarray(np.concatenate([cos, sin], axis=1).astype(np.float32))


def input_specs():
    return [
        ("xin", [T, D], F32), ("cvec", [128, 8, 2], F32),
        ("ada_w", [DEPTH, D, 3 * D], F32), ("ada_b", [DEPTH, 3 * D], F32), ("in_w", [DEPTH, D, IN_W], F32),
        ("ident", [128, 128], BF16), ("sel2", [2, 2, 128], F32),
        ("rope64", [T, 64], F32), ("rope32", [T, 32], F32),
        ("gqa_q_norm", [DEPTH, 64], F32), ("gqa_k_norm", [DEPTH, 64], F32),
        ("mla_q_norm", [DEPTH, 256], F32), ("mla_w_uq", [DEPTH, 256, 384], F32),
        ("mla_kv_norm", [DEPTH, 128], F32), ("mla_w_ukv", [DEPTH, 128, 512], F32),
        ("diff_lambda", [DEPTH, 4, 32], F32), ("subcol", [DEPTH, 128, 1], F32),
        ("rw_cols", [DEPTH, 128, 2, 7], F32), ("rw_mu", [DEPTH, 128, 2, 8], F32),
        ("rwkv_w_up", [DEPTH, 2, 64, 256], F32), ("rwkv_a_up", [DEPTH, 2, 64, 256], F32),
        ("rwkv_gn_w", [DEPTH, 256], F32), ("rwkv_gn_b", [DEPTH, 256], F32),
        ("bones", [128, 128], F32), ("identf", [128, 128], F32), ("hsel", [128, 2], BF16), ("rmask", [2, 128, 640], F32),
        ("lvmask", [128, 7, 4, 128], U16), ("qmask", [128, 6], F32),
        ("merge_bt", [DEPTH, 128, 32], F32), ("branch_w", [DEPTH, 4, 256, D], F32), ("out_w", [DEPTH, D, D], F32),
        ("ln_g", [DEPTH, D], F32), ("ln_b", [DEPTH, D], F32),
    ]


def host_consts():
    c = {}
    c["ident"] = np.eye(128, dtype=np.float32).astype(ml_dtypes.bfloat16)
    sel = np.zeros((2, 2, 128), np.float32)
    sel[0, 0, :] = 1.0
    sel[1, 1, :] = 1.0
    c["sel2"] = sel
    bo = np.zeros((128, 128), np.float32)
    bo[:64, :64] = 1.0
    bo[64:, 64:] = 1.0
    c["bones"] = bo
    c["identf"] = np.eye(128, dtype=np.float32)
    hs = np.zeros((128, 2), np.float32)
    hs[:64, 0] = 1.0
    hs[64:, 1] = 1.0
    c["hsel"] = hs.astype(ml_dtypes.bfloat16)
    r_, c_ = np.meshgrid(np.arange(128), np.arange(128), indexing="ij")
    U_ = (r_ < c_).astype(np.float32)
    Ue = (r_ <= c_).astype(np.float32)
    L_ = (r_ > c_).astype(np.float32)
    Le = (r_ >= c_).astype(np.float32)
    rm = np.zeros((2, 128, 640), np.float32)
    rm[0] = np.concatenate([U_, Ue, -U_, -Ue, -L_], axis=1)
    rm[1] = np.concatenate([L_, Le, -L_, -Le, -U_], axis=1)
    c["rmask"] = rm
    qm = np.zeros((128, 6), np.float32)
    for j in range(2):
        qm[64 * j:64 * j + 64, j] = 1.0
    for j in range(4):
        qm[32 * j:32 * j + 32, 2 + j] = 1.0
    c["qmask"] = qm
    lv = np.zeros((128, 7, 4, 128), np.uint16)
    for j in range(7):
        bsz = 1 << j
        lv[:, j, :, :] = ((r_ // (2 * bsz) == c_ // (2 * bsz)) & (r_ // bsz != c_ // bsz)).astype(np.uint16)[:, None, :]
    c["lvmask"] = lv
    c["rope64"] = rope_table(64)
    c["rope32"] = rope_table(32)
    return c


def make_in_maps(inputs):
    consts = host_consts()
    shared = dict(consts)
    for nm in ["ada_w", "ada_b", "in_w", "gqa_q_norm", "gqa_k_norm", "branch_w", "out_w", "ln_g", "ln_b",
               "mla_q_norm", "mla_w_uq", "mla_kv_norm", "mla_w_ukv", "diff_lambda",
               "rwkv_w_up", "rwkv_a_up", "rwkv_gn_w", "rwkv_gn_b"]:
        shared[nm] = np.ascontiguousarray(inputs[nm], dtype=np.float32)
    def colsplit(a):
        return a.reshape(DEPTH, 2, 128).transpose(0, 2, 1)
    rc = np.stack([colsplit(inputs["rwkv_k_k"]), colsplit(inputs["rwkv_k_a"]), colsplit(inputs["rwkv_r_k"]),
                   colsplit(inputs["rwkv_w0"][:, 0]), colsplit(inputs["rwkv_w0"][:, 1]),
                   colsplit(inputs["rwkv_a0"][:, 0]), colsplit(inputs["rwkv_a0"][:, 1])], axis=-1)
    shared["rw_cols"] = np.ascontiguousarray(rc.astype(np.float32))
    shared["subcol"] = np.ascontiguousarray(np.tile(inputs["diff_subln"], (1, 2)).reshape(DEPTH, 128, 1).astype(np.float32))
    shared["rw_mu"] = np.ascontiguousarray(inputs["rwkv_mu"].reshape(DEPTH, 2, 8, 128).transpose(0, 3, 1, 2))
    shared["merge_bt"] = np.ascontiguousarray(inputs["merge_b"].reshape(DEPTH, 32, 128).transpose(0, 2, 1))
    maps = []
    for b in range(8):
        m = dict(shared)
        m["xin"] = np.ascontiguousarray(np.concatenate([inputs["ctx"][b], inputs["x"][b]], axis=0))
        cv = np.stack([inputs["c"][b].reshape(8, 128).T, inputs["c_ctx"].reshape(8, 128).T], axis=-1)
        m["cvec"] = np.ascontiguousarray(cv.astype(np.float32))
        maps.append(m)
    return maps


def kernel(**inputs):
    inputs = {kk: np.asarray(v) for kk, v in inputs.items()}
    nc = build(active=tuple(MIX))
    maps = make_in_maps(inputs)
    res = run_bass_kernel_spmd(nc, maps, core_ids=list(range(8)))
    return np.stack([np.asarray(r["out"]) for r in res.results], axis=0).astype(np.float32)
```
